# Optimizing a Trainium2 kernel written in Bass

```python
import math
import jax, jax.numpy as jnp
from jax import lax
import numpy as np


D_MODEL = 2048
BATCH = 4
SEQ = 8192
DEPTH = 4

GDN_HEADS = D_MODEL // 256
GDN_HEAD_DIM = 128
GDN_WIDTH = GDN_HEADS * GDN_HEAD_DIM
CONV_WIDTH = 4
CHUNK = 64
DSA_HEADS = D_MODEL // 256
DSA_HEAD_DIM = 128
DSA_WIDTH = DSA_HEADS * DSA_HEAD_DIM
IDX_HEADS = D_MODEL // 256
IDX_HEAD_DIM = 64
TOPK_MAX = 256
Q_BLOCK = 128
REL_BUCKETS = 32
REL_MAX_DIST = 128
D_FF = -(-8 * D_MODEL // (3 * 256)) * 256
DEEPNORM_ALPHA = (2 * DEPTH) ** 0.25
DEEPNORM_BETA = (8 * DEPTH) ** -0.25
LN_EPS = 1e-5
RMS_EPS = 1e-6

SPLIT_SIZES = (
    3 * GDN_WIDTH,
    GDN_HEADS,
    GDN_HEADS,
    GDN_WIDTH,
    DSA_WIDTH, DSA_WIDTH, DSA_WIDTH,
    IDX_HEADS * IDX_HEAD_DIM,
    IDX_HEAD_DIM,
    IDX_HEADS,
    D_MODEL,
    D_MODEL,
)
D_IN = sum(SPLIT_SIZES)

kernel_name = 'hybrid_gdn_dsa_deepnorm_block'


def layer_norm(x, g, b):
    xf = x.astype(jnp.float32)
    mu = jnp.mean(xf, axis=-1, keepdims=True)
    var = jnp.mean(jnp.square(xf - mu), axis=-1, keepdims=True)
    return ((xf - mu) * lax.rsqrt(var + LN_EPS) * g.astype(jnp.float32) + b.astype(jnp.float32)).astype(x.dtype)


def l2_normalize(x):
    return x * lax.rsqrt(jnp.sum(jnp.square(x), axis=-1, keepdims=True) + RMS_EPS)


def causal_short_conv(x, w):
    S = x.shape[1]
    K = w.shape[0]
    xp = jnp.pad(x, ((0, 0), (K - 1, 0), (0, 0)))
    y = sum(xp[:, j:j + S] * w[j] for j in range(K))
    return jax.nn.silu(y)


def gated_delta_rule_chunked(q, k, v, g, beta):
    B, S, H, Dk = q.shape
    Dv = v.shape[-1]
    N = S // CHUNK
    f32 = jnp.float32
    q = l2_normalize(q.astype(f32)) * (Dk ** -0.5)
    k = l2_normalize(k.astype(f32))

    def chunks(a):
        return jnp.moveaxis(a.reshape(B, N, CHUNK, H, *a.shape[3:]), 3, 1)

    q, k, v = chunks(q), chunks(k), chunks(v.astype(f32))
    g, beta = chunks(g.astype(f32)), chunks(beta.astype(f32))
    gc = jnp.cumsum(g, axis=-1)
    causal = jnp.tril(jnp.ones((CHUNK, CHUNK), dtype=bool))
    strict = jnp.tril(jnp.ones((CHUNK, CHUNK), dtype=bool), k=-1)
    decay = jnp.exp(jnp.where(causal, gc[..., :, None] - gc[..., None, :], -jnp.inf))
    k_beta = k * beta[..., None]
    v_beta = v * beta[..., None]
    m = jnp.where(strict, jnp.einsum('bhnid,bhnjd->bhnij', k_beta, k) * decay, 0.0)
    a = m + jnp.eye(CHUNK, dtype=f32)
    u = lax.linalg.triangular_solve(a, v_beta, left_side=True, lower=True, unit_diagonal=True)
    w = lax.linalg.triangular_solve(a, k_beta * jnp.exp(gc)[..., None], left_side=True, lower=True, unit_diagonal=True)
    intra = jnp.einsum('bhnid,bhnjd->bhnij', q, k) * decay
    q_dec = q * jnp.exp(gc)[..., None]
    k_dec = k * jnp.exp(gc[..., -1:] - gc)[..., None]
    g_tot = jnp.exp(gc[..., -1])

    def step(state, inp):
        q_n, k_n, u_n, w_n, a_n, gt = inp
        v_new = u_n - jnp.einsum('bhcd,bhde->bhce', w_n, state)
        o = jnp.einsum('bhcd,bhde->bhce', q_n, state) + jnp.einsum('bhij,bhje->bhie', a_n, v_new)
        state = state * gt[..., None, None] + jnp.einsum('bhcd,bhce->bhde', k_n, v_new)
        return state, o

    xs = tuple(jnp.moveaxis(t, 2, 0) for t in (q_dec, k_dec, u, w, intra, g_tot))
    state0 = jnp.zeros((B, H, Dk, Dv), f32)
    _, o = lax.scan(step, state0, xs)
    return jnp.transpose(o, (1, 0, 3, 2, 4)).reshape(B, S, H, Dv)


def t5_bucket(dist):
    n = jnp.maximum(dist, 0)
    max_exact = REL_BUCKETS // 2
    log_ratio = jnp.log(jnp.maximum(n, 1).astype(jnp.float32) / max_exact) / math.log(REL_MAX_DIST / max_exact)
    large = max_exact + (log_ratio * (REL_BUCKETS - max_exact)).astype(jnp.int32)
    large = jnp.minimum(large, REL_BUCKETS - 1)
    return jnp.where(n < max_exact, n, large)


def dsa_sparse_attention(q, k, v, q_idx, k_idx, w_idx, rel_bias):
    B, S, H, D = q.shape
    topk = min(TOPK_MAX, S // 4)
    nb = S // Q_BLOCK
    f32 = jnp.float32
    k_idx = k_idx.astype(f32)
    key_pos = jnp.arange(S)
    gather = jax.vmap(lambda src, ids: src[ids])

    def blocks(a):
        return jnp.moveaxis(a.reshape(B, nb, Q_BLOCK, *a.shape[2:]), 1, 0)

    def attend_block(inp):
        q_b, qi_b, wi_b, t0 = inp
        q_pos = t0 + jnp.arange(Q_BLOCK)
        rel = jax.nn.relu(jnp.einsum('bqhd,bsd->bqhs', qi_b.astype(f32), k_idx))
        score = jnp.einsum('bqh,bqhs->bqs', wi_b.astype(f32), rel)
        score = jnp.where(key_pos[None, None, :] <= q_pos[None, :, None], score, -jnp.inf)
        _, idx = lax.top_k(score, topk)
        k_sel = gather(k, idx)
        v_sel = gather(v, idx)
        dist = q_pos[None, :, None] - idx
        bias = rel_bias[t5_bucket(dist)].astype(f32)
        logits = jnp.einsum('bqhd,bqkhd->bqhk', q_b, k_sel).astype(f32) * (D ** -0.5) + jnp.moveaxis(bias, 3, 2)
        logits = jnp.where((dist >= 0)[:, :, None, :], logits, -jnp.inf)
        p = jax.nn.softmax(logits, axis=-1).astype(v.dtype)
        return jnp.einsum('bqhk,bqkhd->bqhd', p, v_sel)

    out = lax.map(attend_block, (blocks(q), blocks(q_idx), blocks(w_idx), jnp.arange(nb) * Q_BLOCK))
    return jnp.moveaxis(out, 0, 1).reshape(B, S, H * D)


def setup_inputs(seed: int = 0) -> dict:
    key = jax.random.key(seed)
    ks = jax.random.split(key, 16)
    f32 = jnp.float32

    def nrm(k, shape, scale):
        return jax.random.normal(k, shape, f32) * scale

    x = nrm(ks[0], (BATCH, SEQ, D_MODEL), 1.0)
    rel_bias = nrm(ks[1], (REL_BUCKETS, DSA_HEADS), 0.5)
    w_in = nrm(ks[2], (DEPTH, D_MODEL, D_IN), D_MODEL ** -0.5)
    conv_w = nrm(ks[3], (DEPTH, CONV_WIDTH, 3 * GDN_WIDTH), CONV_WIDTH ** -0.5)
    a_log = jnp.log(jax.random.uniform(ks[4], (DEPTH, GDN_HEADS), f32, 1.0, 16.0))
    dt = jnp.exp(jax.random.uniform(ks[5], (DEPTH, GDN_HEADS), f32, math.log(1e-3), math.log(1e-1)))
    dt_bias = dt + jnp.log(-jnp.expm1(-dt))
    gdn_norm_w = 1.0 + nrm(ks[6], (DEPTH, GDN_HEAD_DIM), 0.02)
    w_branch_a = nrm(ks[7], (DEPTH, GDN_WIDTH, D_MODEL), GDN_WIDTH ** -0.5)
    w_branch_b = nrm(ks[8], (DEPTH, DSA_WIDTH, D_MODEL), DSA_WIDTH ** -0.5)
    w_out = nrm(ks[9], (DEPTH, D_MODEL, D_MODEL), DEEPNORM_BETA * D_MODEL ** -0.5)
    ln1_g = 1.0 + nrm(ks[10], (DEPTH, D_MODEL), 0.02)
    ln1_b = nrm(ks[11], (DEPTH, D_MODEL), 0.02)
    w_ffn_in = nrm(ks[12], (DEPTH, D_MODEL, 2 * D_FF), D_MODEL ** -0.5)
    w_ffn_out = nrm(ks[13], (DEPTH, D_FF, D_MODEL), DEEPNORM_BETA * D_FF ** -0.5)
    ln2_g = 1.0 + nrm(ks[14], (DEPTH, D_MODEL), 0.02)
    ln2_b = nrm(ks[15], (DEPTH, D_MODEL), 0.02)
    return {'x': x, 'rel_bias': rel_bias, 'w_in': w_in, 'conv_w': conv_w, 'a_log': a_log,
            'dt_bias': dt_bias, 'gdn_norm_w': gdn_norm_w, 'w_branch_a': w_branch_a,
            'w_branch_b': w_branch_b, 'w_out': w_out, 'ln1_g': ln1_g, 'ln1_b': ln1_b,
            'w_ffn_in': w_ffn_in, 'w_ffn_out': w_ffn_out, 'ln2_g': ln2_g, 'ln2_b': ln2_b}


def reference(x, rel_bias, w_in, conv_w, a_log, dt_bias, gdn_norm_w, w_branch_a, w_branch_b,
              w_out, ln1_g, ln1_b, w_ffn_in, w_ffn_out, ln2_g, ln2_b):
    B, S, _ = x.shape
    offsets = [int(o) for o in np.cumsum(SPLIT_SIZES)[:-1]]

    def heads(t, n):
        return t.reshape(B, S, n, -1)

    for l in range(DEPTH):
        proj = x @ w_in[l]
        (qkv_a, a_in, b_in, z, q_b, k_b, v_b, q_i, k_i, w_i, gate_a, gate_b) = jnp.split(proj, offsets, axis=-1)

        qkv_a = causal_short_conv(qkv_a, conv_w[l])
        q_a, k_a, v_a = jnp.split(qkv_a, 3, axis=-1)
        log_decay = -jnp.exp(a_log[l]) * jax.nn.softplus(a_in + dt_bias[l])
        beta = jax.nn.sigmoid(b_in)
        o_a = gated_delta_rule_chunked(heads(q_a, GDN_HEADS), heads(k_a, GDN_HEADS),
                                       heads(v_a, GDN_HEADS), log_decay, beta)
        o_a = (o_a * lax.rsqrt(jnp.mean(jnp.square(o_a), axis=-1, keepdims=True) + RMS_EPS)
               * gdn_norm_w[l].astype(jnp.float32)
               * jax.nn.silu(heads(z, GDN_HEADS).astype(jnp.float32)))
        o_a = o_a.reshape(B, S, GDN_WIDTH).astype(x.dtype)

        o_b = dsa_sparse_attention(heads(q_b, DSA_HEADS), heads(k_b, DSA_HEADS), heads(v_b, DSA_HEADS),
                                   heads(q_i, IDX_HEADS), k_i, w_i, rel_bias)

        merged = jax.nn.sigmoid(gate_a) * (o_a @ w_branch_a[l]) + jax.nn.sigmoid(gate_b) * (o_b @ w_branch_b[l])
        x = layer_norm(DEEPNORM_ALPHA * x + merged @ w_out[l], ln1_g[l], ln1_b[l])

        h_gate, h_up = jnp.split(x @ w_ffn_in[l], 2, axis=-1)
        x = layer_norm(DEEPNORM_ALPHA * x + (jax.nn.silu(h_gate) * h_up) @ w_ffn_out[l], ln2_g[l], ln2_b[l])
    return x
```

```python
import contextlib
import math
import numpy as np
import ml_dtypes
import concourse.bass as bass
import concourse.mybir as mybir
from concourse.bass_utils import run_bass_kernel_spmd

F32 = mybir.dt.float32
BF16 = mybir.dt.bfloat16
AF = mybir.ActivationFunctionType
ALU = mybir.AluOpType
AX = mybir.AxisListType
NPBF = ml_dtypes.bfloat16

D_MODEL = 2048
BATCH = 4
SEQ = 8192
DEPTH = 4
NCORE = 4
D_FF = 5632
D_IN = 11864
ALPHA = (2 * DEPTH) ** 0.25
LN_EPS = 1e-5
RMS_EPS = 1e-6


class Buf:
    __slots__ = ("w", "r", "name")

    def __init__(self, name=""):
        self.w = None
        self.r = {}
        self.name = name


class Prog:
    ND = 24

    def __init__(self):
        self.nc = bass.Bass("TRN2", target_bir_lowering=False)
        nc = self.nc
        self.es = contextlib.ExitStack()
        self.eng = {"pe": nc.tensor, "act": nc.scalar, "dve": nc.vector, "pool": nc.gpsimd, "sp": nc.sync}
        self.sem = {e: self.es.enter_context(nc.semaphore("s_" + e)) for e in self.eng}
        self.dsem = [self.es.enter_context(nc.semaphore("d%d" % i)) for i in range(self.ND)]
        self.cnt = {e: 0 for e in self.eng}
        self.pending = {e: False for e in self.eng}
        self.known = {e: {} for e in self.eng}
        self.ndma = 0
        self.nins = 0
        self._names = 0
        self.pes = None

    def dram(self, name, shape, dt, kind):
        return self.nc.dram_tensor(name, list(shape), dt, kind=kind).ap()

    def sb(self, shape, dt, name=None):
        self._names += 1
        st = self.pes if self.pes is not None else self.es
        return st.enter_context(self.nc.sbuf_tensor(name or ("sb%d" % self._names), list(shape), dt))

    def ps(self, shape, dt=F32, name=None):
        self._names += 1
        st = self.pes if self.pes is not None else self.es
        return st.enter_context(self.nc.psum_tensor(name or ("ps%d" % self._names), list(shape), dt))

    def scratch(self, name, shape, dt):
        return self.nc.dram_tensor(name, list(shape), dt, kind="Internal").ap()

    def barrier(self):
        for e in self.pending:
            assert not self.pending[e], "engine %s has un-inc'd instruction at barrier" % e
        deps = [("e", f, self.cnt[f]) for f in self.eng if self.cnt[f] > 0]
        for i in range(min(self.ndma, self.ND)):
            last = self.ndma - 1 - ((self.ndma - 1 - i) % self.ND)
            deps.append(("d", i, 16 * (last // self.ND + 1)))
        for e in self.eng:
            self._wait(e, [d for d in deps if not (d[0] == "e" and d[1] == e)])

    @contextlib.contextmanager
    def phase(self):
        assert self.pes is None
        self.pes = contextlib.ExitStack()
        try:
            yield
            self.barrier()
        finally:
            self.pes.close()
            self.pes = None

    def _deps(self, reads, writes):
        deps = []
        for b in reads:
            if b.w is not None:
                deps.append(b.w)
        for b in writes:
            if b.w is not None:
                deps.append(b.w)
            deps.extend(b.r.values())
        return deps

    def _wait(self, e, deps):
        kn = self.known[e]
        need = {}
        for (kind, key, val) in deps:
            if kind == "e" and key == e and e == "pe":
                continue
            k = (kind, key)
            if kn.get(k, 0) >= val:
                continue
            if need.get(k, 0) < val:
                need[k] = val
        for (kind, key), val in need.items():
            if kind == "e" and key == e:
                assert val <= self.cnt[e], "self-wait on pending (un-inc'd) instruction"
            s = self.sem[key] if kind == "e" else self.dsem[key]
            self.eng[e].wait_ge(s, val)
            kn[(kind, key)] = val
            self.nins += 1

    def _mark(self, tok, reads, writes):
        k = (tok[0], tok[1])
        for b in reads:
            b.r[k] = tok
        for b in writes:
            b.w = tok
            b.r = {}

    def op(self, e, fn, reads=(), writes=(), inc=True):
        self._wait(e, self._deps(reads, writes))
        ins = fn(self.eng[e])
        if inc:
            self.cnt[e] += 1
            ins.then_inc(self.sem[e], 1)
            tok = ("e", e, self.cnt[e])
            self.pending[e] = False
        else:
            tok = ("e", e, self.cnt[e] + 1)
            self.pending[e] = True
        self.nins += 1
        self._mark(tok, reads, writes)
        return ins

    def dma(self, q, out, in_, reads=(), writes=(), sink=None, **kw):
        i = self.ndma
        self.ndma += 1
        s = i % self.ND
        val = 16 * (i // self.ND + 1)
        deps = self._deps(reads, writes)
        if val > 16:
            deps.append(("d", s, val - 16))
        self._wait(q, deps)
        ins = self.eng[q].dma_start(out=out, in_=in_, **kw)
        ins.then_inc(self.dsem[s], 16)
        self.nins += 1
        self._mark(("d", s, val), reads, writes)
        if sink is not None:
            sink.append(("d", s, val))
        return ins

    def finish(self, sinks):
        deps = []
        for b in sinks:
            deps.extend(b)
        self._wait("sp", deps)
        for e in self.pending:
            assert not self.pending[e], "engine %s ends with un-inc'd instruction" % e

    def close(self):
        self.es.close()


class Ring:
    def __init__(self, p, n, shape, dt, psum=False):
        self.items = []
        for _ in range(n):
            t = p.ps(shape, dt) if psum else p.sb(shape, dt)
            self.items.append((t, Buf()))
        self.i = 0

    def next(self):
        it = self.items[self.i % len(self.items)]
        self.i += 1
        return it


def run(prog, in_maps):
    return run_bass_kernel_spmd(prog.nc, in_maps, core_ids=list(range(len(in_maps))))


CAST_CH = 4096


def emit_cast(p, src2d, dst2d, rows, cols, rings, k0=0):
    st, ob = rings
    sv = src2d.rearrange("(p a) c -> p (a c)", p=128)
    dv = dst2d.rearrange("(p a) c -> p (a c)", p=128)
    m = rows // 128 * cols
    k = k0
    for c0 in range(0, m, CAST_CH):
        n = min(CAST_CH, m - c0)
        s_t, s_b = st.next()
        o_t, o_b = ob.next()
        p.dma("sp", s_t[:, :n], sv[:, c0:c0 + n], writes=[s_b])
        e = ("dve", "act", "pool")[k % 3]
        k += 1
        if e == "act":
            p.op(e, lambda en: en.copy(out=o_t[:, :n], in_=s_t[:, :n]), reads=[s_b], writes=[o_b])
        else:
            p.op(e, lambda en: en.tensor_copy(out=o_t[:, :n], in_=s_t[:, :n]), reads=[s_b], writes=[o_b])
        p.dma("sp", dv[:, c0:c0 + n], o_t[:, :n], reads=[o_b])
    return k


C_QKV, C_A, C_Z, C_QB, C_KB, C_VB, C_QI, C_KI, C_WI, C_GA = 0, 3072, 3088, 4112, 5136, 6160, 7184, 7696, 7760, 7768


def proj_groups():
    g = []
    for i in range(6):
        g.append(("F", C_QKV + 512 * i, 512, "qkvT", 512 * i))
    g.append(("T", C_A, 16, "ab", 0))
    for i in range(2):
        g.append(("T", C_Z + 512 * i, 512, "z", 512 * i))
    for i in range(2):
        g.append(("F", C_QB + 512 * i, 512, "qbT", 512 * i))
    for i in range(2):
        g.append(("F", C_KB + 512 * i, 512, "kbT", 512 * i))
    for i in range(2):
        g.append(("T", C_VB + 512 * i, 512, "vb", 512 * i))
    g.append(("F", C_QI, 512, "qiT", 0))
    g.append(("F", C_KI, 64, "kiT", 0))
    g.append(("T", C_WI, 8, "wi", 0))
    for i in range(8):
        g.append(("F", C_GA + 512 * i, 512, "sgT", 512 * i))
    return g


SCR_SPEC = {
    "qkvT": (lambda S: [3072, S], F32), "qbT": (lambda S: [1024, S], BF16), "kbT": (lambda S: [1024, S], BF16),
    "qiT": (lambda S: [512, S], BF16), "kiT": (lambda S: [64, S], BF16), "sgT": (lambda S: [4096, S], F32),
    "ab": (lambda S: [S, 16], F32), "z": (lambda S: [S, 1024], F32), "vb": (lambda S: [S, 1024], BF16),
    "wi": (lambda S: [S, 8], F32), "oaT": (lambda S: [1024, S], BF16), "obT": (lambda S: [1024, S], BF16),
    "x1": (lambda S: [S, 2048], F32), "xA": (lambda S: [S, 2048], F32), "xB": (lambda S: [S, 2048], F32),
    "xTb": (lambda S: [2048, S], BF16),
}


def emit_proj(p, S, xTb, w, scr):
    TG = min(2048, S)
    KC = D_MODEL // 128
    wv = w.rearrange("(kc p) n -> p kc n", p=128)
    xb = [(p.sb([128, TG], BF16), Buf()) for _ in range(KC)]
    wr = Ring(p, 2, [128, KC, 512], BF16)
    pr = Ring(p, 6, [128, 512], F32, psum=True)
    sf = Ring(p, 4, [128, 512], F32)
    sh = Ring(p, 4, [128, 512], BF16)
    ev = 0
    for tg in range(S // TG):
        t0 = tg * TG
        for kc in range(KC):
            xt, xbuf = xb[kc]
            p.dma("sp", xt[:], xTb[kc * 128:(kc + 1) * 128, t0:t0 + TG], writes=[xbuf])
        for (mode, c0, n, oname, r0) in proj_groups():
            w_t, w_b = wr.next()
            p.dma("sp", w_t[:, :, :n], wv[:, :, c0:c0 + n], writes=[w_b])
            o_ap = scr[oname]
            o_dt = SCR_SPEC[oname][1]
            stg = sf if o_dt == F32 else sh
            if mode == "F":
                for ci in range(0, n, 128):
                    cn = min(128, n - ci)
                    for tt in range(TG // 512):
                        ps_t, ps_b = pr.next()
                        for kc in range(KC):
                            xt, xbuf = xb[kc]
                            p.op("pe", lambda en: en.matmul(ps_t[:cn, :], lhsT=w_t[:, kc, ci:ci + cn],
                                                            rhs=xt[:, tt * 512:(tt + 1) * 512],
                                                            start=(kc == 0), stop=(kc == KC - 1)),
                                 reads=[w_b, xbuf], writes=[ps_b], inc=(kc == KC - 1))
                        g_t, g_b = stg.next()
                        if oname == "sgT":
                            p.op("act", lambda en: en.activation(out=g_t[:cn, :], in_=ps_t[:cn, :], func=AF.Sigmoid),
                                 reads=[ps_b], writes=[g_b])
                        else:
                            ev += 1
                            if ev % 2:
                                p.op("dve", lambda en: en.tensor_copy(out=g_t[:cn, :], in_=ps_t[:cn, :]),
                                     reads=[ps_b], writes=[g_b])
                            else:
                                p.op("act", lambda en: en.copy(out=g_t[:cn, :], in_=ps_t[:cn, :]),
                                     reads=[ps_b], writes=[g_b])
                        p.dma("sp", o_ap[r0 + ci:r0 + ci + cn, t0 + tt * 512:t0 + (tt + 1) * 512], g_t[:cn, :],
                              reads=[g_b])
            else:
                for tt in range(TG // 128):
                    ps_t, ps_b = pr.next()
                    for kc in range(KC):
                        xt, xbuf = xb[kc]
                        p.op("pe", lambda en: en.matmul(ps_t[:, :n], lhsT=xt[:, tt * 128:(tt + 1) * 128],
                                                        rhs=w_t[:, kc, :n],
                                                        start=(kc == 0), stop=(kc == KC - 1)),
                             reads=[w_b, xbuf], writes=[ps_b], inc=(kc == KC - 1))
                    g_t, g_b = stg.next()
                    ev += 1
                    if ev % 2:
                        p.op("dve", lambda en: en.tensor_copy(out=g_t[:, :n], in_=ps_t[:, :n]),
                             reads=[ps_b], writes=[g_b])
                    else:
                        p.op("act", lambda en: en.copy(out=g_t[:, :n], in_=ps_t[:, :n]),
                             reads=[ps_b], writes=[g_b])
                    p.dma("sp", o_ap[t0 + tt * 128:t0 + (tt + 1) * 128, r0:r0 + n], g_t[:, :n],
                          reads=[g_b])


CH = 64


def gdn_consts():
    i = np.arange(CH)
    c = {}
    c["ident"] = np.eye(128, dtype=np.float32)
    c["ones"] = np.ones((128, 128), np.float32)
    c["ucum"] = (i[:, None] <= i[None, :]).astype(np.float32)
    c["stril"] = (i[:, None] > i[None, :]).astype(np.float32)
    c["triu"] = (i[:, None] <= i[None, :]).astype(np.float32)
    return c


def emit_gdn(p, S, scr, cst, l, heads):
    NH = len(heads)
    NCH = S // CH
    TL = 256
    NT = S // TL
    CPT = TL // CH
    qkvT, abd, zin, oaT = scr["qkvT"], scr["ab"], scr["z"], scr["oaT"]

    def cload(ap, shape, dt=F32, out_view=None):
        t = p.sb(shape, dt)
        b = Buf()
        p.dma("sp", out_view(t) if out_view else t[:], ap, writes=[b])
        return t, b

    ident, identb = cload(cst["ident"][:, :], [128, 128])
    ones, onesb = cload(cst["ones"][:, :], [128, 128])
    ucum, ucumb = cload(cst["ucum"][:, :], [CH, CH])
    stril, strilb = cload(cst["stril"][:, :], [CH, CH])
    triu, triub = cload(cst["triu"][:, :], [CH, CH])
    gnw_t, gnwb = cload(cst["gnw"][l, :, :], [CH, 128])
    cw_t, cwb = cload(cst["cw"][l, :, :], [128, 96])
    W = NCH * 16
    ab_t, abb = cload(abd.rearrange("(n c) k -> c n k", c=CH), [CH, W],
                      out_view=lambda t: t[:].rearrange("c (n k) -> c n k", k=16))
    dtb_t, dtbb = cload(cst["dtb16"][l, :, :], [CH, W])
    negA_t, negAb = cload(cst["negA16"][l, :, :], [CH, W])
    epsc = p.sb([128, 1], F32); epsb = Buf()
    p.op("pool", lambda e: e.memset(epsc[:], RMS_EPS), writes=[epsb])
    g_t = p.sb([CH, W], F32); gb = Buf()
    beta_t = p.sb([CH, W], F32); betab = Buf()
    nbeta_t = p.sb([CH, W], F32); nbetab = Buf()
    tmpw = p.sb([CH, W], F32); tmpwb = Buf()
    p.op("dve", lambda e: e.tensor_tensor(out=tmpw[:], in0=ab_t[:], in1=dtb_t[:], op=ALU.add), reads=[abb, dtbb], writes=[tmpwb])
    p.op("act", lambda e: e.activation(out=tmpw[:], in_=tmpw[:], func=AF.Exp), reads=[tmpwb], writes=[tmpwb])
    p.op("act", lambda e: e.activation(out=tmpw[:], in_=tmpw[:], func=AF.Ln, bias=1.0), reads=[tmpwb], writes=[tmpwb])
    p.op("dve", lambda e: e.tensor_tensor(out=g_t[:], in0=tmpw[:], in1=negA_t[:], op=ALU.mult), reads=[tmpwb, negAb], writes=[gb])
    p.op("act", lambda e: e.activation(out=beta_t[:], in_=ab_t[:], func=AF.Sigmoid), reads=[abb], writes=[betab])
    p.op("dve", lambda e: e.tensor_scalar(out=nbeta_t[:], in0=beta_t[:], scalar1=-1.0, scalar2=None, op0=ALU.mult), reads=[betab], writes=[nbetab])

    pp = Ring(p, 8, [128, 512], F32, psum=True)
    xin = Ring(p, 3, [128, TL + 3], F32)

    def mk(shape, dt):
        return [[(p.sb(shape, dt), Buf()) for _ in range(NH)] for _ in range(2)]
    qTf = mk([128, TL], F32); kTf = mk([128, TL], F32); vTf = mk([128, TL], F32)
    qTb = mk([128, TL], BF16); kTb = mk([128, TL], BF16)
    oT = mk([128, TL], BF16)

    def mkc(shape, dt):
        return [[[(p.sb(shape, dt), Buf()) for _ in range(CPT)] for _ in range(NH)] for _ in range(2)]
    TTb = mkc([CH, CH], BF16); intraTb = mkc([CH, CH], BF16); qdecTb = mkc([128, CH], BF16)
    kdecb = mkc([CH, 128], BF16); vbeta = mkc([CH, 128], F32); scol = mkc([CH, 2], F32); gtc = mkc([128, 1], F32)
    zc = mkc([CH, 128], F32)
    s64 = Ring(p, 14, [CH, CH], F32)
    s128 = Ring(p, 4, [128, CH], F32)
    scol_r = Ring(p, 8, [CH, 4], F32)
    cvt = Ring(p, 4, [128, TL], F32)
    St = [(p.sb([128, 128], F32), Buf()) for _ in range(NH)]
    Sb = [(p.sb([128, 128], BF16), Buf()) for _ in range(NH)]
    for h in range(NH):
        p.op("pool", lambda e: e.memset(St[h][0][:], 0.0), writes=[St[h][1]])
        p.op("pool", lambda e: e.memset(Sb[h][0][:], 0.0), writes=[Sb[h][1]])
    rn_r = Ring(p, 4, [CH, 128], BF16)
    vn_r = Ring(p, 4, [CH, 128], BF16)
    o_r = Ring(p, 4, [CH, 128], F32)
    o2_r = Ring(p, 8, [CH, 128], F32)
    otb_r = Ring(p, 2, [128, CH], BF16)
    zraw = Ring(p, 3, [CH, 128], F32)

    for ti in range(NT):
        par = ti % 2
        t0 = ti * TL
        for h in range(NH):
            hg = heads[h]
            for a in range(3):
                r0 = a * 1024 + hg * 128
                x_t, x_b = xin.next()
                if t0 == 0:
                    p.op("pool", lambda e: e.memset(x_t[:, 0:3], 0.0), writes=[x_b])
                    p.dma("sp", x_t[:, 3:TL + 3], qkvT[r0:r0 + 128, 0:TL], writes=[x_b])
                else:
                    p.dma("sp", x_t[:, :], qkvT[r0:r0 + 128, t0 - 3:t0 + TL], writes=[x_b])
                c_t, c_b = cvt.next()
                wcol = lambda j: cw_t[:, (a * 8 + hg) * 4 + j:(a * 8 + hg) * 4 + j + 1]
                p.op("dve", lambda e: e.tensor_scalar(out=c_t[:], in0=x_t[:, 0:TL], scalar1=wcol(0), scalar2=None, op0=ALU.mult),
                     reads=[x_b, cwb], writes=[c_b])
                for j in range(1, 4):
                    p.op("dve", lambda e: e.scalar_tensor_tensor(out=c_t[:], in0=x_t[:, j:j + TL], scalar=wcol(j), in1=c_t[:],
                                                                  op0=ALU.mult, op1=ALU.add), reads=[x_b, cwb, c_b], writes=[c_b])
                dstf = (qTf, kTf, vTf)[a][par][h]
                p.op("act", lambda e: e.activation(out=dstf[0][:], in_=c_t[:], func=AF.Silu), reads=[c_b], writes=[dstf[1]])
                if a < 2:
                    p.op("pool", lambda e: e.tensor_tensor(out=c_t[:], in0=dstf[0][:], in1=dstf[0][:], op=ALU.mult),
                         reads=[dstf[1]], writes=[c_b])
                    ps_t, ps_b = pp.next()
                    p.op("pe", lambda e: e.matmul(ps_t[:, :TL], lhsT=ones[:, :], rhs=c_t[:], start=True, stop=True),
                         reads=[onesb, c_b], writes=[ps_b])
                    p.op("act", lambda e: e.activation(out=c_t[:], in_=ps_t[:, :TL], func=AF.Sqrt, bias=epsc[:, 0:1]),
                         reads=[ps_b, epsb], writes=[c_b])
                    p.op("dve", lambda e: e.reciprocal(out=c_t[:], in_=c_t[:]), reads=[c_b], writes=[c_b])
                    sc = (128 ** -0.5) if a == 0 else 1.0
                    p.op("dve", lambda e: e.scalar_tensor_tensor(out=dstf[0][:], in0=dstf[0][:], scalar=sc, in1=c_t[:],
                                                                  op0=ALU.mult, op1=ALU.mult), reads=[dstf[1], c_b], writes=[dstf[1]])
                    dstb = (qTb, kTb)[a][par][h]
                    p.op("pool", lambda e: e.tensor_copy(out=dstb[0][:], in_=dstf[0][:]), reads=[dstf[1]], writes=[dstb[1]])
            for ci in range(CPT):
                n = ti * CPT + ci
                colg = n * 16 + hg
                colb = n * 16 + 8 + hg
                cs = slice(ci * CH, (ci + 1) * CH)
                gcol = g_t[:, colg:colg + 1]
                zr_t, zr_b = zraw.next()
                p.dma("sp", zr_t[:], zin[n * CH:(n + 1) * CH, hg * 128:(hg + 1) * 128], writes=[zr_b])
                p.op("act", lambda e: e.activation(out=zc[par][h][ci][0][:], in_=zr_t[:], func=AF.Silu),
                     reads=[zr_b], writes=[zc[par][h][ci][1]])
                psA, psAb = pp.next()
                gbw_t, gbw_b = cvt.next()
                p.op("dve", lambda e: e.tensor_scalar(out=gbw_t[:CH, :128], in0=ones[:CH, :128], scalar1=gcol, scalar2=None, op0=ALU.mult),
                     reads=[onesb, gb], writes=[gbw_b])
                p.op("pe", lambda e: e.matmul(psA[:, 0:CH], lhsT=gbw_t[:CH, :128], rhs=ucum[:, :], start=True, stop=True),
                     reads=[gbw_b, ucumb], writes=[psAb], inc=False)
                p.op("pe", lambda e: e.matmul(psA[:CH, CH:2 * CH], lhsT=ucum[:, :], rhs=gbw_t[:CH, :CH], start=True, stop=True),
                     reads=[ucumb, gbw_b], writes=[psAb])
                sc_t, sc_b = scol[par][h][ci]
                gc_t, gc_b = scol_r.next()
                p.op("act", lambda e: e.copy(out=gc_t[:, 0:1], in_=psA[:CH, CH:CH + 1]), reads=[psAb], writes=[gc_b])
                p.op("act", lambda e: e.copy(out=gc_t[:, 1:2], in_=psA[:CH, CH - 1:CH]), reads=[psAb], writes=[gc_b])
                p.op("act", lambda e: e.activation(out=gc_t[:, 2:3], in_=gc_t[:, 0:1], func=AF.Exp), reads=[gc_b], writes=[gc_b])
                p.op("dve", lambda e: e.tensor_tensor(out=sc_t[:, 0:1], in0=gc_t[:, 2:3], in1=nbeta_t[:, colb:colb + 1], op=ALU.mult),
                     reads=[gc_b, nbetab], writes=[sc_b])
                p.op("act", lambda e: e.activation(out=sc_t[:, 1:2], in_=gc_t[:, 0:1], func=AF.Exp, scale=-1.0, bias=gc_t[:, 1:2]),
                     reads=[gc_b, sc_b], writes=[sc_b])
                gt_t, gt_b = gtc[par][h][ci]
                p.op("act", lambda e: e.activation(out=gt_t[:, :], in_=psA[:, CH - 1:CH], func=AF.Exp), reads=[psAb], writes=[gt_b])
                gcr_t, gcr_b = s64.next()
                p.op("act", lambda e: e.copy(out=gcr_t[:], in_=psA[:CH, 0:CH]), reads=[psAb], writes=[gcr_b])
                eg_t, eg_b = s128.next()
                p.op("act", lambda e: e.activation(out=eg_t[:, :], in_=psA[:, 0:CH], func=AF.Exp), reads=[psAb], writes=[eg_b])
                e1_t, e1_b = s64.next()
                p.op("dve", lambda e: e.tensor_scalar(out=e1_t[:], in0=gcr_t[:], scalar1=gc_t[:, 0:1], scalar2=0.0,
                                                      op0=ALU.subtract, op1=ALU.max), reads=[gcr_b, gc_b], writes=[e1_b])
                p.op("act", lambda e: e.activation(out=e1_t[:], in_=e1_t[:], func=AF.Exp, scale=-1.0), reads=[e1_b], writes=[e1_b])
                e2_t, e2_b = s64.next()
                p.op("dve", lambda e: e.tensor_scalar(out=e2_t[:], in0=gcr_t[:], scalar1=gc_t[:, 0:1], scalar2=0.0,
                                                      op0=ALU.subtract, op1=ALU.min), reads=[gcr_b, gc_b], writes=[e2_b])
                p.op("act", lambda e: e.activation(out=e2_t[:], in_=e2_t[:], func=AF.Exp), reads=[e2_b], writes=[e2_b])
                p.op("pool", lambda e: e.tensor_tensor(out=e2_t[:], in0=e2_t[:], in1=triu[:, :], op=ALU.mult), reads=[e2_b, triub], writes=[e2_b])
                qd_t, qd_b = qdecTb[par][h][ci]
                p.op("pool", lambda e: e.tensor_tensor(out=qd_t[:], in0=qTf[par][h][0][:, cs], in1=eg_t[:, :], op=ALU.mult),
                     reads=[qTf[par][h][1], eg_b], writes=[qd_b])
                psB, psBb = pp.next()
                kb_t, kb_b = kTb[par][h]
                qb_t, qb_b = qTb[par][h]
                p.op("pe", lambda e: e.matmul(psB[:CH, 0:CH], lhsT=kb_t[:, cs], rhs=kb_t[:, cs], start=True, stop=True),
                     reads=[kb_b], writes=[psBb], inc=False)
                p.op("pe", lambda e: e.matmul(psB[:CH, CH:2 * CH], lhsT=kb_t[:, cs], rhs=qb_t[:, cs], start=True, stop=True),
                     reads=[kb_b, qb_b], writes=[psBb], inc=False)
                p.op("pe", lambda e: e.transpose(psB[:CH, 128:256], kTf[par][h][0][:, cs], ident[:, :]),
                     reads=[kTf[par][h][1], identb], writes=[psBb], inc=False)
                p.op("pe", lambda e: e.transpose(psB[:CH, 256:384], vTf[par][h][0][:, cs], ident[:, :]),
                     reads=[vTf[par][h][1], identb], writes=[psBb])
                it_t, it_b = intraTb[par][h][ci]
                p.op("dve", lambda e: e.tensor_tensor(out=it_t[:], in0=psB[:CH, CH:2 * CH], in1=e2_t[:], op=ALU.mult),
                     reads=[psBb, e2_b], writes=[it_b])
                kd_t, kd_b = kdecb[par][h][ci]
                p.op("dve", lambda e: e.tensor_scalar(out=kd_t[:], in0=psB[:CH, 128:256], scalar1=sc_t[:, 1:2], scalar2=None, op0=ALU.mult),
                     reads=[psBb, sc_b], writes=[kd_b])
                vb_t, vb_b = vbeta[par][h][ci]
                p.op("dve", lambda e: e.tensor_scalar(out=vb_t[:], in0=psB[:CH, 256:384], scalar1=beta_t[:, colb:colb + 1], scalar2=None, op0=ALU.mult),
                     reads=[psBb, betab], writes=[vb_b])
                n_t, n_b = s64.next()
                p.op("dve", lambda e: e.tensor_tensor(out=n_t[:], in0=psB[:CH, 0:CH], in1=e1_t[:], op=ALU.mult), reads=[psBb, e1_b], writes=[n_b])
                p.op("dve", lambda e: e.scalar_tensor_tensor(out=n_t[:], in0=n_t[:], scalar=nbeta_t[:, colb:colb + 1], in1=stril[:, :],
                                                              op0=ALU.mult, op1=ALU.mult), reads=[n_b, nbetab, strilb], writes=[n_b])
                psC, psCb = pp.next()
                p.op("pe", lambda e: e.transpose(psC[:CH, 0:CH], n_t[:, :], ident[:CH, :CH]), reads=[n_b, identb], writes=[psCb])
                nt_t, nt_b = s64.next()
                p.op("dve", lambda e: e.tensor_copy(out=nt_t[:], in_=psC[:CH, 0:CH]), reads=[psCb], writes=[nt_b])
                tt_t, tt_b = s64.next()
                p.op("dve", lambda e: e.tensor_tensor(out=tt_t[:], in0=psC[:CH, 0:CH], in1=ident[:CH, :CH], op=ALU.add),
                     reads=[psCb, identb], writes=[tt_b])
                P_t, P_b, PT_t, PT_b = n_t, n_b, nt_t, nt_b
                for k in range(1, 6):
                    psD, psDb = pp.next()
                    p.op("pe", lambda e: e.matmul(psD[:CH, 0:CH], lhsT=PT_t[:, :], rhs=P_t[:, :], start=True, stop=True),
                         reads=[PT_b, P_b], writes=[psDb], inc=(k == 5))
                    if k < 5:
                        p.op("pe", lambda e: e.matmul(psD[:CH, CH:2 * CH], lhsT=P_t[:, :], rhs=PT_t[:, :], start=True, stop=True),
                             reads=[PT_b, P_b], writes=[psDb])
                    nP_t, nP_b = s64.next()
                    p.op("act", lambda e: e.copy(out=nP_t[:], in_=psD[:CH, 0:CH]), reads=[psDb], writes=[nP_b])
                    if k < 5:
                        nPT_t, nPT_b = s64.next()
                        p.op("act", lambda e: e.copy(out=nPT_t[:], in_=psD[:CH, CH:2 * CH]), reads=[psDb], writes=[nPT_b])
                    psE, psEb = pp.next()
                    p.op("pe", lambda e: e.matmul(psE[:CH, 0:CH], lhsT=nP_t[:, :], rhs=tt_t[:, :], start=True, stop=True),
                         reads=[nP_b, tt_b], writes=[psEb])
                    ntt_t, ntt_b = s64.next()
                    p.op("dve", lambda e: e.tensor_tensor(out=ntt_t[:], in0=psE[:CH, 0:CH], in1=tt_t[:], op=ALU.add),
                         reads=[psEb, tt_b], writes=[ntt_b])
                    tt_t, tt_b = ntt_t, ntt_b
                    P_t, P_b = nP_t, nP_b
                    if k < 5:
                        PT_t, PT_b = nPT_t, nPT_b
                ttb_t, ttb_b = TTb[par][h][ci]
                p.op("pool", lambda e: e.tensor_copy(out=ttb_t[:], in_=tt_t[:]), reads=[tt_b], writes=[ttb_b])
        for ci in range(CPT):
            n = ti * CPT + ci
            cs = slice(ci * CH, (ci + 1) * CH)
            for h in range(NH):
                hg = heads[h]
                S_t, S_b = St[h]
                Sb_t, Sb_b = Sb[h]
                ps1, ps1b = pp.next()
                p.op("pe", lambda e: e.matmul(ps1[:CH, 0:128], lhsT=kTb[par][h][0][:, cs], rhs=Sb_t[:, :], start=True, stop=True),
                     reads=[kTb[par][h][1], Sb_b], writes=[ps1b])
                rn_t, rn_b = rn_r.next()
                sc_t, sc_b = scol[par][h][ci]
                vb_t, vb_b = vbeta[par][h][ci]
                p.op("dve", lambda e: e.scalar_tensor_tensor(out=rn_t[:], in0=ps1[:CH, 0:128], scalar=sc_t[:, 0:1], in1=vb_t[:],
                                                              op0=ALU.mult, op1=ALU.add), reads=[ps1b, sc_b, vb_b], writes=[rn_b])
                ps2, ps2b = pp.next()
                p.op("pe", lambda e: e.matmul(ps2[:CH, 0:128], lhsT=TTb[par][h][ci][0][:, :], rhs=rn_t[:, :], start=True, stop=True),
                     reads=[TTb[par][h][ci][1], rn_b], writes=[ps2b])
                vn_t, vn_b = vn_r.next()
                p.op("act", lambda e: e.copy(out=vn_t[:], in_=ps2[:CH, 0:128]), reads=[ps2b], writes=[vn_b])
                p.op("pe", lambda e: e.matmul(ps2[:CH, 128:256], lhsT=qdecTb[par][h][ci][0][:, :], rhs=Sb_t[:, :], start=True, stop=False),
                     reads=[qdecTb[par][h][ci][1], Sb_b], writes=[ps2b], inc=False)
                p.op("pe", lambda e: e.matmul(ps2[:CH, 128:256], lhsT=intraTb[par][h][ci][0][:, :], rhs=vn_t[:, :], start=False, stop=True),
                     reads=[intraTb[par][h][ci][1], vn_b], writes=[ps2b], inc=False)
                ps3, ps3b = pp.next()
                p.op("pe", lambda e: e.matmul(ps3[:, 0:128], lhsT=kdecb[par][h][ci][0][:, :], rhs=vn_t[:, :], start=True, stop=True),
                     reads=[kdecb[par][h][ci][1], vn_b], writes=[ps2b, ps3b])
                gt_t, gt_b = gtc[par][h][ci]
                p.op("dve", lambda e: e.scalar_tensor_tensor(out=S_t[:], in0=S_t[:], scalar=gt_t[:, 0:1], in1=ps3[:, 0:128],
                                                              op0=ALU.mult, op1=ALU.add), reads=[S_b, gt_b, ps3b], writes=[S_b])
                p.op("pool", lambda e: e.tensor_copy(out=Sb_t[:], in_=S_t[:]), reads=[S_b], writes=[Sb_b])
                o_t, o_b = o_r.next()
                ss_t, ss_b = scol_r.next()
                oraw_t, oraw_b = o2_r.next()
                p.op("act", lambda e: e.copy(out=oraw_t[:], in_=ps2[:CH, 128:256]), reads=[ps2b], writes=[oraw_b])
                p.op("act", lambda e: e.activation(out=o_t[:], in_=oraw_t[:], func=AF.Square), reads=[oraw_b], writes=[o_b])
                p.op("dve", lambda e: e.reduce_sum(out=ss_t[:, 0:1], in_=o_t[:], axis=AX.X), reads=[o_b], writes=[ss_b])
                p.op("act", lambda e: e.activation(out=ss_t[:, 1:2], in_=ss_t[:, 0:1], func=AF.Sqrt, scale=1.0 / 128, bias=epsc[:CH, 0:1]),
                     reads=[ss_b, epsb], writes=[ss_b])
                p.op("dve", lambda e: e.reciprocal(out=ss_t[:, 2:3], in_=ss_t[:, 1:2]), reads=[ss_b], writes=[ss_b])
                p.op("dve", lambda e: e.scalar_tensor_tensor(out=o_t[:], in0=oraw_t[:], scalar=ss_t[:, 2:3], in1=gnw_t[:, :],
                                                              op0=ALU.mult, op1=ALU.mult), reads=[oraw_b, ss_b, gnwb, o_b], writes=[o_b])
                o2_t, o2_b = o2_r.next()
                p.op("pool", lambda e: e.tensor_tensor(out=o2_t[:], in0=o_t[:], in1=zc[par][h][ci][0][:], op=ALU.mult),
                     reads=[o_b, zc[par][h][ci][1]], writes=[o2_b])
                ps4, ps4b = pp.next()
                p.op("pe", lambda e: e.transpose(ps4[:, 0:CH], o2_t[:, :], ident[:CH, :CH]), reads=[o2_b, identb], writes=[ps4b])
                oT_t, oT_b = oT[par][h]
                p.op("act", lambda e: e.copy(out=oT_t[:, cs], in_=ps4[:, 0:CH]), reads=[ps4b], writes=[oT_b])
                if ci == CPT - 1:
                    p.dma("sp", oaT[hg * 128:(hg + 1) * 128, t0:t0 + TL], oT_t[:, :], reads=[oT_b])


NEG_CAUSAL = -1.0e30
NEG_TAKEN = -2.0e30


def t5_bucket_np(dist):
    n = np.maximum(dist, 0)
    max_exact = 16
    lr = np.log(np.maximum(n, 1).astype(np.float32) / np.float32(max_exact)) / np.float32(math.log(128 / max_exact))
    large = max_exact + (lr * np.float32(32 - max_exact)).astype(np.int32)
    large = np.minimum(large, 31)
    return np.where(n < max_exact, n, large)


def dsa_consts(rel_bias):
    c = {}
    sp = np.arange(128)[:, None]
    tq = np.arange(512)[None, :]
    nb = np.zeros((8, 5, 128, 512), np.float32)
    for r in range(-1, 4):
        dist = tq - (r * 128 + sp)
        b = t5_bucket_np(dist)
        for h in range(8):
            nb[h, r + 1] = np.where(dist >= 0, rel_bias[b, h], np.float32(-30000.0))
    c["nb"] = nb
    c["cbias"] = np.ascontiguousarray(np.broadcast_to(rel_bias[31][None, :], (128, 8))).astype(np.float32)
    cadd = np.zeros((4, 128, 512), np.float32)
    for qi in range(4):
        cadd[qi] = np.where(np.arange(512)[None, :] <= qi * 128 + np.arange(128)[:, None], 0.0, NEG_CAUSAL)
    c["cadd"] = cadd
    return c


def emit_dsa(p, S, scr, cst):
    NG = S // 512
    topk = min(256, S // 4)
    NR = topk // 8
    SCALE = 128 ** -0.5
    qiT, kiT, wi, qbT, kbT, vb, obT = scr["qiT"], scr["kiT"], scr["wi"], scr["qbT"], scr["kbT"], scr["vb"], scr["obT"]

    identf = p.sb([128, 128], F32); identfb = Buf()
    p.dma("sp", identf[:], cst["ident"][:, :], writes=[identfb])
    identb = p.sb([128, 128], BF16); identbb = Buf()
    p.op("pool", lambda e: e.tensor_copy(out=identb[:], in_=identf[:]), reads=[identfb], writes=[identbb])
    cadd_t = p.sb([128, 4, 512], F32); caddb = Buf()
    p.dma("sp", cadd_t[:], cst["cadd"].rearrange("q p s -> p q s"), writes=[caddb])
    cbias_t = p.sb([128, 8], F32); cbiasb = Buf()
    p.dma("sp", cbias_t[:], cst["cbias"][:, :], writes=[cbiasb])
    kiT2 = p.sb([128, S], BF16); kib = Buf()
    p.dma("sp", kiT2[0:64, :], kiT[:, :], writes=[kib])
    p.dma("sp", kiT2[64:128, :], kiT[:, :], writes=[kib])
    sc = p.sb([128, S], F32); scb = Buf()
    maskT = p.sb([128, S // 128, 512], BF16); maskTb = Buf()

    lg = Ring(p, 2, [128, 512], F32, psum=True)
    ops = Ring(p, 4, [128, 512], F32, psum=True)
    trb = Ring(p, 1, [128, 512], BF16, psum=True)
    ptf = Ring(p, 1, [128, 512], F32, psum=True)
    qi_r = Ring(p, 2, [128, 4, 128], BF16)
    wi_r = Ring(p, 2, [128, 8], F32)
    aw_r = Ring(p, 2, [128, 8], F32)
    sg_r = Ring(p, 2, [128, 8], F32)
    relu_r = Ring(p, 3, [128, 512], F32)
    m8_r = Ring(p, 4, [128, 8], F32)
    mk_r = Ring(p, 2, [128, 512], BF16)
    qT_r = Ring(p, 2, [128, 512], BF16)
    kT_r = Ring(p, 3, [128, 512], BF16)
    v_r = Ring(p, 3, [128, 4, 129], BF16)
    for (v_t, v_b) in v_r.items:
        p.op("pool", lambda e: e.memset(v_t[:, :, 128:129], 1.0), writes=[v_b])
    nb_r = Ring(p, 5, [128, 512], F32)
    lgt_r = Ring(p, 2, [128, 512], F32)
    P_r = Ring(p, 3, [128, 512], BF16)
    Pm_r = Ring(p, 3, [128, 512], BF16)
    on_r = Ring(p, 2, [128, 128], F32)
    rc_r = Ring(p, 4, [128, 1], F32)
    obT_r = Ring(p, 2, [128, 512], BF16)

    for g in range(NG):
        L = 512 * (g + 1)
        for qi in range(4):
            t0 = g * 512 + qi * 128
            q_t, q_b = qi_r.next()
            p.dma("sp", q_t[:], qiT[:, t0:t0 + 128].rearrange("(hp p) t -> p hp t", p=128), writes=[q_b])
            w_t, w_b = wi_r.next()
            p.dma("sp", w_t[:], wi[t0:t0 + 128, :], writes=[w_b])
            aw_t, aw_b = aw_r.next()
            sg_t, sg_b = sg_r.next()
            p.op("act", lambda e: e.activation(out=aw_t[:], in_=w_t[:], func=AF.Abs), reads=[w_b], writes=[aw_b])
            p.op("act", lambda e: e.activation(out=sg_t[:], in_=w_t[:], func=AF.Sign), reads=[w_b], writes=[sg_b])
            for j in range(g + 1):
                js = slice(j * 512, (j + 1) * 512)
                for h in range(8):
                    hp, off = h // 2, (h % 2) * 64
                    ps, psb = lg.next()
                    p.op("pe", lambda e: e.matmul(ps[:, :], lhsT=q_t[off:off + 64, hp, :], rhs=kiT2[off:off + 64, js], start=True, stop=True),
                         reads=[q_b, kib], writes=[psb])
                    r_t, r_b = relu_r.next()
                    p.op("act", lambda e: e.activation(out=r_t[:], in_=ps[:, :], func=AF.Relu, scale=aw_t[:, h:h + 1]),
                         reads=[psb, aw_b], writes=[r_b])
                    if h == 0:
                        p.op("dve", lambda e: e.tensor_scalar(out=sc[:, js], in0=r_t[:], scalar1=sg_t[:, 0:1], scalar2=None, op0=ALU.mult),
                             reads=[r_b, sg_b], writes=[scb])
                    else:
                        p.op("dve", lambda e: e.scalar_tensor_tensor(out=sc[:, js], in0=r_t[:], scalar=sg_t[:, h:h + 1], in1=sc[:, js],
                                                                      op0=ALU.mult, op1=ALU.add), reads=[r_b, sg_b, scb], writes=[scb])
            ds = slice(g * 512, (g + 1) * 512)
            p.op("dve", lambda e: e.tensor_tensor(out=sc[:, ds], in0=sc[:, ds], in1=cadd_t[:, qi, :], op=ALU.add),
                 reads=[scb, caddb], writes=[scb])
            for r in range(NR):
                m_t, m_b = m8_r.next()
                p.op("dve", lambda e: e.max(out=m_t[:], in_=sc[:, :L]), reads=[scb], writes=[m_b])
                p.op("dve", lambda e: e.match_replace(out=sc[:, :L], in_to_replace=m_t[:], in_values=sc[:, :L], imm_value=NEG_TAKEN),
                     reads=[scb, m_b], writes=[scb])
            for j in range(g + 1):
                js = slice(j * 512, (j + 1) * 512)
                mk_t, mk_b = mk_r.next()
                p.op("dve", lambda e: e.tensor_single_scalar(out=mk_t[:], in_=sc[:, js], scalar=-1.5e30, op=ALU.is_lt),
                     reads=[scb], writes=[mk_b])
                tp, tpb = trb.next()
                for i in range(4):
                    p.op("pe", lambda e: e.transpose(tp[:, i * 128:(i + 1) * 128], mk_t[:, i * 128:(i + 1) * 128], identb[:, :]),
                         reads=[mk_b, identbb], writes=[tpb], inc=(i == 3))
                p.op("act", lambda e: e.copy(out=maskT[:, 4 * j:4 * j + 4, qi * 128:(qi + 1) * 128],
                                             in_=tp[:, :].rearrange("p (a b) -> p a b", b=128)),
                     reads=[tpb], writes=[maskTb])
        for h in range(8):
            qT_t, qT_b = qT_r.next()
            p.dma("sp", qT_t[:], qbT[h * 128:(h + 1) * 128, g * 512:(g + 1) * 512], writes=[qT_b])
            nbt = {}
            for r in range(-1 if g > 0 else 0, 4):
                n_t, n_b = nb_r.next()
                p.dma("sp", n_t[:], cst["nb"][h, r + 1, :, :], writes=[n_b])
                nbt[r] = (n_t, n_b)
            oacc = [ops.next() for _ in range(4)]
            for j in range(g + 1):
                kT_t, kT_b = kT_r.next()
                p.dma("sp", kT_t[:], kbT[h * 128:(h + 1) * 128, j * 512:(j + 1) * 512], writes=[kT_b])
                v_t, v_b = v_r.next()
                p.dma("sp", v_t[:, :, 0:128], vb[j * 512:(j + 1) * 512, h * 128:(h + 1) * 128].rearrange("(kb p) e -> p kb e", p=128),
                      writes=[v_b])
                for kbi in range(4):
                    kb = 4 * j + kbi
                    r = kb - 4 * g
                    ps, psb = lg.next()
                    p.op("pe", lambda e: e.matmul(ps[:, :], lhsT=kT_t[:, kbi * 128:(kbi + 1) * 128], rhs=qT_t[:, :], start=True, stop=True),
                         reads=[kT_b, qT_b], writes=[psb])
                    P_t, P_b = P_r.next()
                    if r >= -1:
                        n_t, n_b = nbt[r]
                        l_t, l_b = lgt_r.next()
                        p.op("dve", lambda e: e.scalar_tensor_tensor(out=l_t[:], in0=ps[:, :], scalar=SCALE, in1=n_t[:],
                                                                      op0=ALU.mult, op1=ALU.add), reads=[psb, n_b], writes=[l_b])
                        p.op("act", lambda e: e.activation(out=P_t[:], in_=l_t[:], func=AF.Exp), reads=[l_b], writes=[P_b])
                    else:
                        p.op("act", lambda e: e.activation(out=P_t[:], in_=ps[:, :], func=AF.Exp, scale=SCALE, bias=cbias_t[:, h:h + 1]),
                             reads=[psb, cbiasb], writes=[P_b])
                    Pm_t, Pm_b = Pm_r.next()
                    p.op("pool", lambda e: e.tensor_tensor(out=Pm_t[:], in0=P_t[:], in1=maskT[:, kb, :], op=ALU.mult),
                         reads=[P_b, maskTb], writes=[Pm_b])
                    qis = [qi for qi in range(4) if kb <= 4 * g + qi]
                    for qi in qis:
                        o_t, o_b = oacc[qi]
                        p.op("pe", lambda e: e.matmul(o_t[:, 0:129], lhsT=Pm_t[:, qi * 128:(qi + 1) * 128], rhs=v_t[:, kbi, :],
                                                      start=(kb == 0), stop=(kb == 4 * g + qi)),
                             reads=[Pm_b, v_b], writes=[o_b], inc=(qi == qis[-1]))
            ob_t, ob_b = obT_r.next()
            for qi in range(4):
                o_t, o_b = oacc[qi]
                rc_t, rc_b = rc_r.next()
                p.op("dve", lambda e: e.reciprocal(out=rc_t[:], in_=o_t[:, 128:129]), reads=[o_b], writes=[rc_b])
                on_t, on_b = on_r.next()
                p.op("dve", lambda e: e.tensor_scalar(out=on_t[:], in0=o_t[:, 0:128], scalar1=rc_t[:, 0:1], scalar2=None, op0=ALU.mult),
                     reads=[o_b, rc_b], writes=[on_b])
                pt, ptb = ptf.next()
                p.op("pe", lambda e: e.transpose(pt[:, 0:128], on_t[:, :], identf[:, :]), reads=[on_b, identfb], writes=[ptb])
                p.op("act", lambda e: e.copy(out=ob_t[:, qi * 128:(qi + 1) * 128], in_=pt[:, 0:128]), reads=[ptb], writes=[ob_b])
            p.dma("sp", obT[h * 128:(h + 1) * 128, g * 512:(g + 1) * 512], ob_t[:, :], reads=[ob_b])


def emit_ffn(p, S, scr, wts, lnp, xres, xout, xTb_next):
    oaT, obT, sgT = scr["oaT"], scr["obT"], scr["sgT"]
    NJ = D_FF // 128
    big = p.sb([128, NJ * 512], BF16)
    slot = [Buf() for _ in range(NJ)]
    aT = lambda j: big[:, j * 512:(j + 1) * 512]
    x1 = [(p.sb([128, 2048], F32), Buf()) for _ in range(4)]
    x1T = [(p.sb([128, 512], BF16), Buf()) for _ in range(16)]
    WS = 8192
    wring = Ring(p, 4, [128, WS], BF16)
    sg_r = Ring(p, 4, [128, 512], F32)
    m_r = Ring(p, 4, [128, 512], F32)
    si_r = Ring(p, 2, [128, 512], F32)
    ln_r = Ring(p, 2, [128, 2048], F32)
    st_r = Ring(p, 2, [128, 4, 6], F32)
    mv_r = Ring(p, 4, [128, 4], F32)
    epsc = p.sb([128, 1], F32); epsb = Buf()
    p.op("pool", lambda e: e.memset(epsc[:], LN_EPS), writes=[epsb])
    identf = p.sb([128, 128], F32); identfb = Buf()
    p.dma("sp", identf[:], scr["ident"][:, :], writes=[identfb])
    pr = Ring(p, 7, [128, 512], F32, psum=True)
    ptr = Ring(p, 1, [128, 512], F32, psum=True)

    def layer_norm(tt, gname, bname):
        x_t, x_b = x1[tt]
        st_t, st_b = st_r.next()
        for c in range(4):
            p.op("dve", lambda e: e.bn_stats(out=st_t[:, c, :], in_=x_t[:, c * 512:(c + 1) * 512]), reads=[x_b], writes=[st_b])
        mv_t, mv_b = mv_r.next()
        p.op("dve", lambda e: e.bn_aggr(out=mv_t[:, 0:2], in_=st_t[:].rearrange("p a b -> p (a b)")), reads=[st_b], writes=[mv_b])
        p.op("act", lambda e: e.activation(out=mv_t[:, 2:3], in_=mv_t[:, 1:2], func=AF.Sqrt, bias=epsc[:, 0:1]),
             reads=[mv_b, epsb], writes=[mv_b])
        p.op("dve", lambda e: e.reciprocal(out=mv_t[:, 2:3], in_=mv_t[:, 2:3]), reads=[mv_b], writes=[mv_b])
        p.op("dve", lambda e: e.scalar_tensor_tensor(out=mv_t[:, 3:4], in0=mv_t[:, 0:1], scalar=-1.0, in1=mv_t[:, 2:3],
                                                      op0=ALU.mult, op1=ALU.mult), reads=[mv_b], writes=[mv_b])
        p.op("act", lambda e: e.activation(out=x_t[:], in_=x_t[:], func=AF.Identity, scale=mv_t[:, 2:3], bias=mv_t[:, 3:4]),
             reads=[x_b, mv_b], writes=[x_b])
        g_t, g_b = ln_r.next()
        p.dma("sp", g_t[:], lnp[gname][:, :], writes=[g_b])
        p.op("pool", lambda e: e.tensor_tensor(out=x_t[:], in0=x_t[:], in1=g_t[:], op=ALU.mult), reads=[x_b, g_b], writes=[x_b])
        b_t, b_b = ln_r.next()
        p.dma("sp", b_t[:], lnp[bname][:, :], writes=[b_b])
        p.op("pool", lambda e: e.tensor_tensor(out=x_t[:], in0=x_t[:], in1=b_t[:], op=ALU.add), reads=[x_b, b_b], writes=[x_b])

    def transposes(tt, dst_tiles):
        x_t, x_b = x1[tt]
        for k4 in range(4):
            pt, ptb = ptr.next()
            for i in range(4):
                k = k4 * 4 + i
                p.op("pe", lambda e: e.transpose(pt[:, i * 128:(i + 1) * 128], x_t[:, k * 128:(k + 1) * 128], identf[:, :]),
                     reads=[x_b, identfb], writes=[ptb], inc=(i == 3))
            for i in range(4):
                k = k4 * 4 + i
                d_t, d_b = dst_tiles[k]
                p.op("act", lambda e: e.copy(out=d_t[:, tt * 128:(tt + 1) * 128], in_=pt[:, i * 128:(i + 1) * 128]),
                     reads=[ptb], writes=[d_b])

    for T in range(S // 512):
        ts = slice(T * 512, (T + 1) * 512)
        oa_v = big[:, 16 * 512:24 * 512].rearrange("p (k t) -> p k t", t=512)
        ob_v = big[:, 24 * 512:32 * 512].rearrange("p (k t) -> p k t", t=512)
        p.dma("sp", oa_v, oaT[:, ts].rearrange("(k p) t -> p k t", p=128), writes=slot[16:24])
        p.dma("sp", ob_v, obT[:, ts].rearrange("(k p) t -> p k t", p=128), writes=slot[24:32])
        for tt in range(4):
            p.dma("sp", x1[tt][0][:], xres[T * 512 + tt * 128:T * 512 + (tt + 1) * 128, :], writes=[x1[tt][1]])
        for fg in range(4):
            w_t, w_b = wring.next()
            wa_v = w_t[:, 0:4096].rearrange("p (k f) -> p k f", f=512)
            wb_v = w_t[:, 4096:8192].rearrange("p (k f) -> p k f", f=512)
            p.dma("sp", wa_v, wts["wa"][:, fg * 512:(fg + 1) * 512].rearrange("(k p) f -> p k f", p=128), writes=[w_b])
            p.dma("sp", wb_v, wts["wb"][:, fg * 512:(fg + 1) * 512].rearrange("(k p) f -> p k f", p=128), writes=[w_b])
            for fi in range(4):
                ft = fg * 4 + fi
                fs = slice(fi * 128, (fi + 1) * 128)
                psA, psAb = pr.next()
                for k in range(8):
                    p.op("pe", lambda e: e.matmul(psA[:, :], lhsT=wa_v[:, k, fs], rhs=oa_v[:, k, :], start=(k == 0), stop=(k == 7)),
                         reads=[w_b] + slot[16:24], writes=[psAb], inc=(k == 7))
                psB, psBb = pr.next()
                for k in range(8):
                    p.op("pe", lambda e: e.matmul(psB[:, :], lhsT=wb_v[:, k, fs], rhs=ob_v[:, k, :], start=(k == 0), stop=(k == 7)),
                         reads=[w_b] + slot[24:32], writes=[psBb], inc=(k == 7))
                ga_t, ga_b = sg_r.next()
                p.dma("sp", ga_t[:], sgT[ft * 128:(ft + 1) * 128, ts], writes=[ga_b])
                gb_t, gb_b = sg_r.next()
                p.dma("sp", gb_t[:], sgT[2048 + ft * 128:2048 + (ft + 1) * 128, ts], writes=[gb_b])
                m1_t, m1_b = m_r.next()
                p.op("dve", lambda e: e.tensor_tensor(out=m1_t[:], in0=psA[:, :], in1=ga_t[:], op=ALU.mult), reads=[psAb, ga_b], writes=[m1_b])
                m2_t, m2_b = m_r.next()
                p.op("dve", lambda e: e.tensor_tensor(out=m2_t[:], in0=psB[:, :], in1=gb_t[:], op=ALU.mult), reads=[psBb, gb_b], writes=[m2_b])
                p.op("pool", lambda e: e.tensor_tensor(out=aT(ft), in0=m1_t[:], in1=m2_t[:], op=ALU.add), reads=[m1_b, m2_b], writes=[slot[ft]])
        for nt in range(4):
            ns = slice(nt * 512, (nt + 1) * 512)
            w_t, w_b = wring.next()
            wo_v = w_t[:, :].rearrange("p (k n) -> p k n", n=512)
            p.dma("sp", wo_v, wts["wout"][:, ns].rearrange("(k p) n -> p k n", p=128), writes=[w_b])
            for tt in range(4):
                ps, psb = pr.next()
                for k in range(16):
                    p.op("pe", lambda e: e.matmul(ps[:, :], lhsT=aT(k)[:, tt * 128:(tt + 1) * 128], rhs=wo_v[:, k, :],
                                                  start=(k == 0), stop=(k == 15)), reads=[w_b, slot[k]], writes=[psb], inc=(k == 15))
                x_t, x_b = x1[tt]
                p.op("dve", lambda e: e.scalar_tensor_tensor(out=x_t[:, ns], in0=x_t[:, ns], scalar=ALPHA, in1=ps[:, :],
                                                              op0=ALU.mult, op1=ALU.add), reads=[x_b, psb], writes=[x_b])
        for tt in range(4):
            layer_norm(tt, "g1", "b1")
            transposes(tt, x1T)
        for cg in range(NJ // 4):
            wg_t, wg_b = wring.next()
            wg_v = wg_t[:, :].rearrange("p (k n) -> p k n", n=512)
            p.dma("sp", wg_v, wts["wfi"][:, cg * 512:(cg + 1) * 512].rearrange("(k p) n -> p k n", p=128), writes=[wg_b])
            wu_t, wu_b = wring.next()
            wu_v = wu_t[:, :].rearrange("p (k n) -> p k n", n=512)
            p.dma("sp", wu_v, wts["wfi"][:, D_FF + cg * 512:D_FF + (cg + 1) * 512].rearrange("(k p) n -> p k n", p=128), writes=[wu_b])
            for ci in range(4):
                jt = cg * 4 + ci
                cs = slice(ci * 128, (ci + 1) * 128)
                psG, psGb = pr.next()
                for k in range(16):
                    p.op("pe", lambda e: e.matmul(psG[:, :], lhsT=wg_v[:, k, cs], rhs=x1T[k][0][:, :], start=(k == 0), stop=(k == 15)),
                         reads=[wg_b, x1T[k][1]], writes=[psGb], inc=(k == 15))
                psU, psUb = pr.next()
                for k in range(16):
                    p.op("pe", lambda e: e.matmul(psU[:, :], lhsT=wu_v[:, k, cs], rhs=x1T[k][0][:, :], start=(k == 0), stop=(k == 15)),
                         reads=[wu_b, x1T[k][1]], writes=[psUb], inc=(k == 15))
                s_t, s_b = si_r.next()
                p.op("act", lambda e: e.activation(out=s_t[:], in_=psG[:, :], func=AF.Silu), reads=[psGb], writes=[s_b])
                p.op("dve", lambda e: e.tensor_tensor(out=aT(jt), in0=psU[:, :], in1=s_t[:], op=ALU.mult), reads=[psUb, s_b], writes=[slot[jt]])
        for nt in range(4):
            ns = slice(nt * 512, (nt + 1) * 512)
            acc = [pr.next() for _ in range(4)]
            for qd in range(4):
                w_t, w_b = wring.next()
                wf_v = w_t[:, 0:11 * 512].rearrange("p (k n) -> p k n", n=512)
                p.dma("sp", wf_v, wts["wfo"][qd * 11 * 128:(qd + 1) * 11 * 128, ns].rearrange("(k p) n -> p k n", p=128), writes=[w_b])
                for tt in range(4):
                    ps, psb = acc[tt]
                    for k in range(11):
                        j = qd * 11 + k
                        p.op("pe", lambda e: e.matmul(ps[:, :], lhsT=aT(j)[:, tt * 128:(tt + 1) * 128], rhs=wf_v[:, k, :],
                                                      start=(j == 0), stop=(j == NJ - 1)), reads=[w_b, slot[j]], writes=[psb],
                             inc=(k == 10))
            for tt in range(4):
                ps, psb = acc[tt]
                x_t, x_b = x1[tt]
                p.op("dve", lambda e: e.scalar_tensor_tensor(out=x_t[:, ns], in0=x_t[:, ns], scalar=ALPHA, in1=ps[:, :],
                                                              op0=ALU.mult, op1=ALU.add), reads=[x_b, psb], writes=[x_b])
        for tt in range(4):
            layer_norm(tt, "g2", "b2")
            p.dma("sp", xout[T * 512 + tt * 128:T * 512 + (tt + 1) * 128, :], x1[tt][0][:], reads=[x1[tt][1]])
            if xTb_next is not None:
                transposes(tt, x1T)
        if xTb_next is not None:
            for k in range(16):
                p.dma("sp", xTb_next[k * 128:(k + 1) * 128, ts], x1T[k][0][:, :], reads=[x1T[k][1]])


W_SHAPES = {"w_in": (D_MODEL, D_IN), "w_branch_a": (1024, D_MODEL), "w_branch_b": (1024, D_MODEL),
            "w_out": (D_MODEL, D_MODEL), "w_ffn_in": (D_MODEL, 2 * D_FF), "w_ffn_out": (D_FF, D_MODEL)}


def const_arrays(inputs, S, depth):
    NCH = S // CH
    c = dict(gdn_consts())
    c.update(dsa_consts(np.asarray(inputs["rel_bias"], np.float32)))
    c["gnw"] = np.ascontiguousarray(np.broadcast_to(np.asarray(inputs["gdn_norm_w"])[:depth, None, :], (depth, CH, 128))).astype(np.float32)
    cw = np.asarray(inputs["conv_w"])[:depth]
    c["cw"] = np.ascontiguousarray(cw.reshape(depth, 4, 3, 8, 128).transpose(0, 4, 2, 3, 1).reshape(depth, 128, 96)).astype(np.float32)
    dtb = np.zeros((depth, CH, NCH, 16), np.float32)
    negA = np.zeros((depth, CH, NCH, 16), np.float32)
    dtb[..., 0:8] = np.asarray(inputs["dt_bias"])[:depth, None, None, :]
    a_log = np.asarray(inputs["a_log"])[:depth]
    negA[..., 0:8] = a_log[:, None, None, :]
    c["dtb16"] = dtb.reshape(depth, CH, NCH * 16)
    c["alog16"] = negA.reshape(depth, CH, NCH * 16)
    for nm in ("ln1_g", "ln1_b", "ln2_g", "ln2_b"):
        c[nm] = np.ascontiguousarray(np.broadcast_to(np.asarray(inputs[nm])[:depth, None, :], (depth, 128, D_MODEL))).astype(np.float32)
    return c


def build_all(S, depth, cshapes):
    p = Prog()
    x = p.dram("x", [S, D_MODEL], F32, "ExternalInput")
    xT = p.dram("xT", [D_MODEL, S], F32, "ExternalInput")
    wext = {nm: p.dram(nm, [depth] + list(sh), F32, "ExternalInput") for nm, sh in W_SHAPES.items()}
    cst = {k: p.dram("c_" + k, list(sh), F32, "ExternalInput") for k, sh in cshapes.items()}
    out = p.dram("out", [S, D_MODEL], F32, "ExternalOutput")
    scr = {k: p.scratch("s_" + k, fn(S), dt) for k, (fn, dt) in SCR_SPEC.items()}
    scr["ident"] = cst["ident"]
    wb16 = {nm: [p.scratch("%s_b%d" % (nm, l), list(sh), BF16) for l in range(depth)] for nm, sh in W_SHAPES.items()}
    with p.phase():
        rings = (Ring(p, 3, [128, CAST_CH], F32), Ring(p, 3, [128, CAST_CH], BF16))
        k = emit_cast(p, xT, scr["xTb"], D_MODEL, S, rings)
        for l in range(depth):
            for nm, sh in W_SHAPES.items():
                k = emit_cast(p, wext[nm][l], wb16[nm][l], sh[0], sh[1], rings, k)
    negA = p.scratch("s_negA", list(cshapes["alog16"]), F32)
    with p.phase():
        for l in range(depth):
            W = cshapes["alog16"][2]
            t = p.sb([CH, W], F32); tb = Buf()
            p.dma("sp", t[:], cst["alog16"][l, :, :], writes=[tb])
            p.op("act", lambda e: e.activation(out=t[:], in_=t[:], func=AF.Exp), reads=[tb], writes=[tb])
            p.op("dve", lambda e: e.tensor_scalar(out=t[:], in0=t[:], scalar1=-1.0, scalar2=None, op0=ALU.mult), reads=[tb], writes=[tb])
            p.dma("sp", negA[l, :, :], t[:], reads=[tb])
    cst["negA16"] = negA
    xbuf = [scr["xA"], scr["xB"]]
    for l in range(depth):
        last = (l == depth - 1)
        with p.phase():
            emit_proj(p, S, scr["xTb"], wb16["w_in"][l], scr)
        for heads in ([0, 1, 2, 3], [4, 5, 6, 7]):
            with p.phase():
                emit_gdn(p, S, scr, cst, l, heads)
        with p.phase():
            emit_dsa(p, S, scr, cst)
        with p.phase():
            wts = {"wa": wb16["w_branch_a"][l], "wb": wb16["w_branch_b"][l], "wout": wb16["w_out"][l],
                   "wfi": wb16["w_ffn_in"][l], "wfo": wb16["w_ffn_out"][l]}
            lnp = {"g1": cst["ln1_g"][l], "b1": cst["ln1_b"][l], "g2": cst["ln2_g"][l], "b2": cst["ln2_b"][l]}
            emit_ffn(p, S, scr, wts, lnp, x if l == 0 else xbuf[(l - 1) % 2], out if last else xbuf[l % 2],
                     None if last else scr["xTb"])
    p.close()
    return p


def make_in_map(inputs, b, S, depth, cst):
    xb = np.ascontiguousarray(np.asarray(inputs["x"])[b, :S])
    m = {"x": xb, "xT": np.ascontiguousarray(xb.T)}
    for nm in W_SHAPES:
        m[nm] = np.ascontiguousarray(np.asarray(inputs[nm])[:depth])
    for k, v in cst.items():
        m["c_" + k] = v
    return m


def kernel(**inputs):
    S, depth = SEQ, DEPTH
    cst = const_arrays(inputs, S, depth)
    p = build_all(S, depth, {k: v.shape for k, v in cst.items()})
    in_maps = [make_in_map(inputs, b, S, depth, cst) for b in range(BATCH)]
    res = run(p, in_maps).results
    return np.stack([np.asarray(res[b]["out"], np.float32) for b in range(BATCH)], 0)
```

```python
import contextlib
import math
import numpy as np
import ml_dtypes
import concourse.bass as bass
import concourse.mybir as mybir
from concourse.bass_utils import run_bass_kernel_spmd

F32 = mybir.dt.float32
BF16 = mybir.dt.bfloat16
AF = mybir.ActivationFunctionType
ALU = mybir.AluOpType
AX = mybir.AxisListType
NPBF = ml_dtypes.bfloat16

D_MODEL = 2048
BATCH = 4
SEQ = 8192
DEPTH = 4
NCORE = 4
D_FF = 5632
D_IN = 11864
ALPHA = (2 * DEPTH) ** 0.25
LN_EPS = 1e-5
RMS_EPS = 1e-6


class Buf:
    __slots__ = ("w", "r", "name")

    def __init__(self, name=""):
        self.w = None
        self.r = {}
        self.name = name


class Prog:
    ND = 24

    def __init__(self):
        self.nc = bass.Bass("TRN2", target_bir_lowering=False)
        nc = self.nc
        self.es = contextlib.ExitStack()
        self.eng = {"pe": nc.tensor, "act": nc.scalar, "dve": nc.vector, "pool": nc.gpsimd, "sp": nc.sync}
        self.sem = {e: self.es.enter_context(nc.semaphore("s_" + e)) for e in self.eng}
        self.dsem = [self.es.enter_context(nc.semaphore("d%d" % i)) for i in range(self.ND)]
        self.cnt = {e: 0 for e in self.eng}
        self.pending = {e: False for e in self.eng}
        self.known = {e: {} for e in self.eng}
        self.ndma = 0
        self.nins = 0
        self._names = 0
        self.pes = None

    def dram(self, name, shape, dt, kind):
        return self.nc.dram_tensor(name, list(shape), dt, kind=kind).ap()

    def sb(self, shape, dt, name=None):
        self._names += 1
        st = self.pes if self.pes is not None else self.es
        return st.enter_context(self.nc.sbuf_tensor(name or ("sb%d" % self._names), list(shape), dt))

    def ps(self, shape, dt=F32, name=None):
        self._names += 1
        st = self.pes if self.pes is not None else self.es
        return st.enter_context(self.nc.psum_tensor(name or ("ps%d" % self._names), list(shape), dt))

    def scratch(self, name, shape, dt):
        return self.nc.dram_tensor(name, list(shape), dt, kind="Internal").ap()

    def barrier(self):
        for e in self.pending:
            assert not self.pending[e], "engine %s has un-inc'd instruction at barrier" % e
        deps = [("e", f, self.cnt[f]) for f in self.eng if self.cnt[f] > 0]
        for i in range(min(self.ndma, self.ND)):
            last = self.ndma - 1 - ((self.ndma - 1 - i) % self.ND)
            deps.append(("d", i, 16 * (last // self.ND + 1)))
        for e in self.eng:
            self._wait(e, [d for d in deps if not (d[0] == "e" and d[1] == e)])

    @contextlib.contextmanager
    def phase(self):
        assert self.pes is None
        self.pes = contextlib.ExitStack()
        try:
            yield
            self.barrier()
        finally:
            self.pes.close()
            self.pes = None

    def _deps(self, reads, writes):
        deps = []
        for b in reads:
            if b.w is not None:
                deps.append(b.w)
        for b in writes:
            if b.w is not None:
                deps.append(b.w)
            deps.extend(b.r.values())
        return deps

    def _wait(self, e, deps):
        kn = self.known[e]
        need = {}
        for (kind, key, val) in deps:
            if kind == "e" and key == e and e == "pe":
                continue
            k = (kind, key)
            if kn.get(k, 0) >= val:
                continue
            if need.get(k, 0) < val:
                need[k] = val
        for (kind, key), val in need.items():
            if kind == "e" and key == e:
                assert val <= self.cnt[e], "self-wait on pending (un-inc'd) instruction"
            s = self.sem[key] if kind == "e" else self.dsem[key]
            self.eng[e].wait_ge(s, val)
            kn[(kind, key)] = val
            self.nins += 1

    def _mark(self, tok, reads, writes):
        k = (tok[0], tok[1])
        for b in reads:
            b.r[k] = tok
        for b in writes:
            b.w = tok
            b.r = {}

    def op(self, e, fn, reads=(), writes=(), inc=True):
        self._wait(e, self._deps(reads, writes))
        ins = fn(self.eng[e])
        if inc:
            self.cnt[e] += 1
            ins.then_inc(self.sem[e], 1)
            tok = ("e", e, self.cnt[e])
            self.pending[e] = False
        else:
            tok = ("e", e, self.cnt[e] + 1)
            self.pending[e] = True
        self.nins += 1
        self._mark(tok, reads, writes)
        return ins

    def dma(self, q, out, in_, reads=(), writes=(), sink=None, **kw):
        i = self.ndma
        self.ndma += 1
        s = i % self.ND
        val = 16 * (i // self.ND + 1)
        deps = self._deps(reads, writes)
        if val > 16:
            deps.append(("d", s, val - 16))
        self._wait(q, deps)
        ins = self.eng[q].dma_start(out=out, in_=in_, **kw)
        ins.then_inc(self.dsem[s], 16)
        self.nins += 1
        self._mark(("d", s, val), reads, writes)
        if sink is not None:
            sink.append(("d", s, val))
        return ins

    def finish(self, sinks):
        deps = []
        for b in sinks:
            deps.extend(b)
        self._wait("sp", deps)
        for e in self.pending:
            assert not self.pending[e], "engine %s ends with un-inc'd instruction" % e

    def close(self):
        self.es.close()


class Ring:
    def __init__(self, p, n, shape, dt, psum=False):
        self.items = []
        for _ in range(n):
            t = p.ps(shape, dt) if psum else p.sb(shape, dt)
            self.items.append((t, Buf()))
        self.i = 0

    def next(self):
        it = self.items[self.i % len(self.items)]
        self.i += 1
        return it


def run(prog, in_maps):
    return run_bass_kernel_spmd(prog.nc, in_maps, core_ids=list(range(len(in_maps))))


CAST_CH = 4096


def emit_cast(p, src2d, dst2d, rows, cols, rings, k0=0):
    st, ob = rings
    sv = src2d.rearrange("(p a) c -> p (a c)", p=128)
    dv = dst2d.rearrange("(p a) c -> p (a c)", p=128)
    m = rows // 128 * cols
    k = k0
    for c0 in range(0, m, CAST_CH):
        n = min(CAST_CH, m - c0)
        s_t, s_b = st.next()
        o_t, o_b = ob.next()
        p.dma("sp", s_t[:, :n], sv[:, c0:c0 + n], writes=[s_b])
        e = ("dve", "act", "pool")[k % 3]
        k += 1
        if e == "act":
            p.op(e, lambda en: en.copy(out=o_t[:, :n], in_=s_t[:, :n]), reads=[s_b], writes=[o_b])
        else:
            p.op(e, lambda en: en.tensor_copy(out=o_t[:, :n], in_=s_t[:, :n]), reads=[s_b], writes=[o_b])
        p.dma("sp", dv[:, c0:c0 + n], o_t[:, :n], reads=[o_b])
    return k


C_QKV, C_A, C_Z, C_QB, C_KB, C_VB, C_QI, C_KI, C_WI, C_GA = 0, 3072, 3088, 4112, 5136, 6160, 7184, 7696, 7760, 7768


def proj_groups():
    g = []
    for i in range(6):
        g.append(("F", C_QKV + 512 * i, 512, "qkvT", 512 * i))
    g.append(("T", C_A, 16, "ab", 0))
    for i in range(2):
        g.append(("T", C_Z + 512 * i, 512, "z", 512 * i))
    for i in range(2):
        g.append(("F", C_QB + 512 * i, 512, "qbT", 512 * i))
    for i in range(2):
        g.append(("F", C_KB + 512 * i, 512, "kbT", 512 * i))
    for i in range(2):
        g.append(("T", C_VB + 512 * i, 512, "vb", 512 * i))
    g.append(("F", C_QI, 512, "qiT", 0))
    g.append(("F", C_KI, 64, "kiT", 0))
    g.append(("T", C_WI, 8, "wi", 0))
    for i in range(8):
        g.append(("F", C_GA + 512 * i, 512, "sgT", 512 * i))
    return g


SCR_SPEC = {
    "qkvT": (lambda S: [3072, S], F32), "qbT": (lambda S: [1024, S], BF16), "kbT": (lambda S: [1024, S], BF16),
    "qiT": (lambda S: [512, S], BF16), "kiT": (lambda S: [64, S], BF16), "sgT": (lambda S: [4096, S], F32),
    "ab": (lambda S: [S, 16], F32), "z": (lambda S: [S, 1024], F32), "vb": (lambda S: [S, 1024], BF16),
    "wi": (lambda S: [S, 8], F32), "oaT": (lambda S: [1024, S], BF16), "obT": (lambda S: [1024, S], BF16),
    "x1": (lambda S: [S, 2048], F32), "xA": (lambda S: [S, 2048], F32), "xB": (lambda S: [S, 2048], F32),
    "xTb": (lambda S: [2048, S], BF16),
}


def emit_proj(p, S, xTb, w, scr):
    TG = min(2048, S)
    KC = D_MODEL // 128
    wv = w.rearrange("(kc p) n -> p kc n", p=128)
    xb = [(p.sb([128, TG], BF16), Buf()) for _ in range(KC)]
    wr = Ring(p, 2, [128, KC, 512], BF16)
    pr = Ring(p, 6, [128, 512], F32, psum=True)
    sf = Ring(p, 4, [128, 512], F32)
    sh = Ring(p, 4, [128, 512], BF16)
    ev = 0
    for tg in range(S // TG):
        t0 = tg * TG
        for kc in range(KC):
            xt, xbuf = xb[kc]
            p.dma("sp", xt[:], xTb[kc * 128:(kc + 1) * 128, t0:t0 + TG], writes=[xbuf])
        for (mode, c0, n, oname, r0) in proj_groups():
            w_t, w_b = wr.next()
            p.dma("sp", w_t[:, :, :n], wv[:, :, c0:c0 + n], writes=[w_b])
            o_ap = scr[oname]
            o_dt = SCR_SPEC[oname][1]
            stg = sf if o_dt == F32 else sh
            if mode == "F":
                for ci in range(0, n, 128):
                    cn = min(128, n - ci)
                    for tt in range(TG // 512):
                        ps_t, ps_b = pr.next()
                        for kc in range(KC):
                            xt, xbuf = xb[kc]
                            p.op("pe", lambda en: en.matmul(ps_t[:cn, :], lhsT=w_t[:, kc, ci:ci + cn],
                                                            rhs=xt[:, tt * 512:(tt + 1) * 512],
                                                            start=(kc == 0), stop=(kc == KC - 1)),
                                 reads=[w_b, xbuf], writes=[ps_b], inc=(kc == KC - 1))
                        g_t, g_b = stg.next()
                        if oname == "sgT":
                            p.op("act", lambda en: en.activation(out=g_t[:cn, :], in_=ps_t[:cn, :], func=AF.Sigmoid),
                                 reads=[ps_b], writes=[g_b])
                        else:
                            ev += 1
                            if ev % 2:
                                p.op("dve", lambda en: en.tensor_copy(out=g_t[:cn, :], in_=ps_t[:cn, :]),
                                     reads=[ps_b], writes=[g_b])
                            else:
                                p.op("act", lambda en: en.copy(out=g_t[:cn, :], in_=ps_t[:cn, :]),
                                     reads=[ps_b], writes=[g_b])
                        p.dma("sp", o_ap[r0 + ci:r0 + ci + cn, t0 + tt * 512:t0 + (tt + 1) * 512], g_t[:cn, :],
                              reads=[g_b])
            else:
                for tt in range(TG // 128):
                    ps_t, ps_b = pr.next()
                    for kc in range(KC):
                        xt, xbuf = xb[kc]
                        p.op("pe", lambda en: en.matmul(ps_t[:, :n], lhsT=xt[:, tt * 128:(tt + 1) * 128],
                                                        rhs=w_t[:, kc, :n],
                                                        start=(kc == 0), stop=(kc == KC - 1)),
                             reads=[w_b, xbuf], writes=[ps_b], inc=(kc == KC - 1))
                    g_t, g_b = stg.next()
                    ev += 1
                    if ev % 2:
                        p.op("dve", lambda en: en.tensor_copy(out=g_t[:, :n], in_=ps_t[:, :n]),
                             reads=[ps_b], writes=[g_b])
                    else:
                        p.op("act", lambda en: en.copy(out=g_t[:, :n], in_=ps_t[:, :n]),
                             reads=[ps_b], writes=[g_b])
                    p.dma("sp", o_ap[t0 + tt * 128:t0 + (tt + 1) * 128, r0:r0 + n], g_t[:, :n],
                          reads=[g_b])


CH = 64


def gdn_consts():
    i = np.arange(CH)
    c = {}
    c["ident"] = np.eye(128, dtype=np.float32)
    c["ones"] = np.ones((128, 128), np.float32)
    c["ucum"] = (i[:, None] <= i[None, :]).astype(np.float32)
    c["stril"] = (i[:, None] > i[None, :]).astype(np.float32)
    c["triu"] = (i[:, None] <= i[None, :]).astype(np.float32)
    return c


def run_gens(always, stages=()):
    active = list(always)
    stages = [list(st) for st in stages]
    cur = stages.pop(0) if stages else []
    active += cur
    while active:
        for g in list(active):
            try:
                next(g)
            except StopIteration:
                active.remove(g)
                if g in cur:
                    cur.remove(g)
        if not cur and stages:
            cur = stages.pop(0)
            active += cur


def emit_gdn_pre(p, S, scr, cst, l, gbt):
    NCH = S // CH
    W = NCH * 16
    ab_t = p.sb([CH, W], F32); abb = Buf()
    p.dma("sp", ab_t[:].rearrange("c (n k) -> c n k", k=16), scr["ab"].rearrange("(n c) k -> c n k", c=CH), writes=[abb])
    dtb_t = p.sb([CH, W], F32); dtbb = Buf()
    p.dma("sp", dtb_t[:], cst["dtb16"][l, :, :], writes=[dtbb])
    negA_t = p.sb([CH, W], F32); negAb = Buf()
    p.dma("sp", negA_t[:], cst["negA16"][l, :, :], writes=[negAb])
    g_t = p.sb([CH, W], F32); gb = Buf()
    beta_t = p.sb([CH, W], F32); betab = Buf()
    nbeta_t = p.sb([CH, W], F32); nbetab = Buf()
    tmpw = p.sb([CH, W], F32); tmpwb = Buf()
    p.op("dve", lambda e: e.tensor_tensor(out=tmpw[:], in0=ab_t[:], in1=dtb_t[:], op=ALU.add), reads=[abb, dtbb], writes=[tmpwb])
    p.op("act", lambda e: e.activation(out=tmpw[:], in_=tmpw[:], func=AF.Exp), reads=[tmpwb], writes=[tmpwb])
    p.op("act", lambda e: e.activation(out=tmpw[:], in_=tmpw[:], func=AF.Ln, bias=1.0), reads=[tmpwb], writes=[tmpwb])
    p.op("dve", lambda e: e.tensor_tensor(out=g_t[:], in0=tmpw[:], in1=negA_t[:], op=ALU.mult), reads=[tmpwb, negAb], writes=[gb])
    p.op("act", lambda e: e.activation(out=beta_t[:], in_=ab_t[:], func=AF.Sigmoid), reads=[abb], writes=[betab])
    p.op("dve", lambda e: e.tensor_scalar(out=nbeta_t[:], in0=beta_t[:], scalar1=-1.0, scalar2=None, op0=ALU.mult), reads=[betab], writes=[nbetab])
    p.dma("sp", gbt[0, :, :], g_t[:], reads=[gb])
    p.dma("sp", gbt[1, :, :], beta_t[:], reads=[betab])
    p.dma("sp", gbt[2, :, :], nbeta_t[:], reads=[nbetab])


def emit_gdn(p, S, scr, cst, l, heads, gbt):
    NH = len(heads)
    NCH = S // CH
    TL = 256
    NT = S // TL
    CPT = TL // CH
    qkvT, zin, oaT = scr["qkvT"], scr["z"], scr["oaT"]

    def cload(ap, shape, dt=F32):
        t = p.sb(shape, dt)
        b = Buf()
        p.dma("sp", t[:], ap, writes=[b])
        return t, b

    ident, identb = cload(cst["ident"][:, :], [128, 128])
    ones, onesb = cload(cst["ones"][:, :], [128, 128])
    ucum, ucumb = cload(cst["ucum"][:, :], [CH, CH])
    stril, strilb = cload(cst["stril"][:, :], [CH, CH])
    triu, triub = cload(cst["triu"][:, :], [CH, CH])
    gnw_t, gnwb = cload(cst["gnw"][l, :, :], [CH, 128])
    cw_t, cwb = cload(cst["cw"][l, :, :], [128, 96])
    W = NCH * 16
    g_t, gb = cload(gbt[0, :, :], [CH, W])
    beta_t, betab = cload(gbt[1, :, :], [CH, W])
    nbeta_t, nbetab = cload(gbt[2, :, :], [CH, W])
    epsc = p.sb([128, 1], F32); epsb = Buf()
    p.op("pool", lambda e: e.memset(epsc[:], RMS_EPS), writes=[epsb])

    banks = [p.ps([128, 512], F32) for _ in range(8)]

    class QRing:
        def __init__(self, bank_ids):
            self.items = [(banks[b], Buf()) for b in bank_ids]
            self.i = 0

        def next(self):
            it = self.items[self.i % len(self.items)]
            self.i += 1
            return it
    qa = QRing([0, 1, 2])
    qd = QRing([3, 4, 5, 6])
    nrmb = Buf()
    nrm = [(banks[7][:, 0:256], nrmb), (banks[7][:, 0:256], nrmb)]
    nrm_i = [0]

    def mk(shape, dt):
        return [[(p.sb(shape, dt), Buf()) for _ in range(NH)] for _ in range(2)]
    qTf = mk([128, TL], F32); kTf = mk([128, TL], F32); vTf = mk([128, TL], F32)
    qTb = mk([128, TL], BF16); kTb = mk([128, TL], BF16)
    oT = mk([128, TL], BF16)

    def mkc(shape, dt):
        return [[[(p.sb(shape, dt), Buf()) for _ in range(CPT)] for _ in range(NH)] for _ in range(2)]
    TTb = mkc([CH, CH], BF16); intraTb = mkc([CH, CH], BF16); qdecTb = mkc([128, CH], BF16)
    kdecb = mkc([CH, 128], BF16); vbeta = mkc([CH, 128], F32); scol = mkc([CH, 2], F32); gtc = mkc([128, 1], F32)
    zc = mkc([CH, 128], F32)

    def mkp(shape, dt):
        return [[(p.sb(shape, dt), Buf()) for _ in range(CPT)] for _ in range(NH)]
    t_gbw = mkp([CH, 128], F32); t_gc = mkp([CH, 4], F32); t_eg = mkp([128, CH], F32); t_zr = mkp([CH, 128], F32)
    t_s64 = [mkp([CH, CH], F32) for _ in range(9)]
    t_x = [(p.sb([128, TL + 3], F32), Buf()) for _ in range(NH)]
    t_c = [(p.sb([128, TL], F32), Buf()) for _ in range(NH)]
    t_rn = [(p.sb([CH, 128], BF16), Buf()) for _ in range(NH)]
    t_vn = [(p.sb([CH, 128], BF16), Buf()) for _ in range(NH)]
    t_oraw = [(p.sb([CH, 128], F32), Buf()) for _ in range(NH)]
    t_o = [(p.sb([CH, 128], F32), Buf()) for _ in range(NH)]
    t_o2 = [(p.sb([CH, 128], F32), Buf()) for _ in range(NH)]
    t_ss = [(p.sb([CH, 4], F32), Buf()) for _ in range(NH)]
    St = [(p.sb([128, 128], F32), Buf()) for _ in range(NH)]
    Sb = [(p.sb([128, 128], BF16), Buf()) for _ in range(NH)]
    for h in range(NH):
        p.op("pool", lambda e: e.memset(St[h][0][:], 0.0), writes=[St[h][1]])
        p.op("pool", lambda e: e.memset(Sb[h][0][:], 0.0), writes=[Sb[h][1]])

    def conv_gen(ti, h):
        par, t0, hg = ti % 2, ti * TL, heads[h]
        x_t, x_b = t_x[h]
        c_t, c_b = t_c[h]
        for a in range(3):
            r0 = a * 1024 + hg * 128
            if t0 == 0:
                p.op("pool", lambda e: e.memset(x_t[:, 0:3], 0.0), writes=[x_b])
                p.dma("sp", x_t[:, 3:TL + 3], qkvT[r0:r0 + 128, 0:TL], writes=[x_b])
            else:
                p.dma("sp", x_t[:, :], qkvT[r0:r0 + 128, t0 - 3:t0 + TL], writes=[x_b])
            wcol = lambda j: cw_t[:, (a * 8 + hg) * 4 + j:(a * 8 + hg) * 4 + j + 1]
            p.op("dve", lambda e: e.tensor_scalar(out=c_t[:], in0=x_t[:, 0:TL], scalar1=wcol(0), scalar2=None, op0=ALU.mult),
                 reads=[x_b, cwb], writes=[c_b])
            yield
            for j in range(1, 4):
                p.op("dve", lambda e: e.scalar_tensor_tensor(out=c_t[:], in0=x_t[:, j:j + TL], scalar=wcol(j), in1=c_t[:],
                                                              op0=ALU.mult, op1=ALU.add), reads=[x_b, cwb, c_b], writes=[c_b])
                yield
            dstf = (qTf, kTf, vTf)[a][par][h]
            p.op("act", lambda e: e.activation(out=dstf[0][:], in_=c_t[:], func=AF.Silu), reads=[c_b], writes=[dstf[1]])
            yield
            if a < 2:
                p.op("pool", lambda e: e.tensor_tensor(out=c_t[:], in0=dstf[0][:], in1=dstf[0][:], op=ALU.mult),
                     reads=[dstf[1]], writes=[c_b])
                yield
                ps_t, ps_b = nrm[nrm_i[0] % 2]
                nrm_i[0] += 1
                p.op("pe", lambda e: e.matmul(ps_t, lhsT=ones[:, :], rhs=c_t[:], start=True, stop=True),
                     reads=[onesb, c_b], writes=[ps_b])
                p.op("act", lambda e: e.activation(out=c_t[:], in_=ps_t, func=AF.Sqrt, bias=epsc[:, 0:1]),
                     reads=[ps_b, epsb], writes=[c_b])
                yield
                p.op("dve", lambda e: e.reciprocal(out=c_t[:], in_=c_t[:]), reads=[c_b], writes=[c_b])
                sc = (128 ** -0.5) if a == 0 else 1.0
                p.op("dve", lambda e: e.scalar_tensor_tensor(out=dstf[0][:], in0=dstf[0][:], scalar=sc, in1=c_t[:],
                                                              op0=ALU.mult, op1=ALU.mult), reads=[dstf[1], c_b], writes=[dstf[1]])
                yield
                dstb = (qTb, kTb)[a][par][h]
                p.op("pool", lambda e: e.tensor_copy(out=dstb[0][:], in_=dstf[0][:]), reads=[dstf[1]], writes=[dstb[1]])
                yield

    def prep_gen(ti, h, ci):
        par, hg = ti % 2, heads[h]
        n = ti * CPT + ci
        colg, colb = n * 16 + hg, n * 16 + 8 + hg
        cs = slice(ci * CH, (ci + 1) * CH)
        gcol = g_t[:, colg:colg + 1]
        tmp = [t_s64[k][h][ci] for k in range(9)]
        (gcr_t, gcr_b), (e1_t, e1_b), (e2_t, e2_b) = tmp[0], tmp[1], tmp[2]
        Pa, Pb_, PTa, PTb_, TTa, TTb_ = tmp[3], tmp[4], tmp[5], tmp[6], tmp[7], tmp[8]
        zr_t, zr_b = t_zr[h][ci]
        gbw_t, gbw_b = t_gbw[h][ci]
        gc_t, gc_b = t_gc[h][ci]
        eg_t, eg_b = t_eg[h][ci]
        sc_t, sc_b = scol[par][h][ci]
        gt_t, gt_b = gtc[par][h][ci]
        p.dma("sp", zr_t[:], zin[n * CH:(n + 1) * CH, hg * 128:(hg + 1) * 128], writes=[zr_b])
        p.op("act", lambda e: e.activation(out=zc[par][h][ci][0][:], in_=zr_t[:], func=AF.Silu),
             reads=[zr_b], writes=[zc[par][h][ci][1]])
        p.op("dve", lambda e: e.tensor_scalar(out=gbw_t[:, :], in0=ones[:CH, :128], scalar1=gcol, scalar2=None, op0=ALU.mult),
             reads=[onesb, gb], writes=[gbw_b])
        psA, psAb = qa.next()
        p.op("pe", lambda e: e.matmul(psA[:, 0:CH], lhsT=gbw_t[:, :], rhs=ucum[:, :], start=True, stop=True),
             reads=[gbw_b, ucumb], writes=[psAb], inc=False)
        p.op("pe", lambda e: e.matmul(psA[:CH, CH:2 * CH], lhsT=ucum[:, :], rhs=gbw_t[:, :CH], start=True, stop=True),
             reads=[ucumb, gbw_b], writes=[psAb])
        p.op("act", lambda e: e.copy(out=gc_t[:, 0:1], in_=psA[:CH, CH:CH + 1]), reads=[psAb], writes=[gc_b])
        p.op("act", lambda e: e.copy(out=gc_t[:, 1:2], in_=psA[:CH, CH - 1:CH]), reads=[psAb], writes=[gc_b])
        p.op("act", lambda e: e.activation(out=gt_t[:, :], in_=psA[:, CH - 1:CH], func=AF.Exp), reads=[psAb], writes=[gt_b])
        p.op("act", lambda e: e.copy(out=gcr_t[:], in_=psA[:CH, 0:CH]), reads=[psAb], writes=[gcr_b])
        p.op("act", lambda e: e.activation(out=eg_t[:, :], in_=psA[:, 0:CH], func=AF.Exp), reads=[psAb], writes=[eg_b])
        yield
        p.op("act", lambda e: e.activation(out=gc_t[:, 2:3], in_=gc_t[:, 0:1], func=AF.Exp), reads=[gc_b], writes=[gc_b])
        p.op("act", lambda e: e.activation(out=sc_t[:, 1:2], in_=gc_t[:, 0:1], func=AF.Exp, scale=-1.0, bias=gc_t[:, 1:2]),
             reads=[gc_b, sc_b], writes=[sc_b])
        p.op("dve", lambda e: e.tensor_scalar(out=e1_t[:], in0=gcr_t[:], scalar1=gc_t[:, 0:1], scalar2=0.0,
                                              op0=ALU.subtract, op1=ALU.max), reads=[gcr_b, gc_b], writes=[e1_b])
        p.op("dve", lambda e: e.tensor_scalar(out=e2_t[:], in0=gcr_t[:], scalar1=gc_t[:, 0:1], scalar2=0.0,
                                              op0=ALU.subtract, op1=ALU.min), reads=[gcr_b, gc_b], writes=[e2_b])
        p.op("pool", lambda e: e.tensor_tensor(out=qdecTb[par][h][ci][0][:], in0=qTf[par][h][0][:, cs], in1=eg_t[:, :], op=ALU.mult),
             reads=[qTf[par][h][1], eg_b], writes=[qdecTb[par][h][ci][1]])
        yield
        p.op("dve", lambda e: e.tensor_tensor(out=sc_t[:, 0:1], in0=gc_t[:, 2:3], in1=nbeta_t[:, colb:colb + 1], op=ALU.mult),
             reads=[gc_b, nbetab, sc_b], writes=[sc_b])
        p.op("act", lambda e: e.activation(out=e1_t[:], in_=e1_t[:], func=AF.Exp, scale=-1.0), reads=[e1_b], writes=[e1_b])
        p.op("act", lambda e: e.activation(out=e2_t[:], in_=e2_t[:], func=AF.Exp), reads=[e2_b], writes=[e2_b])
        yield
        p.op("pool", lambda e: e.tensor_tensor(out=e2_t[:], in0=e2_t[:], in1=triu[:, :], op=ALU.mult), reads=[e2_b, triub], writes=[e2_b])
        kb_t, kb_b = kTb[par][h]
        qb_t, qb_b = qTb[par][h]
        psBk, psKb = qd.next()
        psK, psT, psV = psBk[:, 0:128], psBk[:, 128:256], psBk[:, 256:384]
        psTb = psVb = psKb
        p.op("pe", lambda e: e.matmul(psK[:CH, 0:CH], lhsT=kb_t[:, cs], rhs=kb_t[:, cs], start=True, stop=True),
             reads=[kb_b], writes=[psKb], inc=False)
        p.op("pe", lambda e: e.matmul(psK[:CH, CH:2 * CH], lhsT=kb_t[:, cs], rhs=qb_t[:, cs], start=True, stop=True),
             reads=[kb_b, qb_b], writes=[psKb], inc=False)
        p.op("pe", lambda e: e.transpose(psT[:CH, :], kTf[par][h][0][:, cs], ident[:, :]),
             reads=[kTf[par][h][1], identb], writes=[psTb], inc=False)
        p.op("pe", lambda e: e.transpose(psV[:CH, :], vTf[par][h][0][:, cs], ident[:, :]),
             reads=[vTf[par][h][1], identb], writes=[psVb])
        it_t, it_b = intraTb[par][h][ci]
        p.op("dve", lambda e: e.tensor_tensor(out=it_t[:], in0=psK[:CH, CH:2 * CH], in1=e2_t[:], op=ALU.mult),
             reads=[psKb, e2_b], writes=[it_b])
        kd_t, kd_b = kdecb[par][h][ci]
        p.op("dve", lambda e: e.tensor_scalar(out=kd_t[:], in0=psT[:CH, :], scalar1=sc_t[:, 1:2], scalar2=None, op0=ALU.mult),
             reads=[psTb, sc_b], writes=[kd_b])
        vb_t, vb_b = vbeta[par][h][ci]
        p.op("dve", lambda e: e.tensor_scalar(out=vb_t[:], in0=psV[:CH, :], scalar1=beta_t[:, colb:colb + 1], scalar2=None, op0=ALU.mult),
             reads=[psVb, betab], writes=[vb_b])
        n_t, n_b = Pa
        p.op("dve", lambda e: e.tensor_tensor(out=n_t[:], in0=psK[:CH, 0:CH], in1=e1_t[:], op=ALU.mult), reads=[psKb, e1_b], writes=[n_b])
        yield
        p.op("dve", lambda e: e.scalar_tensor_tensor(out=n_t[:], in0=n_t[:], scalar=nbeta_t[:, colb:colb + 1], in1=stril[:, :],
                                                      op0=ALU.mult, op1=ALU.mult), reads=[n_b, nbetab, strilb], writes=[n_b])
        yield
        psC, psCb = qd.next()
        p.op("pe", lambda e: e.transpose(psC[:CH, 0:CH], n_t[:, :], ident[:CH, :CH]), reads=[n_b, identb], writes=[psCb])
        nt_t, nt_b = PTa
        p.op("dve", lambda e: e.tensor_copy(out=nt_t[:], in_=psC[:CH, 0:CH]), reads=[psCb], writes=[nt_b])
        tt_t, tt_b = TTa
        p.op("dve", lambda e: e.tensor_tensor(out=tt_t[:], in0=psC[:CH, 0:CH], in1=ident[:CH, :CH], op=ALU.add),
             reads=[psCb, identb], writes=[tt_b])
        yield
        P_cur, PT_cur, TT_cur = Pa, PTa, TTa
        P_alt, PT_alt, TT_alt = Pb_, PTb_, TTb_
        for k in range(1, 6):
            psD, psDb = qa.next()
            p.op("pe", lambda e: e.matmul(psD[:CH, 0:CH], lhsT=PT_cur[0][:, :], rhs=P_cur[0][:, :], start=True, stop=True),
                 reads=[PT_cur[1], P_cur[1]], writes=[psDb], inc=(k == 5))
            if k < 5:
                p.op("pe", lambda e: e.matmul(psD[:CH, CH:2 * CH], lhsT=P_cur[0][:, :], rhs=PT_cur[0][:, :], start=True, stop=True),
                     reads=[PT_cur[1], P_cur[1]], writes=[psDb])
            p.op("act", lambda e: e.copy(out=P_alt[0][:], in_=psD[:CH, 0:CH]), reads=[psDb], writes=[P_alt[1]])
            if k < 5:
                p.op("act", lambda e: e.copy(out=PT_alt[0][:], in_=psD[:CH, CH:2 * CH]), reads=[psDb], writes=[PT_alt[1]])
            yield
            psE, psEb = qd.next()
            p.op("pe", lambda e: e.matmul(psE[:CH, 0:CH], lhsT=P_alt[0][:, :], rhs=TT_cur[0][:, :], start=True, stop=True),
                 reads=[P_alt[1], TT_cur[1]], writes=[psEb])
            p.op("dve", lambda e: e.tensor_tensor(out=TT_alt[0][:], in0=psE[:CH, 0:CH], in1=TT_cur[0][:], op=ALU.add),
                 reads=[psEb, TT_cur[1]], writes=[TT_alt[1]])
            yield
            P_cur, P_alt = P_alt, P_cur
            PT_cur, PT_alt = PT_alt, PT_cur
            TT_cur, TT_alt = TT_alt, TT_cur
        ttb_t, ttb_b = TTb[par][h][ci]
        p.op("pool", lambda e: e.tensor_copy(out=ttb_t[:], in_=TT_cur[0][:]), reads=[TT_cur[1]], writes=[ttb_b])
        yield

    def scan_gen(ti, h):
        par, t0, hg = ti % 2, ti * TL, heads[h]
        S_t, S_b = St[h]
        Sb_t, Sb_b = Sb[h]
        rn_t, rn_b = t_rn[h]
        vn_t, vn_b = t_vn[h]
        oraw_t, oraw_b = t_oraw[h]
        o_t, o_b = t_o[h]
        o2_t, o2_b = t_o2[h]
        ss_t, ss_b = t_ss[h]
        oT_t, oT_b = oT[par][h]
        for ci in range(CPT):
            cs = slice(ci * CH, (ci + 1) * CH)
            sc_t, sc_b = scol[par][h][ci]
            vb_t, vb_b = vbeta[par][h][ci]
            ps1, ps1b = qd.next()
            p.op("pe", lambda e: e.matmul(ps1[:CH, 0:128], lhsT=kTb[par][h][0][:, cs], rhs=Sb_t[:, :], start=True, stop=True),
                 reads=[kTb[par][h][1], Sb_b], writes=[ps1b])
            p.op("dve", lambda e: e.scalar_tensor_tensor(out=rn_t[:], in0=ps1[:CH, 0:128], scalar=sc_t[:, 0:1], in1=vb_t[:],
                                                          op0=ALU.mult, op1=ALU.add), reads=[ps1b, sc_b, vb_b], writes=[rn_b])
            yield
            ps2, ps2b = qa.next()
            p.op("pe", lambda e: e.matmul(ps2[:CH, 0:128], lhsT=TTb[par][h][ci][0][:, :], rhs=rn_t[:, :], start=True, stop=True),
                 reads=[TTb[par][h][ci][1], rn_b], writes=[ps2b])
            p.op("act", lambda e: e.copy(out=vn_t[:], in_=ps2[:CH, 0:128]), reads=[ps2b], writes=[vn_b])
            yield
            pso, psob = qa.next()
            p.op("pe", lambda e: e.matmul(pso[:CH, 0:128], lhsT=qdecTb[par][h][ci][0][:, :], rhs=Sb_t[:, :], start=True, stop=False),
                 reads=[qdecTb[par][h][ci][1], Sb_b], writes=[psob], inc=False)
            p.op("pe", lambda e: e.matmul(pso[:CH, 0:128], lhsT=intraTb[par][h][ci][0][:, :], rhs=vn_t[:, :], start=False, stop=True),
                 reads=[intraTb[par][h][ci][1], vn_b], writes=[psob], inc=False)
            ps3, ps3b = qd.next()
            p.op("pe", lambda e: e.matmul(ps3[:, 0:128], lhsT=kdecb[par][h][ci][0][:, :], rhs=vn_t[:, :], start=True, stop=True),
                 reads=[kdecb[par][h][ci][1], vn_b], writes=[psob, ps3b])
            gt_t, gt_b = gtc[par][h][ci]
            p.op("dve", lambda e: e.scalar_tensor_tensor(out=S_t[:], in0=S_t[:], scalar=gt_t[:, 0:1], in1=ps3[:, 0:128],
                                                          op0=ALU.mult, op1=ALU.add), reads=[S_b, gt_b, ps3b], writes=[S_b])
            p.op("act", lambda e: e.copy(out=oraw_t[:], in_=pso[:CH, 0:128]), reads=[psob], writes=[oraw_b])
            yield
            p.op("pool", lambda e: e.tensor_copy(out=Sb_t[:], in_=S_t[:]), reads=[S_b], writes=[Sb_b])
            p.op("act", lambda e: e.activation(out=o_t[:], in_=oraw_t[:], func=AF.Square), reads=[oraw_b], writes=[o_b])
            yield
            p.op("dve", lambda e: e.reduce_sum(out=ss_t[:, 0:1], in_=o_t[:], axis=AX.X), reads=[o_b], writes=[ss_b])
            yield
            p.op("act", lambda e: e.activation(out=ss_t[:, 1:2], in_=ss_t[:, 0:1], func=AF.Sqrt, scale=1.0 / 128, bias=epsc[:CH, 0:1]),
                 reads=[ss_b, epsb], writes=[ss_b])
            yield
            p.op("dve", lambda e: e.reciprocal(out=ss_t[:, 2:3], in_=ss_t[:, 1:2]), reads=[ss_b], writes=[ss_b])
            yield
            p.op("dve", lambda e: e.scalar_tensor_tensor(out=o_t[:], in0=oraw_t[:], scalar=ss_t[:, 2:3], in1=gnw_t[:, :],
                                                          op0=ALU.mult, op1=ALU.mult), reads=[oraw_b, ss_b, gnwb, o_b], writes=[o_b])
            yield
            p.op("pool", lambda e: e.tensor_tensor(out=o2_t[:], in0=o_t[:], in1=zc[par][h][ci][0][:], op=ALU.mult),
                 reads=[o_b, zc[par][h][ci][1]], writes=[o2_b])
            yield
            ps4, ps4b = qa.next()
            p.op("pe", lambda e: e.transpose(ps4[:, 0:CH], o2_t[:, :], ident[:CH, :CH]), reads=[o2_b, identb], writes=[ps4b])
            p.op("act", lambda e: e.copy(out=oT_t[:, cs], in_=ps4[:, 0:CH]), reads=[ps4b], writes=[oT_b])
            if ci == CPT - 1:
                p.dma("sp", oaT[hg * 128:(hg + 1) * 128, t0:t0 + TL], oT_t[:, :], reads=[oT_b])
            yield

    def conv_all(ti):
        return [conv_gen(ti, h) for h in range(NH)]

    def chunk_all(ti):
        return [prep_gen(ti, h, ci) for ci in range(CPT) for h in range(NH)]

    run_gens([], [conv_all(0), chunk_all(0)])
    for ti in range(NT):
        scans = [scan_gen(ti, h) for h in range(NH)]
        if ti + 1 < NT:
            run_gens(scans, [conv_all(ti + 1), chunk_all(ti + 1)])
        else:
            run_gens(scans)


NEG_CAUSAL = -1.0e30
NEG_TAKEN = -2.0e30


def t5_bucket_np(dist):
    n = np.maximum(dist, 0)
    max_exact = 16
    lr = np.log(np.maximum(n, 1).astype(np.float32) / np.float32(max_exact)) / np.float32(math.log(128 / max_exact))
    large = max_exact + (lr * np.float32(32 - max_exact)).astype(np.int32)
    large = np.minimum(large, 31)
    return np.where(n < max_exact, n, large)


def dsa_consts(rel_bias):
    c = {}
    sp = np.arange(128)[:, None]
    tq = np.arange(512)[None, :]
    nb = np.zeros((8, 5, 128, 512), np.float32)
    for r in range(-1, 4):
        dist = tq - (r * 128 + sp)
        b = t5_bucket_np(dist)
        for h in range(8):
            nb[h, r + 1] = np.where(dist >= 0, rel_bias[b, h], np.float32(-30000.0))
    c["nb"] = nb
    c["cbias"] = np.ascontiguousarray(np.broadcast_to(rel_bias[31][None, :], (128, 8))).astype(np.float32)
    cadd = np.zeros((4, 128, 512), np.float32)
    for qi in range(4):
        cadd[qi] = np.where(np.arange(512)[None, :] <= qi * 128 + np.arange(128)[:, None], 0.0, NEG_CAUSAL)
    c["cadd"] = cadd
    return c


def emit_dsa(p, S, scr, cst):
    NG = S // 512
    topk = min(256, S // 4)
    NR = topk // 8
    SCALE = 128 ** -0.5
    qiT, kiT, wi, qbT, kbT, vb, obT = scr["qiT"], scr["kiT"], scr["wi"], scr["qbT"], scr["kbT"], scr["vb"], scr["obT"]

    identf = p.sb([128, 128], F32); identfb = Buf()
    p.dma("sp", identf[:], cst["ident"][:, :], writes=[identfb])
    identb = p.sb([128, 128], BF16); identbb = Buf()
    p.op("pool", lambda e: e.tensor_copy(out=identb[:], in_=identf[:]), reads=[identfb], writes=[identbb])
    cadd_t = p.sb([128, 4, 512], F32); caddb = Buf()
    p.dma("sp", cadd_t[:], cst["cadd"].rearrange("q p s -> p q s"), writes=[caddb])
    cbias_t = p.sb([128, 8], F32); cbiasb = Buf()
    p.dma("sp", cbias_t[:], cst["cbias"][:, :], writes=[cbiasb])
    kiT2 = p.sb([128, S], BF16); kib = Buf()
    p.dma("sp", kiT2[0:64, :], kiT[:, :], writes=[kib])
    p.dma("sp", kiT2[64:128, :], kiT[:, :], writes=[kib])
    sc = p.sb([128, S], F32); scb = Buf()
    maskT = p.sb([128, S // 128, 512], BF16); maskTb = Buf()

    lg = Ring(p, 3, [128, 512], F32, psum=True)
    ops = Ring(p, 4, [128, 512], F32, psum=True)
    ptf = Ring(p, 1, [128, 512], F32, psum=True)
    trb = ptf
    qi_r = Ring(p, 2, [128, 4, 128], BF16)
    wi_r = Ring(p, 2, [128, 8], F32)
    aw_r = Ring(p, 2, [128, 8], F32)
    sg_r = Ring(p, 2, [128, 8], F32)
    relu_r = Ring(p, 3, [128, 512], F32)
    m8_r = Ring(p, 4, [128, 8], F32)
    mk_r = Ring(p, 2, [128, 512], F32)
    qT_r = Ring(p, 2, [128, 512], BF16)
    kT_r = Ring(p, 3, [128, 512], BF16)
    v_r = Ring(p, 4, [128, 4, 129], BF16)
    for (v_t, v_b) in v_r.items:
        p.op("pool", lambda e: e.memset(v_t[:, :, 128:129], 1.0), writes=[v_b])
    nb_r = Ring(p, 5, [128, 512], F32)
    lgt_r = Ring(p, 2, [128, 512], F32)
    P_r = Ring(p, 3, [128, 512], BF16)
    Pm_r = Ring(p, 5, [128, 512], BF16)
    on_r = Ring(p, 2, [128, 128], F32)
    rc_r = Ring(p, 4, [128, 1], F32)
    obT_r = Ring(p, 2, [128, 512], BF16)

    for g in range(NG):
        L = 512 * (g + 1)
        for qi in range(4):
            t0 = g * 512 + qi * 128
            q_t, q_b = qi_r.next()
            p.dma("sp", q_t[:], qiT[:, t0:t0 + 128].rearrange("(hp p) t -> p hp t", p=128), writes=[q_b])
            w_t, w_b = wi_r.next()
            p.dma("sp", w_t[:], wi[t0:t0 + 128, :], writes=[w_b])
            aw_t, aw_b = aw_r.next()
            sg_t, sg_b = sg_r.next()
            p.op("act", lambda e: e.activation(out=aw_t[:], in_=w_t[:], func=AF.Abs), reads=[w_b], writes=[aw_b])
            p.op("act", lambda e: e.activation(out=sg_t[:], in_=w_t[:], func=AF.Sign), reads=[w_b], writes=[sg_b])
            for j in range(g + 1):
                js = slice(j * 512, (j + 1) * 512)
                for h in range(8):
                    hp, off = h // 2, (h % 2) * 64
                    ps, psb = lg.next()
                    p.op("pe", lambda e: e.matmul(ps[:, :], lhsT=q_t[off:off + 64, hp, :], rhs=kiT2[off:off + 64, js], start=True, stop=True),
                         reads=[q_b, kib], writes=[psb])
                    r_t, r_b = relu_r.next()
                    p.op("act", lambda e: e.activation(out=r_t[:], in_=ps[:, :], func=AF.Relu, scale=aw_t[:, h:h + 1]),
                         reads=[psb, aw_b], writes=[r_b])
                    if h == 0:
                        p.op("dve", lambda e: e.tensor_scalar(out=sc[:, js], in0=r_t[:], scalar1=sg_t[:, 0:1], scalar2=None, op0=ALU.mult),
                             reads=[r_b, sg_b], writes=[scb])
                    else:
                        p.op("dve", lambda e: e.scalar_tensor_tensor(out=sc[:, js], in0=r_t[:], scalar=sg_t[:, h:h + 1], in1=sc[:, js],
                                                                      op0=ALU.mult, op1=ALU.add), reads=[r_b, sg_b, scb], writes=[scb])
            ds = slice(g * 512, (g + 1) * 512)
            p.op("dve", lambda e: e.tensor_tensor(out=sc[:, ds], in0=sc[:, ds], in1=cadd_t[:, qi, :], op=ALU.add),
                 reads=[scb, caddb], writes=[scb])
            for r in range(NR):
                m_t, m_b = m8_r.next()
                p.op("dve", lambda e: e.max(out=m_t[:], in_=sc[:, :L]), reads=[scb], writes=[m_b])
                p.op("dve", lambda e: e.match_replace(out=sc[:, :L], in_to_replace=m_t[:], in_values=sc[:, :L], imm_value=NEG_TAKEN),
                     reads=[scb, m_b], writes=[scb])
            for j in range(g + 1):
                js = slice(j * 512, (j + 1) * 512)
                mk_t, mk_b = mk_r.next()
                p.op("dve", lambda e: e.tensor_single_scalar(out=mk_t[:], in_=sc[:, js], scalar=-1.5e30, op=ALU.is_lt),
                     reads=[scb], writes=[mk_b])
                tp, tpb = trb.next()
                for i in range(4):
                    p.op("pe", lambda e: e.transpose(tp[:, i * 128:(i + 1) * 128], mk_t[:, i * 128:(i + 1) * 128], identf[:, :]),
                         reads=[mk_b, identfb], writes=[tpb], inc=(i == 3))
                p.op("act", lambda e: e.copy(out=maskT[:, 4 * j:4 * j + 4, qi * 128:(qi + 1) * 128],
                                             in_=tp[:, :].rearrange("p (a b) -> p a b", b=128)),
                     reads=[tpb], writes=[maskTb])
        for h in range(8):
            qT_t, qT_b = qT_r.next()
            p.dma("sp", qT_t[:], qbT[h * 128:(h + 1) * 128, g * 512:(g + 1) * 512], writes=[qT_b])
            nbt = {}
            for r in range(-1 if g > 0 else 0, 4):
                n_t, n_b = nb_r.next()
                p.dma("sp", n_t[:], cst["nb"][h, r + 1, :, :], writes=[n_b])
                nbt[r] = (n_t, n_b)
            obank = [ops.next() for _ in range(4)]
            oacc = [(obank[qi][0][:, 0:129], obank[qi][1]) for qi in range(4)]
            steps = [(j, kbi) for j in range(g + 1) for kbi in range(4)]
            SKEW = 2
            live = {}
            kv = {}
            for i in range(len(steps) + SKEW):
                if i < len(steps):
                    j, kbi = steps[i]
                    if kbi == 0:
                        kT_t, kT_b = kT_r.next()
                        p.dma("sp", kT_t[:], kbT[h * 128:(h + 1) * 128, j * 512:(j + 1) * 512], writes=[kT_b])
                        v_t, v_b = v_r.next()
                        p.dma("sp", v_t[:, :, 0:128], vb[j * 512:(j + 1) * 512, h * 128:(h + 1) * 128].rearrange("(kb p) e -> p kb e", p=128),
                              writes=[v_b])
                        kv[j] = (kT_t, kT_b, v_t, v_b)
                    kT_t, kT_b, v_t, v_b = kv[j]
                    kb = 4 * j + kbi
                    r = kb - 4 * g
                    ps, psb = lg.next()
                    p.op("pe", lambda e: e.matmul(ps[:, :], lhsT=kT_t[:, kbi * 128:(kbi + 1) * 128], rhs=qT_t[:, :], start=True, stop=True),
                         reads=[kT_b, qT_b], writes=[psb])
                    P_t, P_b = P_r.next()
                    if r >= -1:
                        n_t, n_b = nbt[r]
                        l_t, l_b = lgt_r.next()
                        p.op("dve", lambda e: e.scalar_tensor_tensor(out=l_t[:], in0=ps[:, :], scalar=SCALE, in1=n_t[:],
                                                                      op0=ALU.mult, op1=ALU.add), reads=[psb, n_b], writes=[l_b])
                        p.op("act", lambda e: e.activation(out=P_t[:], in_=l_t[:], func=AF.Exp), reads=[l_b], writes=[P_b])
                    else:
                        p.op("act", lambda e: e.activation(out=P_t[:], in_=ps[:, :], func=AF.Exp, scale=SCALE, bias=cbias_t[:, h:h + 1]),
                             reads=[psb, cbiasb], writes=[P_b])
                    Pm_t, Pm_b = Pm_r.next()
                    p.op("pool", lambda e: e.tensor_tensor(out=Pm_t[:], in0=P_t[:], in1=maskT[:, kb, :], op=ALU.mult),
                         reads=[P_b, maskTb], writes=[Pm_b])
                    live[i] = (Pm_t, Pm_b, v_t, v_b, kb, kbi)
                if i - SKEW >= 0:
                    Pm_t, Pm_b, v_t, v_b, kb, kbi = live.pop(i - SKEW)
                    qis = [qi for qi in range(4) if kb <= 4 * g + qi]
                    for qi in qis:
                        o_t, o_b = oacc[qi]
                        p.op("pe", lambda e: e.matmul(o_t, lhsT=Pm_t[:, qi * 128:(qi + 1) * 128], rhs=v_t[:, kbi, :],
                                                      start=(kb == 0), stop=(kb == 4 * g + qi)),
                             reads=[Pm_b, v_b], writes=[o_b], inc=(qi == qis[-1]))
            ob_t, ob_b = obT_r.next()
            for qi in range(4):
                o_t, o_b = oacc[qi]
                rc_t, rc_b = rc_r.next()
                p.op("dve", lambda e: e.reciprocal(out=rc_t[:], in_=o_t[:, 128:129]), reads=[o_b], writes=[rc_b])
                on_t, on_b = on_r.next()
                p.op("dve", lambda e: e.tensor_scalar(out=on_t[:], in0=o_t[:, 0:128], scalar1=rc_t[:, 0:1], scalar2=None, op0=ALU.mult),
                     reads=[o_b, rc_b], writes=[on_b])
                pt, ptb = ptf.next()
                p.op("pe", lambda e: e.transpose(pt[:, 0:128], on_t[:, :], identf[:, :]), reads=[on_b, identfb], writes=[ptb])
                p.op("act", lambda e: e.copy(out=ob_t[:, qi * 128:(qi + 1) * 128], in_=pt[:, 0:128]), reads=[ptb], writes=[ob_b])
            p.dma("sp", obT[h * 128:(h + 1) * 128, g * 512:(g + 1) * 512], ob_t[:, :], reads=[ob_b])


def emit_ffn(p, S, scr, wts, lnp, xres, xout, xTb_next):
    oaT, obT, sgT = scr["oaT"], scr["obT"], scr["sgT"]
    NJ = D_FF // 128
    big = p.sb([128, NJ * 512], BF16)
    slot = [Buf() for _ in range(NJ)]
    aT = lambda j: big[:, j * 512:(j + 1) * 512]
    x1 = [(p.sb([128, 2048], F32), Buf()) for _ in range(4)]
    x1T = [(p.sb([128, 512], BF16), Buf()) for _ in range(16)]
    WS = 8192
    wring = Ring(p, 4, [128, WS], BF16)
    sg_r = Ring(p, 4, [128, 512], F32)
    m_r = Ring(p, 4, [128, 512], F32)
    si_r = Ring(p, 2, [128, 512], F32)
    ln_r = Ring(p, 2, [128, 2048], F32)
    st_r = Ring(p, 2, [128, 4, 6], F32)
    mv_r = Ring(p, 4, [128, 4], F32)
    epsc = p.sb([128, 1], F32); epsb = Buf()
    p.op("pool", lambda e: e.memset(epsc[:], LN_EPS), writes=[epsb])
    identf = p.sb([128, 128], F32); identfb = Buf()
    p.dma("sp", identf[:], scr["ident"][:, :], writes=[identfb])
    pr = Ring(p, 7, [128, 512], F32, psum=True)
    ptr = Ring(p, 1, [128, 512], F32, psum=True)

    def layer_norm(tt, gname, bname):
        x_t, x_b = x1[tt]
        st_t, st_b = st_r.next()
        for c in range(4):
            p.op("dve", lambda e: e.bn_stats(out=st_t[:, c, :], in_=x_t[:, c * 512:(c + 1) * 512]), reads=[x_b], writes=[st_b])
        mv_t, mv_b = mv_r.next()
        p.op("dve", lambda e: e.bn_aggr(out=mv_t[:, 0:2], in_=st_t[:].rearrange("p a b -> p (a b)")), reads=[st_b], writes=[mv_b])
        p.op("act", lambda e: e.activation(out=mv_t[:, 2:3], in_=mv_t[:, 1:2], func=AF.Sqrt, bias=epsc[:, 0:1]),
             reads=[mv_b, epsb], writes=[mv_b])
        p.op("dve", lambda e: e.reciprocal(out=mv_t[:, 2:3], in_=mv_t[:, 2:3]), reads=[mv_b], writes=[mv_b])
        p.op("dve", lambda e: e.scalar_tensor_tensor(out=mv_t[:, 3:4], in0=mv_t[:, 0:1], scalar=-1.0, in1=mv_t[:, 2:3],
                                                      op0=ALU.mult, op1=ALU.mult), reads=[mv_b], writes=[mv_b])
        p.op("act", lambda e: e.activation(out=x_t[:], in_=x_t[:], func=AF.Identity, scale=mv_t[:, 2:3], bias=mv_t[:, 3:4]),
             reads=[x_b, mv_b], writes=[x_b])
        g_t, g_b = ln_r.next()
        p.dma("sp", g_t[:], lnp[gname][:, :], writes=[g_b])
        p.op("pool", lambda e: e.tensor_tensor(out=x_t[:], in0=x_t[:], in1=g_t[:], op=ALU.mult), reads=[x_b, g_b], writes=[x_b])
        b_t, b_b = ln_r.next()
        p.dma("sp", b_t[:], lnp[bname][:, :], writes=[b_b])
        p.op("pool", lambda e: e.tensor_tensor(out=x_t[:], in0=x_t[:], in1=b_t[:], op=ALU.add), reads=[x_b, b_b], writes=[x_b])

    def transposes(tt, dst_tiles):
        x_t, x_b = x1[tt]
        for k4 in range(4):
            pt, ptb = ptr.next()
            for i in range(4):
                k = k4 * 4 + i
                p.op("pe", lambda e: e.transpose(pt[:, i * 128:(i + 1) * 128], x_t[:, k * 128:(k + 1) * 128], identf[:, :]),
                     reads=[x_b, identfb], writes=[ptb], inc=(i == 3))
            for i in range(4):
                k = k4 * 4 + i
                d_t, d_b = dst_tiles[k]
                p.op("act", lambda e: e.copy(out=d_t[:, tt * 128:(tt + 1) * 128], in_=pt[:, i * 128:(i + 1) * 128]),
                     reads=[ptb], writes=[d_b])

    for T in range(S // 512):
        ts = slice(T * 512, (T + 1) * 512)
        oa_v = big[:, 16 * 512:24 * 512].rearrange("p (k t) -> p k t", t=512)
        ob_v = big[:, 24 * 512:32 * 512].rearrange("p (k t) -> p k t", t=512)
        p.dma("sp", oa_v, oaT[:, ts].rearrange("(k p) t -> p k t", p=128), writes=slot[16:24])
        p.dma("sp", ob_v, obT[:, ts].rearrange("(k p) t -> p k t", p=128), writes=slot[24:32])
        for tt in range(4):
            p.dma("sp", x1[tt][0][:], xres[T * 512 + tt * 128:T * 512 + (tt + 1) * 128, :], writes=[x1[tt][1]])
        for fg in range(4):
            w_t, w_b = wring.next()
            wa_v = w_t[:, 0:4096].rearrange("p (k f) -> p k f", f=512)
            wb_v = w_t[:, 4096:8192].rearrange("p (k f) -> p k f", f=512)
            p.dma("sp", wa_v, wts["wa"][:, fg * 512:(fg + 1) * 512].rearrange("(k p) f -> p k f", p=128), writes=[w_b])
            p.dma("sp", wb_v, wts["wb"][:, fg * 512:(fg + 1) * 512].rearrange("(k p) f -> p k f", p=128), writes=[w_b])
            for fi in range(4):
                ft = fg * 4 + fi
                fs = slice(fi * 128, (fi + 1) * 128)
                psA, psAb = pr.next()
                for k in range(8):
                    p.op("pe", lambda e: e.matmul(psA[:, :], lhsT=wa_v[:, k, fs], rhs=oa_v[:, k, :], start=(k == 0), stop=(k == 7)),
                         reads=[w_b] + slot[16:24], writes=[psAb], inc=(k == 7))
                psB, psBb = pr.next()
                for k in range(8):
                    p.op("pe", lambda e: e.matmul(psB[:, :], lhsT=wb_v[:, k, fs], rhs=ob_v[:, k, :], start=(k == 0), stop=(k == 7)),
                         reads=[w_b] + slot[24:32], writes=[psBb], inc=(k == 7))
                ga_t, ga_b = sg_r.next()
                p.dma("sp", ga_t[:], sgT[ft * 128:(ft + 1) * 128, ts], writes=[ga_b])
                gb_t, gb_b = sg_r.next()
                p.dma("sp", gb_t[:], sgT[2048 + ft * 128:2048 + (ft + 1) * 128, ts], writes=[gb_b])
                m1_t, m1_b = m_r.next()
                p.op("dve", lambda e: e.tensor_tensor(out=m1_t[:], in0=psA[:, :], in1=ga_t[:], op=ALU.mult), reads=[psAb, ga_b], writes=[m1_b])
                m2_t, m2_b = m_r.next()
                p.op("dve", lambda e: e.tensor_tensor(out=m2_t[:], in0=psB[:, :], in1=gb_t[:], op=ALU.mult), reads=[psBb, gb_b], writes=[m2_b])
                p.op("pool", lambda e: e.tensor_tensor(out=aT(ft), in0=m1_t[:], in1=m2_t[:], op=ALU.add), reads=[m1_b, m2_b], writes=[slot[ft]])
        for nt in range(4):
            ns = slice(nt * 512, (nt + 1) * 512)
            w_t, w_b = wring.next()
            wo_v = w_t[:, :].rearrange("p (k n) -> p k n", n=512)
            p.dma("sp", wo_v, wts["wout"][:, ns].rearrange("(k p) n -> p k n", p=128), writes=[w_b])
            for tt in range(4):
                ps, psb = pr.next()
                for k in range(16):
                    p.op("pe", lambda e: e.matmul(ps[:, :], lhsT=aT(k)[:, tt * 128:(tt + 1) * 128], rhs=wo_v[:, k, :],
                                                  start=(k == 0), stop=(k == 15)), reads=[w_b, slot[k]], writes=[psb], inc=(k == 15))
                x_t, x_b = x1[tt]
                p.op("dve", lambda e: e.scalar_tensor_tensor(out=x_t[:, ns], in0=x_t[:, ns], scalar=ALPHA, in1=ps[:, :],
                                                              op0=ALU.mult, op1=ALU.add), reads=[x_b, psb], writes=[x_b])
        for tt in range(4):
            layer_norm(tt, "g1", "b1")
            transposes(tt, x1T)
        for cg in range(NJ // 4):
            wg_t, wg_b = wring.next()
            wg_v = wg_t[:, :].rearrange("p (k n) -> p k n", n=512)
            p.dma("sp", wg_v, wts["wfi"][:, cg * 512:(cg + 1) * 512].rearrange("(k p) n -> p k n", p=128), writes=[wg_b])
            wu_t, wu_b = wring.next()
            wu_v = wu_t[:, :].rearrange("p (k n) -> p k n", n=512)
            p.dma("sp", wu_v, wts["wfi"][:, D_FF + cg * 512:D_FF + (cg + 1) * 512].rearrange("(k p) n -> p k n", p=128), writes=[wu_b])
            for ci in range(4):
                jt = cg * 4 + ci
                cs = slice(ci * 128, (ci + 1) * 128)
                psG, psGb = pr.next()
                for k in range(16):
                    p.op("pe", lambda e: e.matmul(psG[:, :], lhsT=wg_v[:, k, cs], rhs=x1T[k][0][:, :], start=(k == 0), stop=(k == 15)),
                         reads=[wg_b, x1T[k][1]], writes=[psGb], inc=(k == 15))
                psU, psUb = pr.next()
                for k in range(16):
                    p.op("pe", lambda e: e.matmul(psU[:, :], lhsT=wu_v[:, k, cs], rhs=x1T[k][0][:, :], start=(k == 0), stop=(k == 15)),
                         reads=[wu_b, x1T[k][1]], writes=[psUb], inc=(k == 15))
                s_t, s_b = si_r.next()
                p.op("act", lambda e: e.activation(out=s_t[:], in_=psG[:, :], func=AF.Silu), reads=[psGb], writes=[s_b])
                p.op("dve", lambda e: e.tensor_tensor(out=aT(jt), in0=psU[:, :], in1=s_t[:], op=ALU.mult), reads=[psUb, s_b], writes=[slot[jt]])
        for nt in range(4):
            ns = slice(nt * 512, (nt + 1) * 512)
            acc = [pr.next() for _ in range(4)]
            for qd in range(4):
                w_t, w_b = wring.next()
                wf_v = w_t[:, 0:11 * 512].rearrange("p (k n) -> p k n", n=512)
                p.dma("sp", wf_v, wts["wfo"][qd * 11 * 128:(qd + 1) * 11 * 128, ns].rearrange("(k p) n -> p k n", p=128), writes=[w_b])
                for tt in range(4):
                    ps, psb = acc[tt]
                    for k in range(11):
                        j = qd * 11 + k
                        p.op("pe", lambda e: e.matmul(ps[:, :], lhsT=aT(j)[:, tt * 128:(tt + 1) * 128], rhs=wf_v[:, k, :],
                                                      start=(j == 0), stop=(j == NJ - 1)), reads=[w_b, slot[j]], writes=[psb],
                             inc=(k == 10))
            for tt in range(4):
                ps, psb = acc[tt]
                x_t, x_b = x1[tt]
                p.op("dve", lambda e: e.scalar_tensor_tensor(out=x_t[:, ns], in0=x_t[:, ns], scalar=ALPHA, in1=ps[:, :],
                                                              op0=ALU.mult, op1=ALU.add), reads=[x_b, psb], writes=[x_b])
        for tt in range(4):
            layer_norm(tt, "g2", "b2")
            p.dma("sp", xout[T * 512 + tt * 128:T * 512 + (tt + 1) * 128, :], x1[tt][0][:], reads=[x1[tt][1]])
            if xTb_next is not None:
                transposes(tt, x1T)
        if xTb_next is not None:
            for k in range(16):
                p.dma("sp", xTb_next[k * 128:(k + 1) * 128, ts], x1T[k][0][:, :], reads=[x1T[k][1]])


W_SHAPES = {"w_in": (D_MODEL, D_IN), "w_branch_a": (1024, D_MODEL), "w_branch_b": (1024, D_MODEL),
            "w_out": (D_MODEL, D_MODEL), "w_ffn_in": (D_MODEL, 2 * D_FF), "w_ffn_out": (D_FF, D_MODEL)}


def const_arrays(inputs, S, depth):
    NCH = S // CH
    c = dict(gdn_consts())
    c.update(dsa_consts(np.asarray(inputs["rel_bias"], np.float32)))
    c["gnw"] = np.ascontiguousarray(np.broadcast_to(np.asarray(inputs["gdn_norm_w"])[:depth, None, :], (depth, CH, 128))).astype(np.float32)
    cw = np.asarray(inputs["conv_w"])[:depth]
    c["cw"] = np.ascontiguousarray(cw.reshape(depth, 4, 3, 8, 128).transpose(0, 4, 2, 3, 1).reshape(depth, 128, 96)).astype(np.float32)
    dtb = np.zeros((depth, CH, NCH, 16), np.float32)
    negA = np.zeros((depth, CH, NCH, 16), np.float32)
    dtb[..., 0:8] = np.asarray(inputs["dt_bias"])[:depth, None, None, :]
    a_log = np.asarray(inputs["a_log"])[:depth]
    negA[..., 0:8] = a_log[:, None, None, :]
    c["dtb16"] = dtb.reshape(depth, CH, NCH * 16)
    c["alog16"] = negA.reshape(depth, CH, NCH * 16)
    for nm in ("ln1_g", "ln1_b", "ln2_g", "ln2_b"):
        c[nm] = np.ascontiguousarray(np.broadcast_to(np.asarray(inputs[nm])[:depth, None, :], (depth, 128, D_MODEL))).astype(np.float32)
    return c


def build_all(S, depth, cshapes, phases=("cast", "proj", "gdn", "dsa", "ffn")):
    p = Prog()
    x = p.dram("x", [S, D_MODEL], F32, "ExternalInput")
    xT = p.dram("xT", [D_MODEL, S], F32, "ExternalInput")
    wext = {nm: p.dram(nm, [depth] + list(sh), F32, "ExternalInput") for nm, sh in W_SHAPES.items()} if "cast" in phases else {}
    cst = {k: p.dram("c_" + k, list(sh), F32, "ExternalInput") for k, sh in cshapes.items()}
    out = p.dram("out", [S, D_MODEL], F32, "ExternalOutput")
    scr = {k: p.scratch("s_" + k, fn(S), dt) for k, (fn, dt) in SCR_SPEC.items()}
    scr["ident"] = cst["ident"]
    wb16 = {nm: [p.scratch("%s_b%d" % (nm, l), list(sh), BF16) for l in range(depth)] for nm, sh in W_SHAPES.items()}
    with p.phase():
        rings = (Ring(p, 3, [128, CAST_CH], F32), Ring(p, 3, [128, CAST_CH], BF16))
        k = emit_cast(p, xT, scr["xTb"], D_MODEL, S, rings)
        for l in range(depth if "cast" in phases else 0):
            for nm, sh in W_SHAPES.items():
                k = emit_cast(p, wext[nm][l], wb16[nm][l], sh[0], sh[1], rings, k)
    negA = p.scratch("s_negA", list(cshapes["alog16"]), F32)
    with p.phase():
        for l in range(depth):
            W = cshapes["alog16"][2]
            t = p.sb([CH, W], F32); tb = Buf()
            p.dma("sp", t[:], cst["alog16"][l, :, :], writes=[tb])
            p.op("act", lambda e: e.activation(out=t[:], in_=t[:], func=AF.Exp), reads=[tb], writes=[tb])
            p.op("dve", lambda e: e.tensor_scalar(out=t[:], in0=t[:], scalar1=-1.0, scalar2=None, op0=ALU.mult), reads=[tb], writes=[tb])
            p.dma("sp", negA[l, :, :], t[:], reads=[tb])
    cst["negA16"] = negA
    gbt = p.scratch("s_gbt", [3, CH, cshapes["alog16"][2]], F32)
    xbuf = [scr["xA"], scr["xB"]]
    for l in range(depth):
        last = (l == depth - 1)
        if "proj" in phases:
            with p.phase():
                emit_proj(p, S, scr["xTb"], wb16["w_in"][l], scr)
        if "gdn" in phases:
            with p.phase():
                emit_gdn_pre(p, S, scr, cst, l, gbt)
            for heads in ([0, 1, 2, 3], [4, 5, 6, 7]):
                with p.phase():
                    emit_gdn(p, S, scr, cst, l, heads, gbt)
        if "dsa" in phases:
            with p.phase():
                emit_dsa(p, S, scr, cst)
        if "ffn" not in phases:
            continue
        with p.phase():
            wts = {"wa": wb16["w_branch_a"][l], "wb": wb16["w_branch_b"][l], "wout": wb16["w_out"][l],
                   "wfi": wb16["w_ffn_in"][l], "wfo": wb16["w_ffn_out"][l]}
            lnp = {"g1": cst["ln1_g"][l], "b1": cst["ln1_b"][l], "g2": cst["ln2_g"][l], "b2": cst["ln2_b"][l]}
            emit_ffn(p, S, scr, wts, lnp, x if l == 0 else xbuf[(l - 1) % 2], out if last else xbuf[l % 2],
                     None if last else scr["xTb"])
    p.close()
    return p


def make_in_map(inputs, b, S, depth, cst):
    xb = np.ascontiguousarray(np.asarray(inputs["x"])[b, :S])
    m = {"x": xb, "xT": np.ascontiguousarray(xb.T)}
    for nm in W_SHAPES:
        m[nm] = np.ascontiguousarray(np.asarray(inputs[nm])[:depth])
    for k, v in cst.items():
        m["c_" + k] = v
    return m


def kernel(**inputs):
    S, depth = SEQ, DEPTH
    cst = const_arrays(inputs, S, depth)
    p = build_all(S, depth, {k: v.shape for k, v in cst.items()})
    in_maps = [make_in_map(inputs, b, S, depth, cst) for b in range(BATCH)]
    res = run(p, in_maps).results
    return np.stack([np.asarray(res[b]["out"], np.float32) for b in range(BATCH)], 0)
```

```python
import contextlib
import math
import numpy as np
import ml_dtypes
import concourse.bass as bass
import concourse.mybir as mybir
from concourse.bass_utils import run_bass_kernel_spmd

F32 = mybir.dt.float32
BF16 = mybir.dt.bfloat16
AF = mybir.ActivationFunctionType
ALU = mybir.AluOpType
AX = mybir.AxisListType
NPBF = ml_dtypes.bfloat16

D_MODEL = 2048
BATCH = 4
SEQ = 8192
DEPTH = 4
NCORE = 4
D_FF = 5632
D_IN = 11864
ALPHA = (2 * DEPTH) ** 0.25
LN_EPS = 1e-5
RMS_EPS = 1e-6


class Buf:
    __slots__ = ("w", "r", "name")

    def __init__(self, name=""):
        self.w = None
        self.r = {}
        self.name = name


class Prog:
    ND = 24

    def __init__(self):
        self.nc = bass.Bass("TRN2", target_bir_lowering=False)
        nc = self.nc
        self.es = contextlib.ExitStack()
        self.eng = {"pe": nc.tensor, "act": nc.scalar, "dve": nc.vector, "pool": nc.gpsimd, "sp": nc.sync}
        self.sem = {e: self.es.enter_context(nc.semaphore("s_" + e)) for e in self.eng}
        self.dsem = [self.es.enter_context(nc.semaphore("d%d" % i)) for i in range(self.ND)]
        self.cnt = {e: 0 for e in self.eng}
        self.pending = {e: False for e in self.eng}
        self.known = {e: {} for e in self.eng}
        self.ndma = 0
        self.nins = 0
        self._names = 0
        self.pes = None

    def dram(self, name, shape, dt, kind):
        return self.nc.dram_tensor(name, list(shape), dt, kind=kind).ap()

    def sb(self, shape, dt, name=None):
        self._names += 1
        st = self.pes if self.pes is not None else self.es
        return st.enter_context(self.nc.sbuf_tensor(name or ("sb%d" % self._names), list(shape), dt))

    def ps(self, shape, dt=F32, name=None):
        self._names += 1
        st = self.pes if self.pes is not None else self.es
        return st.enter_context(self.nc.psum_tensor(name or ("ps%d" % self._names), list(shape), dt))

    def scratch(self, name, shape, dt):
        return self.nc.dram_tensor(name, list(shape), dt, kind="Internal").ap()

    def barrier(self):
        for e in self.pending:
            assert not self.pending[e], "engine %s has un-inc'd instruction at barrier" % e
        deps = [("e", f, self.cnt[f]) for f in self.eng if self.cnt[f] > 0]
        for i in range(min(self.ndma, self.ND)):
            last = self.ndma - 1 - ((self.ndma - 1 - i) % self.ND)
            deps.append(("d", i, 16 * (last // self.ND + 1)))
        for e in self.eng:
            self._wait(e, [d for d in deps if not (d[0] == "e" and d[1] == e)])

    @contextlib.contextmanager
    def phase(self):
        assert self.pes is None
        self.pes = contextlib.ExitStack()
        try:
            yield
            self.barrier()
        finally:
            self.pes.close()
            self.pes = None

    def _deps(self, reads, writes):
        deps = []
        for b in reads:
            if b.w is not None:
                deps.append(b.w)
        for b in writes:
            if b.w is not None:
                deps.append(b.w)
            deps.extend(b.r.values())
        return deps

    def _wait(self, e, deps):
        kn = self.known[e]
        need = {}
        for (kind, key, val) in deps:
            if kind == "e" and key == e and e == "pe":
                continue
            k = (kind, key)
            if kn.get(k, 0) >= val:
                continue
            if need.get(k, 0) < val:
                need[k] = val
        for (kind, key), val in need.items():
            if kind == "e" and key == e:
                assert val <= self.cnt[e], "self-wait on pending (un-inc'd) instruction"
            s = self.sem[key] if kind == "e" else self.dsem[key]
            self.eng[e].wait_ge(s, val)
            kn[(kind, key)] = val
            self.nins += 1

    def _mark(self, tok, reads, writes):
        k = (tok[0], tok[1])
        for b in reads:
            b.r[k] = tok
        for b in writes:
            b.w = tok
            b.r = {}

    def op(self, e, fn, reads=(), writes=(), inc=True):
        self._wait(e, self._deps(reads, writes))
        ins = fn(self.eng[e])
        if inc:
            self.cnt[e] += 1
            ins.then_inc(self.sem[e], 1)
            tok = ("e", e, self.cnt[e])
            self.pending[e] = False
        else:
            tok = ("e", e, self.cnt[e] + 1)
            self.pending[e] = True
        self.nins += 1
        self._mark(tok, reads, writes)
        return ins

    def dma(self, q, out, in_, reads=(), writes=(), sink=None, **kw):
        i = self.ndma
        self.ndma += 1
        s = i % self.ND
        val = 16 * (i // self.ND + 1)
        deps = self._deps(reads, writes)
        if val > 16:
            deps.append(("d", s, val - 16))
        self._wait(q, deps)
        ins = self.eng[q].dma_start(out=out, in_=in_, **kw)
        ins.then_inc(self.dsem[s], 16)
        self.nins += 1
        self._mark(("d", s, val), reads, writes)
        if sink is not None:
            sink.append(("d", s, val))
        return ins

    def finish(self, sinks):
        deps = []
        for b in sinks:
            deps.extend(b)
        self._wait("sp", deps)
        for e in self.pending:
            assert not self.pending[e], "engine %s ends with un-inc'd instruction" % e

    def close(self):
        self.es.close()


class Ring:
    def __init__(self, p, n, shape, dt, psum=False):
        self.items = []
        for _ in range(n):
            t = p.ps(shape, dt) if psum else p.sb(shape, dt)
            self.items.append((t, Buf()))
        self.i = 0

    def next(self):
        it = self.items[self.i % len(self.items)]
        self.i += 1
        return it


def run(prog, in_maps):
    return run_bass_kernel_spmd(prog.nc, in_maps, core_ids=list(range(len(in_maps))))


CAST_CH = 4096


def emit_cast(p, src2d, dst2d, rows, cols, rings, k0=0):
    st, ob = rings
    sv = src2d.rearrange("(p a) c -> p (a c)", p=128)
    dv = dst2d.rearrange("(p a) c -> p (a c)", p=128)
    m = rows // 128 * cols
    k = k0
    for c0 in range(0, m, CAST_CH):
        n = min(CAST_CH, m - c0)
        s_t, s_b = st.next()
        o_t, o_b = ob.next()
        p.dma("sp", s_t[:, :n], sv[:, c0:c0 + n], writes=[s_b])
        e = ("dve", "act", "pool")[k % 3]
        k += 1
        if e == "act":
            p.op(e, lambda en: en.copy(out=o_t[:, :n], in_=s_t[:, :n]), reads=[s_b], writes=[o_b])
        else:
            p.op(e, lambda en: en.tensor_copy(out=o_t[:, :n], in_=s_t[:, :n]), reads=[s_b], writes=[o_b])
        p.dma("sp", dv[:, c0:c0 + n], o_t[:, :n], reads=[o_b])
    return k


C_QKV, C_A, C_Z, C_QB, C_KB, C_VB, C_QI, C_KI, C_WI, C_GA = 0, 3072, 3088, 4112, 5136, 6160, 7184, 7696, 7760, 7768


def proj_groups():
    g = []
    for i in range(6):
        g.append(("F", C_QKV + 512 * i, 512, "qkvT", 512 * i))
    g.append(("T", C_A, 16, "ab", 0))
    for i in range(2):
        g.append(("T", C_Z + 512 * i, 512, "z", 512 * i))
    for i in range(2):
        g.append(("F", C_QB + 512 * i, 512, "qbT", 512 * i))
    for i in range(2):
        g.append(("F", C_KB + 512 * i, 512, "kbT", 512 * i))
    for i in range(2):
        g.append(("T", C_VB + 512 * i, 512, "vb", 512 * i))
    g.append(("F", C_QI, 512, "qiT", 0))
    g.append(("F", C_KI, 64, "kiT", 0))
    g.append(("T", C_WI, 8, "wi", 0))
    for i in range(8):
        g.append(("F", C_GA + 512 * i, 512, "sgT", 512 * i))
    return g


SCR_SPEC = {
    "qkvT": (lambda S: [3072, S], F32), "qbT": (lambda S: [1024, S], BF16), "kbT": (lambda S: [1024, S], BF16),
    "qiT": (lambda S: [512, S], BF16), "kiT": (lambda S: [64, S], BF16), "sgT": (lambda S: [4096, S], F32),
    "ab": (lambda S: [S, 16], F32), "z": (lambda S: [S, 1024], F32), "vb": (lambda S: [S, 1024], BF16),
    "wi": (lambda S: [S, 8], F32), "oaT": (lambda S: [1024, S], BF16), "obT": (lambda S: [1024, S], BF16),
    "x1": (lambda S: [S, 2048], F32), "xA": (lambda S: [S, 2048], F32), "xB": (lambda S: [S, 2048], F32),
    "xTb": (lambda S: [2048, S], BF16),
}


def emit_proj(p, S, xTb, w, scr):
    TG = min(2048, S)
    KC = D_MODEL // 128
    wv = w.rearrange("(kc p) n -> p kc n", p=128)
    xb = [(p.sb([128, TG], BF16), Buf()) for _ in range(KC)]
    wr = Ring(p, 2, [128, KC, 512], BF16)
    pr = Ring(p, 6, [128, 512], F32, psum=True)
    sf = Ring(p, 4, [128, 512], F32)
    sh = Ring(p, 4, [128, 512], BF16)
    ev = 0
    for tg in range(S // TG):
        t0 = tg * TG
        for kc in range(KC):
            xt, xbuf = xb[kc]
            p.dma("sp", xt[:], xTb[kc * 128:(kc + 1) * 128, t0:t0 + TG], writes=[xbuf])
        for (mode, c0, n, oname, r0) in proj_groups():
            w_t, w_b = wr.next()
            p.dma("sp", w_t[:, :, :n], wv[:, :, c0:c0 + n], writes=[w_b])
            o_ap = scr[oname]
            o_dt = SCR_SPEC[oname][1]
            stg = sf if o_dt == F32 else sh
            if mode == "F":
                for ci in range(0, n, 128):
                    cn = min(128, n - ci)
                    for tt in range(TG // 512):
                        ps_t, ps_b = pr.next()
                        for kc in range(KC):
                            xt, xbuf = xb[kc]
                            p.op("pe", lambda en: en.matmul(ps_t[:cn, :], lhsT=w_t[:, kc, ci:ci + cn],
                                                            rhs=xt[:, tt * 512:(tt + 1) * 512],
                                                            start=(kc == 0), stop=(kc == KC - 1)),
                                 reads=[w_b, xbuf], writes=[ps_b], inc=(kc == KC - 1))
                        g_t, g_b = stg.next()
                        if oname == "sgT":
                            p.op("act", lambda en: en.activation(out=g_t[:cn, :], in_=ps_t[:cn, :], func=AF.Sigmoid),
                                 reads=[ps_b], writes=[g_b])
                        else:
                            ev += 1
                            if ev % 2:
                                p.op("dve", lambda en: en.tensor_copy(out=g_t[:cn, :], in_=ps_t[:cn, :]),
                                     reads=[ps_b], writes=[g_b])
                            else:
                                p.op("act", lambda en: en.copy(out=g_t[:cn, :], in_=ps_t[:cn, :]),
                                     reads=[ps_b], writes=[g_b])
                        p.dma("sp", o_ap[r0 + ci:r0 + ci + cn, t0 + tt * 512:t0 + (tt + 1) * 512], g_t[:cn, :],
                              reads=[g_b])
            else:
                for tt in range(TG // 128):
                    ps_t, ps_b = pr.next()
                    for kc in range(KC):
                        xt, xbuf = xb[kc]
                        p.op("pe", lambda en: en.matmul(ps_t[:, :n], lhsT=xt[:, tt * 128:(tt + 1) * 128],
                                                        rhs=w_t[:, kc, :n],
                                                        start=(kc == 0), stop=(kc == KC - 1)),
                             reads=[w_b, xbuf], writes=[ps_b], inc=(kc == KC - 1))
                    g_t, g_b = stg.next()
                    ev += 1
                    if ev % 2:
                        p.op("dve", lambda en: en.tensor_copy(out=g_t[:, :n], in_=ps_t[:, :n]),
                             reads=[ps_b], writes=[g_b])
                    else:
                        p.op("act", lambda en: en.copy(out=g_t[:, :n], in_=ps_t[:, :n]),
                             reads=[ps_b], writes=[g_b])
                    p.dma("sp", o_ap[t0 + tt * 128:t0 + (tt + 1) * 128, r0:r0 + n], g_t[:, :n],
                          reads=[g_b])


CH = 64


def gdn_consts():
    i = np.arange(CH)
    c = {}
    c["ident"] = np.eye(128, dtype=np.float32)
    c["ones"] = np.ones((128, 128), np.float32)
    c["ucum"] = (i[:, None] <= i[None, :]).astype(np.float32)
    c["stril"] = (i[:, None] > i[None, :]).astype(np.float32)
    c["triu"] = (i[:, None] <= i[None, :]).astype(np.float32)
    return c


def run_gens(always, stages=()):
    active = list(always)
    stages = [list(st) for st in stages]
    cur = stages.pop(0) if stages else []
    active += cur
    while active:
        for g in list(active):
            try:
                next(g)
            except StopIteration:
                active.remove(g)
                if g in cur:
                    cur.remove(g)
        if not cur and stages:
            cur = stages.pop(0)
            active += cur


def emit_gdn_pre(p, S, scr, cst, l, gbt):
    NCH = S // CH
    W = NCH * 16
    ab_t = p.sb([CH, W], F32); abb = Buf()
    p.dma("sp", ab_t[:].rearrange("c (n k) -> c n k", k=16), scr["ab"].rearrange("(n c) k -> c n k", c=CH), writes=[abb])
    dtb_t = p.sb([CH, W], F32); dtbb = Buf()
    p.dma("sp", dtb_t[:], cst["dtb16"][l, :, :], writes=[dtbb])
    negA_t = p.sb([CH, W], F32); negAb = Buf()
    p.dma("sp", negA_t[:], cst["negA16"][l, :, :], writes=[negAb])
    g_t = p.sb([CH, W], F32); gb = Buf()
    beta_t = p.sb([CH, W], F32); betab = Buf()
    nbeta_t = p.sb([CH, W], F32); nbetab = Buf()
    tmpw = p.sb([CH, W], F32); tmpwb = Buf()
    p.op("dve", lambda e: e.tensor_tensor(out=tmpw[:], in0=ab_t[:], in1=dtb_t[:], op=ALU.add), reads=[abb, dtbb], writes=[tmpwb])
    p.op("act", lambda e: e.activation(out=tmpw[:], in_=tmpw[:], func=AF.Exp), reads=[tmpwb], writes=[tmpwb])
    p.op("act", lambda e: e.activation(out=tmpw[:], in_=tmpw[:], func=AF.Ln, bias=1.0), reads=[tmpwb], writes=[tmpwb])
    p.op("dve", lambda e: e.tensor_tensor(out=g_t[:], in0=tmpw[:], in1=negA_t[:], op=ALU.mult), reads=[tmpwb, negAb], writes=[gb])
    p.op("act", lambda e: e.activation(out=beta_t[:], in_=ab_t[:], func=AF.Sigmoid), reads=[abb], writes=[betab])
    p.op("dve", lambda e: e.tensor_scalar(out=nbeta_t[:], in0=beta_t[:], scalar1=-1.0, scalar2=None, op0=ALU.mult), reads=[betab], writes=[nbetab])
    p.dma("sp", gbt[0, :, :], g_t[:], reads=[gb])
    p.dma("sp", gbt[1, :, :], beta_t[:], reads=[betab])
    p.dma("sp", gbt[2, :, :], nbeta_t[:], reads=[nbetab])


def emit_gdn(p, S, scr, cst, l, heads, gbt):
    NH = len(heads)
    NCH = S // CH
    TL = 256
    NT = S // TL
    CPT = TL // CH
    qkvT, zin, oaT = scr["qkvT"], scr["z"], scr["oaT"]

    def cload(ap, shape, dt=F32):
        t = p.sb(shape, dt)
        b = Buf()
        p.dma("sp", t[:], ap, writes=[b])
        return t, b

    ident, identb = cload(cst["ident"][:, :], [128, 128])
    ones, onesb = cload(cst["ones"][:, :], [128, 128])
    ucum, ucumb = cload(cst["ucum"][:, :], [CH, CH])
    stril, strilb = cload(cst["stril"][:, :], [CH, CH])
    triu, triub = cload(cst["triu"][:, :], [CH, CH])
    gnw_t, gnwb = cload(cst["gnw"][l, :, :], [CH, 128])
    cw_t, cwb = cload(cst["cw"][l, :, :], [128, 96])
    W = NCH * 16
    g_t, gb = cload(gbt[0, :, :], [CH, W])
    beta_t, betab = cload(gbt[1, :, :], [CH, W])
    nbeta_t, nbetab = cload(gbt[2, :, :], [CH, W])
    epsc = p.sb([128, 1], F32); epsb = Buf()
    p.op("pool", lambda e: e.memset(epsc[:], RMS_EPS), writes=[epsb])

    banks = [p.ps([128, 512], F32) for _ in range(8)]

    class QRing:
        def __init__(self, bank_ids):
            self.items = [(banks[b], Buf()) for b in bank_ids]
            self.i = 0

        def next(self):
            it = self.items[self.i % len(self.items)]
            self.i += 1
            return it
    qa = QRing([0, 1, 2])
    qd = QRing([3, 4, 5, 6])
    nrmb = Buf()
    nrm = [(banks[7][:, 0:256], nrmb), (banks[7][:, 0:256], nrmb)]
    nrm_i = [0]

    def mk(shape, dt):
        return [[(p.sb(shape, dt), Buf()) for _ in range(NH)] for _ in range(2)]
    qTf = mk([128, TL], F32); kTf = mk([128, TL], F32); vTf = mk([128, TL], F32)
    qTb = mk([128, TL], BF16); kTb = mk([128, TL], BF16)
    oT = mk([128, TL], BF16)

    def mkc(shape, dt):
        return [[[(p.sb(shape, dt), Buf()) for _ in range(CPT)] for _ in range(NH)] for _ in range(2)]
    TTb = mkc([CH, CH], BF16); intraTb = mkc([CH, CH], BF16); qdecTb = mkc([128, CH], BF16)
    kdecb = mkc([CH, 128], BF16); vbeta = mkc([CH, 128], F32); scol = mkc([CH, 2], F32); gtc = mkc([128, 1], F32)
    zc = mkc([CH, 128], F32)

    def mkp(shape, dt):
        return [[(p.sb(shape, dt), Buf()) for _ in range(CPT)] for _ in range(NH)]
    t_gbw = mkp([CH, 128], F32); t_gc = mkp([CH, 4], F32); t_eg = mkp([128, CH], F32); t_zr = mkp([CH, 128], F32)
    t_s64 = [mkp([CH, CH], F32) for _ in range(9)]
    t_x = [(p.sb([128, TL + 3], F32), Buf()) for _ in range(NH)]
    t_c = [(p.sb([128, TL], F32), Buf()) for _ in range(NH)]
    t_rn = [(p.sb([CH, 128], BF16), Buf()) for _ in range(NH)]
    t_vn = [(p.sb([CH, 128], BF16), Buf()) for _ in range(NH)]
    t_oraw = [(p.sb([CH, 128], F32), Buf()) for _ in range(NH)]
    t_o = [(p.sb([CH, 128], F32), Buf()) for _ in range(NH)]
    t_o2 = [(p.sb([CH, 128], F32), Buf()) for _ in range(NH)]
    t_ss = [(p.sb([CH, 4], F32), Buf()) for _ in range(NH)]
    St = [(p.sb([128, 128], F32), Buf()) for _ in range(NH)]
    Sb = [(p.sb([128, 128], BF16), Buf()) for _ in range(NH)]
    for h in range(NH):
        p.op("pool", lambda e: e.memset(St[h][0][:], 0.0), writes=[St[h][1]])
        p.op("pool", lambda e: e.memset(Sb[h][0][:], 0.0), writes=[Sb[h][1]])

    def conv_gen(ti, h):
        par, t0, hg = ti % 2, ti * TL, heads[h]
        x_t, x_b = t_x[h]
        c_t, c_b = t_c[h]
        for a in range(3):
            r0 = a * 1024 + hg * 128
            if t0 == 0:
                p.op("pool", lambda e: e.memset(x_t[:, 0:3], 0.0), writes=[x_b])
                p.dma("sp", x_t[:, 3:TL + 3], qkvT[r0:r0 + 128, 0:TL], writes=[x_b])
            else:
                p.dma("sp", x_t[:, :], qkvT[r0:r0 + 128, t0 - 3:t0 + TL], writes=[x_b])
            wcol = lambda j: cw_t[:, (a * 8 + hg) * 4 + j:(a * 8 + hg) * 4 + j + 1]
            p.op("dve", lambda e: e.tensor_scalar(out=c_t[:], in0=x_t[:, 0:TL], scalar1=wcol(0), scalar2=None, op0=ALU.mult),
                 reads=[x_b, cwb], writes=[c_b])
            yield
            for j in range(1, 4):
                p.op("dve", lambda e: e.scalar_tensor_tensor(out=c_t[:], in0=x_t[:, j:j + TL], scalar=wcol(j), in1=c_t[:],
                                                              op0=ALU.mult, op1=ALU.add), reads=[x_b, cwb, c_b], writes=[c_b])
                yield
            dstf = (qTf, kTf, vTf)[a][par][h]
            p.op("act", lambda e: e.activation(out=dstf[0][:], in_=c_t[:], func=AF.Silu), reads=[c_b], writes=[dstf[1]])
            yield
            if a < 2:
                p.op("pool", lambda e: e.tensor_tensor(out=c_t[:], in0=dstf[0][:], in1=dstf[0][:], op=ALU.mult),
                     reads=[dstf[1]], writes=[c_b])
                yield
                ps_t, ps_b = nrm[nrm_i[0] % 2]
                nrm_i[0] += 1
                p.op("pe", lambda e: e.matmul(ps_t, lhsT=ones[:, :], rhs=c_t[:], start=True, stop=True),
                     reads=[onesb, c_b], writes=[ps_b])
                p.op("act", lambda e: e.activation(out=c_t[:], in_=ps_t, func=AF.Sqrt, bias=epsc[:, 0:1]),
                     reads=[ps_b, epsb], writes=[c_b])
                yield
                p.op("dve", lambda e: e.reciprocal(out=c_t[:], in_=c_t[:]), reads=[c_b], writes=[c_b])
                sc = (128 ** -0.5) if a == 0 else 1.0
                p.op("dve", lambda e: e.scalar_tensor_tensor(out=dstf[0][:], in0=dstf[0][:], scalar=sc, in1=c_t[:],
                                                              op0=ALU.mult, op1=ALU.mult), reads=[dstf[1], c_b], writes=[dstf[1]])
                yield
                dstb = (qTb, kTb)[a][par][h]
                p.op("pool", lambda e: e.tensor_copy(out=dstb[0][:], in_=dstf[0][:]), reads=[dstf[1]], writes=[dstb[1]])
                yield

    def prep_gen(ti, h, ci):
        par, hg = ti % 2, heads[h]
        n = ti * CPT + ci
        colg, colb = n * 16 + hg, n * 16 + 8 + hg
        cs = slice(ci * CH, (ci + 1) * CH)
        gcol = g_t[:, colg:colg + 1]
        tmp = [t_s64[k][h][ci] for k in range(9)]
        (gcr_t, gcr_b), (e1_t, e1_b), (e2_t, e2_b) = tmp[0], tmp[1], tmp[2]
        Pa, Pb_, PTa, PTb_, TTa, TTb_ = tmp[3], tmp[4], tmp[5], tmp[6], tmp[7], tmp[8]
        zr_t, zr_b = t_zr[h][ci]
        gbw_t, gbw_b = t_gbw[h][ci]
        gc_t, gc_b = t_gc[h][ci]
        eg_t, eg_b = t_eg[h][ci]
        sc_t, sc_b = scol[par][h][ci]
        gt_t, gt_b = gtc[par][h][ci]
        p.dma("sp", zr_t[:], zin[n * CH:(n + 1) * CH, hg * 128:(hg + 1) * 128], writes=[zr_b])
        p.op("act", lambda e: e.activation(out=zc[par][h][ci][0][:], in_=zr_t[:], func=AF.Silu),
             reads=[zr_b], writes=[zc[par][h][ci][1]])
        p.op("dve", lambda e: e.tensor_scalar(out=gbw_t[:, :], in0=ones[:CH, :128], scalar1=gcol, scalar2=None, op0=ALU.mult),
             reads=[onesb, gb], writes=[gbw_b])
        psA, psAb = qa.next()
        p.op("pe", lambda e: e.matmul(psA[:, 0:CH], lhsT=gbw_t[:, :], rhs=ucum[:, :], start=True, stop=True),
             reads=[gbw_b, ucumb], writes=[psAb], inc=False)
        p.op("pe", lambda e: e.matmul(psA[:CH, CH:2 * CH], lhsT=ucum[:, :], rhs=gbw_t[:, :CH], start=True, stop=True),
             reads=[ucumb, gbw_b], writes=[psAb])
        p.op("act", lambda e: e.copy(out=gc_t[:, 0:1], in_=psA[:CH, CH:CH + 1]), reads=[psAb], writes=[gc_b])
        p.op("act", lambda e: e.copy(out=gc_t[:, 1:2], in_=psA[:CH, CH - 1:CH]), reads=[psAb], writes=[gc_b])
        p.op("act", lambda e: e.activation(out=gt_t[:, :], in_=psA[:, CH - 1:CH], func=AF.Exp), reads=[psAb], writes=[gt_b])
        p.op("act", lambda e: e.copy(out=gcr_t[:], in_=psA[:CH, 0:CH]), reads=[psAb], writes=[gcr_b])
        p.op("act", lambda e: e.activation(out=eg_t[:, :], in_=psA[:, 0:CH], func=AF.Exp), reads=[psAb], writes=[eg_b])
        yield
        p.op("act", lambda e: e.activation(out=gc_t[:, 2:3], in_=gc_t[:, 0:1], func=AF.Exp), reads=[gc_b], writes=[gc_b])
        p.op("act", lambda e: e.activation(out=sc_t[:, 1:2], in_=gc_t[:, 0:1], func=AF.Exp, scale=-1.0, bias=gc_t[:, 1:2]),
             reads=[gc_b, sc_b], writes=[sc_b])
        p.op("dve", lambda e: e.tensor_scalar(out=e1_t[:], in0=gcr_t[:], scalar1=gc_t[:, 0:1], scalar2=0.0,
                                              op0=ALU.subtract, op1=ALU.max), reads=[gcr_b, gc_b], writes=[e1_b])
        p.op("dve", lambda e: e.tensor_scalar(out=e2_t[:], in0=gcr_t[:], scalar1=gc_t[:, 0:1], scalar2=0.0,
                                              op0=ALU.subtract, op1=ALU.min), reads=[gcr_b, gc_b], writes=[e2_b])
        p.op("pool", lambda e: e.tensor_tensor(out=qdecTb[par][h][ci][0][:], in0=qTf[par][h][0][:, cs], in1=eg_t[:, :], op=ALU.mult),
             reads=[qTf[par][h][1], eg_b], writes=[qdecTb[par][h][ci][1]])
        yield
        p.op("dve", lambda e: e.tensor_tensor(out=sc_t[:, 0:1], in0=gc_t[:, 2:3], in1=nbeta_t[:, colb:colb + 1], op=ALU.mult),
             reads=[gc_b, nbetab, sc_b], writes=[sc_b])
        p.op("act", lambda e: e.activation(out=e1_t[:], in_=e1_t[:], func=AF.Exp, scale=-1.0), reads=[e1_b], writes=[e1_b])
        p.op("act", lambda e: e.activation(out=e2_t[:], in_=e2_t[:], func=AF.Exp), reads=[e2_b], writes=[e2_b])
        yield
        p.op("pool", lambda e: e.tensor_tensor(out=e2_t[:], in0=e2_t[:], in1=triu[:, :], op=ALU.mult), reads=[e2_b, triub], writes=[e2_b])
        kb_t, kb_b = kTb[par][h]
        qb_t, qb_b = qTb[par][h]
        psBk, psKb = qd.next()
        psK, psT, psV = psBk[:, 0:128], psBk[:, 128:256], psBk[:, 256:384]
        psTb = psVb = psKb
        p.op("pe", lambda e: e.matmul(psK[:CH, 0:CH], lhsT=kb_t[:, cs], rhs=kb_t[:, cs], start=True, stop=True),
             reads=[kb_b], writes=[psKb], inc=False)
        p.op("pe", lambda e: e.matmul(psK[:CH, CH:2 * CH], lhsT=kb_t[:, cs], rhs=qb_t[:, cs], start=True, stop=True),
             reads=[kb_b, qb_b], writes=[psKb], inc=False)
        p.op("pe", lambda e: e.transpose(psT[:CH, :], kTf[par][h][0][:, cs], ident[:, :]),
             reads=[kTf[par][h][1], identb], writes=[psTb], inc=False)
        p.op("pe", lambda e: e.transpose(psV[:CH, :], vTf[par][h][0][:, cs], ident[:, :]),
             reads=[vTf[par][h][1], identb], writes=[psVb])
        it_t, it_b = intraTb[par][h][ci]
        p.op("dve", lambda e: e.tensor_tensor(out=it_t[:], in0=psK[:CH, CH:2 * CH], in1=e2_t[:], op=ALU.mult),
             reads=[psKb, e2_b], writes=[it_b])
        kd_t, kd_b = kdecb[par][h][ci]
        p.op("dve", lambda e: e.tensor_scalar(out=kd_t[:], in0=psT[:CH, :], scalar1=sc_t[:, 1:2], scalar2=None, op0=ALU.mult),
             reads=[psTb, sc_b], writes=[kd_b])
        vb_t, vb_b = vbeta[par][h][ci]
        p.op("dve", lambda e: e.tensor_scalar(out=vb_t[:], in0=psV[:CH, :], scalar1=beta_t[:, colb:colb + 1], scalar2=None, op0=ALU.mult),
             reads=[psVb, betab], writes=[vb_b])
        n_t, n_b = Pa
        p.op("dve", lambda e: e.tensor_tensor(out=n_t[:], in0=psK[:CH, 0:CH], in1=e1_t[:], op=ALU.mult), reads=[psKb, e1_b], writes=[n_b])
        yield
        p.op("dve", lambda e: e.scalar_tensor_tensor(out=n_t[:], in0=n_t[:], scalar=nbeta_t[:, colb:colb + 1], in1=stril[:, :],
                                                      op0=ALU.mult, op1=ALU.mult), reads=[n_b, nbetab, strilb], writes=[n_b])
        yield
        psC, psCb = qd.next()
        p.op("pe", lambda e: e.transpose(psC[:CH, 0:CH], n_t[:, :], ident[:CH, :CH]), reads=[n_b, identb], writes=[psCb])
        nt_t, nt_b = PTa
        p.op("dve", lambda e: e.tensor_copy(out=nt_t[:], in_=psC[:CH, 0:CH]), reads=[psCb], writes=[nt_b])
        tt_t, tt_b = TTa
        p.op("dve", lambda e: e.tensor_tensor(out=tt_t[:], in0=psC[:CH, 0:CH], in1=ident[:CH, :CH], op=ALU.add),
             reads=[psCb, identb], writes=[tt_b])
        yield
        P_cur, PT_cur, TT_cur = Pa, PTa, TTa
        P_alt, PT_alt, TT_alt = Pb_, PTb_, TTb_
        for k in range(1, 6):
            psD, psDb = qa.next()
            p.op("pe", lambda e: e.matmul(psD[:CH, 0:CH], lhsT=PT_cur[0][:, :], rhs=P_cur[0][:, :], start=True, stop=True),
                 reads=[PT_cur[1], P_cur[1]], writes=[psDb], inc=(k == 5))
            if k < 5:
                p.op("pe", lambda e: e.matmul(psD[:CH, CH:2 * CH], lhsT=P_cur[0][:, :], rhs=PT_cur[0][:, :], start=True, stop=True),
                     reads=[PT_cur[1], P_cur[1]], writes=[psDb])
            p.op("act", lambda e: e.copy(out=P_alt[0][:], in_=psD[:CH, 0:CH]), reads=[psDb], writes=[P_alt[1]])
            if k < 5:
                p.op("act", lambda e: e.copy(out=PT_alt[0][:], in_=psD[:CH, CH:2 * CH]), reads=[psDb], writes=[PT_alt[1]])
            yield
            psE, psEb = qd.next()
            p.op("pe", lambda e: e.matmul(psE[:CH, 0:CH], lhsT=P_alt[0][:, :], rhs=TT_cur[0][:, :], start=True, stop=True),
                 reads=[P_alt[1], TT_cur[1]], writes=[psEb])
            p.op("dve", lambda e: e.tensor_tensor(out=TT_alt[0][:], in0=psE[:CH, 0:CH], in1=TT_cur[0][:], op=ALU.add),
                 reads=[psEb, TT_cur[1]], writes=[TT_alt[1]])
            yield
            P_cur, P_alt = P_alt, P_cur
            PT_cur, PT_alt = PT_alt, PT_cur
            TT_cur, TT_alt = TT_alt, TT_cur
        ttb_t, ttb_b = TTb[par][h][ci]
        p.op("pool", lambda e: e.tensor_copy(out=ttb_t[:], in_=TT_cur[0][:]), reads=[TT_cur[1]], writes=[ttb_b])
        yield

    def scan_gen(ti, h):
        par, t0, hg = ti % 2, ti * TL, heads[h]
        S_t, S_b = St[h]
        Sb_t, Sb_b = Sb[h]
        rn_t, rn_b = t_rn[h]
        vn_t, vn_b = t_vn[h]
        oraw_t, oraw_b = t_oraw[h]
        o_t, o_b = t_o[h]
        o2_t, o2_b = t_o2[h]
        ss_t, ss_b = t_ss[h]
        oT_t, oT_b = oT[par][h]
        for ci in range(CPT):
            cs = slice(ci * CH, (ci + 1) * CH)
            sc_t, sc_b = scol[par][h][ci]
            vb_t, vb_b = vbeta[par][h][ci]
            ps1, ps1b = qd.next()
            p.op("pe", lambda e: e.matmul(ps1[:CH, 0:128], lhsT=kTb[par][h][0][:, cs], rhs=Sb_t[:, :], start=True, stop=True),
                 reads=[kTb[par][h][1], Sb_b], writes=[ps1b])
            p.op("dve", lambda e: e.scalar_tensor_tensor(out=rn_t[:], in0=ps1[:CH, 0:128], scalar=sc_t[:, 0:1], in1=vb_t[:],
                                                          op0=ALU.mult, op1=ALU.add), reads=[ps1b, sc_b, vb_b], writes=[rn_b])
            yield
            ps2, ps2b = qa.next()
            p.op("pe", lambda e: e.matmul(ps2[:CH, 0:128], lhsT=TTb[par][h][ci][0][:, :], rhs=rn_t[:, :], start=True, stop=True),
                 reads=[TTb[par][h][ci][1], rn_b], writes=[ps2b])
            p.op("act", lambda e: e.copy(out=vn_t[:], in_=ps2[:CH, 0:128]), reads=[ps2b], writes=[vn_b])
            yield
            pso, psob = qa.next()
            p.op("pe", lambda e: e.matmul(pso[:CH, 0:128], lhsT=qdecTb[par][h][ci][0][:, :], rhs=Sb_t[:, :], start=True, stop=False),
                 reads=[qdecTb[par][h][ci][1], Sb_b], writes=[psob], inc=False)
            p.op("pe", lambda e: e.matmul(pso[:CH, 0:128], lhsT=intraTb[par][h][ci][0][:, :], rhs=vn_t[:, :], start=False, stop=True),
                 reads=[intraTb[par][h][ci][1], vn_b], writes=[psob], inc=False)
            ps3, ps3b = qd.next()
            p.op("pe", lambda e: e.matmul(ps3[:, 0:128], lhsT=kdecb[par][h][ci][0][:, :], rhs=vn_t[:, :], start=True, stop=True),
                 reads=[kdecb[par][h][ci][1], vn_b], writes=[psob, ps3b])
            gt_t, gt_b = gtc[par][h][ci]
            p.op("dve", lambda e: e.scalar_tensor_tensor(out=S_t[:], in0=S_t[:], scalar=gt_t[:, 0:1], in1=ps3[:, 0:128],
                                                          op0=ALU.mult, op1=ALU.add), reads=[S_b, gt_b, ps3b], writes=[S_b])
            p.op("act", lambda e: e.copy(out=oraw_t[:], in_=pso[:CH, 0:128]), reads=[psob], writes=[oraw_b])
            yield
            p.op("pool", lambda e: e.tensor_copy(out=Sb_t[:], in_=S_t[:]), reads=[S_b], writes=[Sb_b])
            p.op("act", lambda e: e.activation(out=o_t[:], in_=oraw_t[:], func=AF.Square), reads=[oraw_b], writes=[o_b])
            yield
            p.op("dve", lambda e: e.reduce_sum(out=ss_t[:, 0:1], in_=o_t[:], axis=AX.X), reads=[o_b], writes=[ss_b])
            yield
            p.op("act", lambda e: e.activation(out=ss_t[:, 1:2], in_=ss_t[:, 0:1], func=AF.Sqrt, scale=1.0 / 128, bias=epsc[:CH, 0:1]),
                 reads=[ss_b, epsb], writes=[ss_b])
            yield
            p.op("dve", lambda e: e.reciprocal(out=ss_t[:, 2:3], in_=ss_t[:, 1:2]), reads=[ss_b], writes=[ss_b])
            yield
            p.op("dve", lambda e: e.scalar_tensor_tensor(out=o_t[:], in0=oraw_t[:], scalar=ss_t[:, 2:3], in1=gnw_t[:, :],
                                                          op0=ALU.mult, op1=ALU.mult), reads=[oraw_b, ss_b, gnwb, o_b], writes=[o_b])
            yield
            p.op("pool", lambda e: e.tensor_tensor(out=o2_t[:], in0=o_t[:], in1=zc[par][h][ci][0][:], op=ALU.mult),
                 reads=[o_b, zc[par][h][ci][1]], writes=[o2_b])
            yield
            ps4, ps4b = qa.next()
            p.op("pe", lambda e: e.transpose(ps4[:, 0:CH], o2_t[:, :], ident[:CH, :CH]), reads=[o2_b, identb], writes=[ps4b])
            p.op("act", lambda e: e.copy(out=oT_t[:, cs], in_=ps4[:, 0:CH]), reads=[ps4b], writes=[oT_b])
            if ci == CPT - 1:
                p.dma("sp", oaT[hg * 128:(hg + 1) * 128, t0:t0 + TL], oT_t[:, :], reads=[oT_b])
            yield

    def conv_all(ti):
        return [conv_gen(ti, h) for h in range(NH)]

    def chunk_all(ti):
        return [prep_gen(ti, h, ci) for ci in range(CPT) for h in range(NH)]

    run_gens([], [conv_all(0), chunk_all(0)])
    for ti in range(NT):
        scans = [scan_gen(ti, h) for h in range(NH)]
        if ti + 1 < NT:
            run_gens(scans, [conv_all(ti + 1), chunk_all(ti + 1)])
        else:
            run_gens(scans)


NEG_CAUSAL = -1.0e30
NEG_TAKEN = -2.0e30
BIS_RANGE = 4096.0
BIS_ITERS = 30


def t5_bucket_np(dist):
    n = np.maximum(dist, 0)
    max_exact = 16
    lr = np.log(np.maximum(n, 1).astype(np.float32) / np.float32(max_exact)) / np.float32(math.log(128 / max_exact))
    large = max_exact + (lr * np.float32(32 - max_exact)).astype(np.int32)
    large = np.minimum(large, 31)
    return np.where(n < max_exact, n, large)


def dsa_consts(rel_bias):
    c = {}
    sp = np.arange(128)[:, None]
    tq = np.arange(512)[None, :]
    nb = np.zeros((8, 5, 128, 512), np.float32)
    for r in range(-1, 4):
        dist = tq - (r * 128 + sp)
        b = t5_bucket_np(dist)
        for h in range(8):
            nb[h, r + 1] = np.where(dist >= 0, rel_bias[b, h], np.float32(-30000.0))
    c["nb"] = nb
    c["cbias"] = np.ascontiguousarray(np.broadcast_to(rel_bias[31][None, :], (128, 8))).astype(np.float32)
    cadd = np.zeros((4, 128, 512), np.float32)
    for qi in range(4):
        cadd[qi] = np.where(np.arange(512)[None, :] <= qi * 128 + np.arange(128)[:, None], 0.0, NEG_CAUSAL)
    c["cadd"] = cadd
    return c


def emit_dsa(p, S, scr, cst):
    NG = S // 512
    topk = min(256, S // 4)
    NR = topk // 8
    SCALE = 128 ** -0.5
    qiT, kiT, wi, qbT, kbT, vb, obT = scr["qiT"], scr["kiT"], scr["wi"], scr["qbT"], scr["kbT"], scr["vb"], scr["obT"]

    identf = p.sb([128, 128], F32); identfb = Buf()
    p.dma("sp", identf[:], cst["ident"][:, :], writes=[identfb])
    identb = p.sb([128, 128], BF16); identbb = Buf()
    p.op("pool", lambda e: e.tensor_copy(out=identb[:], in_=identf[:]), reads=[identfb], writes=[identbb])
    cadd_t = p.sb([128, 4, 512], F32); caddb = Buf()
    p.dma("sp", cadd_t[:], cst["cadd"].rearrange("q p s -> p q s"), writes=[caddb])
    cbias_t = p.sb([128, 8], F32); cbiasb = Buf()
    p.dma("sp", cbias_t[:], cst["cbias"][:, :], writes=[cbiasb])
    kiT2 = p.sb([128, S], BF16); kib = Buf()
    p.dma("sp", kiT2[0:64, :], kiT[:, :], writes=[kib])
    p.dma("sp", kiT2[64:128, :], kiT[:, :], writes=[kib])
    sc = p.sb([128, S], F32); scb = Buf()
    maskTs = [(p.sb([128, S // 128, 512], mybir.dt.uint8), Buf()) for _ in range(2)]

    lg = Ring(p, 3, [128, 512], F32, psum=True)
    ops = Ring(p, 4, [128, 512], F32, psum=True)
    ptf = Ring(p, 1, [128, 512], F32, psum=True)
    trb = ptf
    qi_r = Ring(p, 2, [128, 4, 128], BF16)
    wi_r = Ring(p, 2, [128, 8], F32)
    aw_r = Ring(p, 2, [128, 8], F32)
    sg_r = Ring(p, 2, [128, 8], F32)
    relu_r = Ring(p, 3, [128, 512], F32)
    lo_r = Ring(p, 2, [128, 4], F32)
    cn_r = Ring(p, 2, [128, BIS_ITERS * 16], F32)
    mk_r = Ring(p, 2, [128, 512], F32)
    qT_r = Ring(p, 2, [128, 512], BF16)
    kT_r = Ring(p, 3, [128, 512], BF16)
    v_r = Ring(p, 4, [128, 4, 129], BF16)
    for (v_t, v_b) in v_r.items:
        p.op("pool", lambda e: e.memset(v_t[:, :, 128:129], 1.0), writes=[v_b])
    nb_r = Ring(p, 5, [128, 512], F32)
    lgt_r = Ring(p, 2, [128, 512], F32)
    P_r = Ring(p, 3, [128, 512], BF16)
    Pm_r = Ring(p, 5, [128, 512], BF16)
    on_r = Ring(p, 2, [128, 128], F32)
    rc_r = Ring(p, 4, [128, 1], F32)
    obT_r = Ring(p, 2, [128, 512], BF16)

    def indexer_gen(g):
        maskT, maskTb = maskTs[g % 2]
        L = 512 * (g + 1)
        for qi in range(4):
            t0 = g * 512 + qi * 128
            q_t, q_b = qi_r.next()
            p.dma("sp", q_t[:], qiT[:, t0:t0 + 128].rearrange("(hp p) t -> p hp t", p=128), writes=[q_b])
            w_t, w_b = wi_r.next()
            p.dma("sp", w_t[:], wi[t0:t0 + 128, :], writes=[w_b])
            aw_t, aw_b = aw_r.next()
            sg_t, sg_b = sg_r.next()
            p.op("act", lambda e: e.activation(out=aw_t[:], in_=w_t[:], func=AF.Abs), reads=[w_b], writes=[aw_b])
            p.op("act", lambda e: e.activation(out=sg_t[:], in_=w_t[:], func=AF.Sign), reads=[w_b], writes=[sg_b])
            for j in range(g + 1):
                js = slice(j * 512, (j + 1) * 512)
                for h in range(8):
                    hp, off = h // 2, (h % 2) * 64
                    ps, psb = lg.next()
                    p.op("pe", lambda e: e.matmul(ps[:, :], lhsT=q_t[off:off + 64, hp, :], rhs=kiT2[off:off + 64, js], start=True, stop=True),
                         reads=[q_b, kib], writes=[psb])
                    r_t, r_b = relu_r.next()
                    p.op("act", lambda e: e.activation(out=r_t[:], in_=ps[:, :], func=AF.Relu, scale=aw_t[:, h:h + 1]),
                         reads=[psb, aw_b], writes=[r_b])
                    if h == 0:
                        p.op("dve", lambda e: e.tensor_scalar(out=sc[:, js], in0=r_t[:], scalar1=sg_t[:, 0:1], scalar2=None, op0=ALU.mult),
                             reads=[r_b, sg_b], writes=[scb])
                    else:
                        p.op("dve", lambda e: e.scalar_tensor_tensor(out=sc[:, js], in0=r_t[:], scalar=sg_t[:, h:h + 1], in1=sc[:, js],
                                                                      op0=ALU.mult, op1=ALU.add), reads=[r_b, sg_b, scb], writes=[scb])
                    yield
            ds = slice(g * 512, (g + 1) * 512)
            p.op("dve", lambda e: e.tensor_tensor(out=sc[:, ds], in0=sc[:, ds], in1=cadd_t[:, qi, :], op=ALU.add),
                 reads=[scb, caddb], writes=[scb])
            lo_t, lo_b = lo_r.next()
            cn_t, cn_b = cn_r.next()
            p.op("dve", lambda e: e.memset(lo_t[:, 0:1], -BIS_RANGE), writes=[lo_b])
            p.op("dve", lambda e: e.memset(cn_t[:], 0.0), writes=[cn_b])
            yield
            for r in range(BIS_ITERS):
                w_r = BIS_RANGE / (2.0 ** r)
                p.op("dve", lambda e: e.tensor_scalar(out=lo_t[:, 1:2], in0=lo_t[:, 0:1], scalar1=w_r, scalar2=None, op0=ALU.add),
                     reads=[lo_b], writes=[lo_b])
                j_t, j_b = relu_r.next()
                for c0 in range(0, L, 512):
                    p.op("dve", lambda e: e.tensor_scalar(out=j_t[:], in0=sc[:, c0:c0 + 512], scalar1=lo_t[:, 1:2], scalar2=0.0,
                                                          op0=ALU.is_ge, op1=ALU.add, accum_out=cn_t[:, r * 16 + c0 // 512:r * 16 + c0 // 512 + 1]),
                         reads=[scb, lo_b, cn_b], writes=[j_b, cn_b])
                nch = L // 512
                if nch > 1:
                    p.op("dve", lambda e: e.reduce_sum(out=cn_t[:, r * 16:r * 16 + 1], in_=cn_t[:, r * 16:r * 16 + nch], axis=AX.X),
                         reads=[cn_b], writes=[cn_b])
                p.op("dve", lambda e: e.tensor_single_scalar(out=lo_t[:, 2:3], in_=cn_t[:, r * 16:r * 16 + 1], scalar=topk - 0.5, op=ALU.is_ge),
                     reads=[cn_b, lo_b], writes=[lo_b])
                p.op("dve", lambda e: e.scalar_tensor_tensor(out=lo_t[:, 0:1], in0=lo_t[:, 2:3], scalar=w_r, in1=lo_t[:, 0:1],
                                                              op0=ALU.mult, op1=ALU.add), reads=[lo_b], writes=[lo_b])
                yield
            for j in range(g + 1):
                js = slice(j * 512, (j + 1) * 512)
                mk_t, mk_b = mk_r.next()
                p.op("dve", lambda e: e.tensor_scalar(out=mk_t[:], in0=sc[:, js], scalar1=lo_t[:, 0:1], scalar2=None, op0=ALU.is_ge),
                     reads=[scb, lo_b], writes=[mk_b])
                tp, tpb = trb.next()
                for i in range(4):
                    p.op("pe", lambda e: e.transpose(tp[:, i * 128:(i + 1) * 128], mk_t[:, i * 128:(i + 1) * 128], identf[:, :]),
                         reads=[mk_b, identfb], writes=[tpb], inc=(i == 3))
                p.op("act", lambda e: e.copy(out=maskT[:, 4 * j:4 * j + 4, qi * 128:(qi + 1) * 128],
                                             in_=tp[:, :].rearrange("p (a b) -> p a b", b=128)),
                     reads=[tpb], writes=[maskTb])
                yield
    def attention_gen(g):
        maskT, maskTb = maskTs[g % 2]
        for h in range(8):
            qT_t, qT_b = qT_r.next()
            p.dma("sp", qT_t[:], qbT[h * 128:(h + 1) * 128, g * 512:(g + 1) * 512], writes=[qT_b])
            nbt = {}
            for r in range(-1 if g > 0 else 0, 4):
                n_t, n_b = nb_r.next()
                p.dma("sp", n_t[:], cst["nb"][h, r + 1, :, :], writes=[n_b])
                nbt[r] = (n_t, n_b)
            obank = [ops.next() for _ in range(4)]
            oacc = [(obank[qi][0][:, 0:129], obank[qi][1]) for qi in range(4)]
            steps = [(j, kbi) for j in range(g + 1) for kbi in range(4)]
            SKEW = 2
            live = {}
            kv = {}
            for i in range(len(steps) + SKEW):
                if i < len(steps):
                    j, kbi = steps[i]
                    if kbi == 0:
                        kT_t, kT_b = kT_r.next()
                        p.dma("sp", kT_t[:], kbT[h * 128:(h + 1) * 128, j * 512:(j + 1) * 512], writes=[kT_b])
                        v_t, v_b = v_r.next()
                        p.dma("sp", v_t[:, :, 0:128], vb[j * 512:(j + 1) * 512, h * 128:(h + 1) * 128].rearrange("(kb p) e -> p kb e", p=128),
                              writes=[v_b])
                        kv[j] = (kT_t, kT_b, v_t, v_b)
                    kT_t, kT_b, v_t, v_b = kv[j]
                    kb = 4 * j + kbi
                    r = kb - 4 * g
                    ps, psb = lg.next()
                    p.op("pe", lambda e: e.matmul(ps[:, :], lhsT=kT_t[:, kbi * 128:(kbi + 1) * 128], rhs=qT_t[:, :], start=True, stop=True),
                         reads=[kT_b, qT_b], writes=[psb])
                    P_t, P_b = P_r.next()
                    if r >= -1:
                        n_t, n_b = nbt[r]
                        l_t, l_b = lgt_r.next()
                        p.op("dve", lambda e: e.scalar_tensor_tensor(out=l_t[:], in0=ps[:, :], scalar=SCALE, in1=n_t[:],
                                                                      op0=ALU.mult, op1=ALU.add), reads=[psb, n_b], writes=[l_b])
                        p.op("act", lambda e: e.activation(out=P_t[:], in_=l_t[:], func=AF.Exp), reads=[l_b], writes=[P_b])
                    else:
                        p.op("act", lambda e: e.activation(out=P_t[:], in_=ps[:, :], func=AF.Exp, scale=SCALE, bias=cbias_t[:, h:h + 1]),
                             reads=[psb, cbiasb], writes=[P_b])
                    Pm_t, Pm_b = Pm_r.next()
                    p.op("pool", lambda e: e.tensor_tensor(out=Pm_t[:], in0=P_t[:], in1=maskT[:, kb, :], op=ALU.mult),
                         reads=[P_b, maskTb], writes=[Pm_b])
                    live[i] = (Pm_t, Pm_b, v_t, v_b, kb, kbi)
                if i - SKEW >= 0:
                    Pm_t, Pm_b, v_t, v_b, kb, kbi = live.pop(i - SKEW)
                    qis = [qi for qi in range(4) if kb <= 4 * g + qi]
                    for qi in qis:
                        o_t, o_b = oacc[qi]
                        p.op("pe", lambda e: e.matmul(o_t, lhsT=Pm_t[:, qi * 128:(qi + 1) * 128], rhs=v_t[:, kbi, :],
                                                      start=(kb == 0), stop=(kb == 4 * g + qi)),
                             reads=[Pm_b, v_b], writes=[o_b], inc=(qi == qis[-1]))
                yield
            ob_t, ob_b = obT_r.next()
            for qi in range(4):
                o_t, o_b = oacc[qi]
                rc_t, rc_b = rc_r.next()
                p.op("dve", lambda e: e.reciprocal(out=rc_t[:], in_=o_t[:, 128:129]), reads=[o_b], writes=[rc_b])
                on_t, on_b = on_r.next()
                p.op("dve", lambda e: e.tensor_scalar(out=on_t[:], in0=o_t[:, 0:128], scalar1=rc_t[:, 0:1], scalar2=None, op0=ALU.mult),
                     reads=[o_b, rc_b], writes=[on_b])
                pt, ptb = ptf.next()
                p.op("pe", lambda e: e.transpose(pt[:, 0:128], on_t[:, :], identf[:, :]), reads=[on_b, identfb], writes=[ptb])
                p.op("act", lambda e: e.copy(out=ob_t[:, qi * 128:(qi + 1) * 128], in_=pt[:, 0:128]), reads=[ptb], writes=[ob_b])
                yield
            p.dma("sp", obT[h * 128:(h + 1) * 128, g * 512:(g + 1) * 512], ob_t[:, :], reads=[ob_b])

    run_gens([indexer_gen(0)])
    for g in range(NG):
        if g + 1 < NG:
            run_gens([attention_gen(g), indexer_gen(g + 1)])
        else:
            run_gens([attention_gen(g)])


def emit_ffn(p, S, scr, wts, lnp, xres, xout, xTb_next):
    oaT, obT, sgT = scr["oaT"], scr["obT"], scr["sgT"]
    NJ = D_FF // 128
    big = p.sb([128, NJ * 512], BF16)
    slot = [Buf() for _ in range(NJ)]
    aT = lambda j: big[:, j * 512:(j + 1) * 512]
    x1 = [(p.sb([128, 2048], F32), Buf()) for _ in range(4)]
    x1T = [(p.sb([128, 512], BF16), Buf()) for _ in range(16)]
    WS = 8192
    wring = Ring(p, 4, [128, WS], BF16)
    sg_r = Ring(p, 4, [128, 512], F32)
    m_r = Ring(p, 4, [128, 512], F32)
    si_r = Ring(p, 2, [128, 512], F32)
    ln_r = Ring(p, 2, [128, 2048], F32)
    st_r = Ring(p, 2, [128, 4, 6], F32)
    mv_r = Ring(p, 4, [128, 4], F32)
    epsc = p.sb([128, 1], F32); epsb = Buf()
    p.op("pool", lambda e: e.memset(epsc[:], LN_EPS), writes=[epsb])
    identf = p.sb([128, 128], F32); identfb = Buf()
    p.dma("sp", identf[:], scr["ident"][:, :], writes=[identfb])
    pr = Ring(p, 7, [128, 512], F32, psum=True)
    ptr = Ring(p, 1, [128, 512], F32, psum=True)

    def layer_norm(tt, gname, bname):
        x_t, x_b = x1[tt]
        st_t, st_b = st_r.next()
        for c in range(4):
            p.op("dve", lambda e: e.bn_stats(out=st_t[:, c, :], in_=x_t[:, c * 512:(c + 1) * 512]), reads=[x_b], writes=[st_b])
        mv_t, mv_b = mv_r.next()
        p.op("dve", lambda e: e.bn_aggr(out=mv_t[:, 0:2], in_=st_t[:].rearrange("p a b -> p (a b)")), reads=[st_b], writes=[mv_b])
        p.op("act", lambda e: e.activation(out=mv_t[:, 2:3], in_=mv_t[:, 1:2], func=AF.Sqrt, bias=epsc[:, 0:1]),
             reads=[mv_b, epsb], writes=[mv_b])
        p.op("dve", lambda e: e.reciprocal(out=mv_t[:, 2:3], in_=mv_t[:, 2:3]), reads=[mv_b], writes=[mv_b])
        p.op("dve", lambda e: e.scalar_tensor_tensor(out=mv_t[:, 3:4], in0=mv_t[:, 0:1], scalar=-1.0, in1=mv_t[:, 2:3],
                                                      op0=ALU.mult, op1=ALU.mult), reads=[mv_b], writes=[mv_b])
        p.op("act", lambda e: e.activation(out=x_t[:], in_=x_t[:], func=AF.Identity, scale=mv_t[:, 2:3], bias=mv_t[:, 3:4]),
             reads=[x_b, mv_b], writes=[x_b])
        g_t, g_b = ln_r.next()
        p.dma("sp", g_t[:], lnp[gname][:, :], writes=[g_b])
        p.op("pool", lambda e: e.tensor_tensor(out=x_t[:], in0=x_t[:], in1=g_t[:], op=ALU.mult), reads=[x_b, g_b], writes=[x_b])
        b_t, b_b = ln_r.next()
        p.dma("sp", b_t[:], lnp[bname][:, :], writes=[b_b])
        p.op("pool", lambda e: e.tensor_tensor(out=x_t[:], in0=x_t[:], in1=b_t[:], op=ALU.add), reads=[x_b, b_b], writes=[x_b])

    def transposes(tt, dst_tiles):
        x_t, x_b = x1[tt]
        for k4 in range(4):
            pt, ptb = ptr.next()
            for i in range(4):
                k = k4 * 4 + i
                p.op("pe", lambda e: e.transpose(pt[:, i * 128:(i + 1) * 128], x_t[:, k * 128:(k + 1) * 128], identf[:, :]),
                     reads=[x_b, identfb], writes=[ptb], inc=(i == 3))
            for i in range(4):
                k = k4 * 4 + i
                d_t, d_b = dst_tiles[k]
                p.op("act", lambda e: e.copy(out=d_t[:, tt * 128:(tt + 1) * 128], in_=pt[:, i * 128:(i + 1) * 128]),
                     reads=[ptb], writes=[d_b])

    for T in range(S // 512):
        ts = slice(T * 512, (T + 1) * 512)
        oa_v = big[:, 16 * 512:24 * 512].rearrange("p (k t) -> p k t", t=512)
        ob_v = big[:, 24 * 512:32 * 512].rearrange("p (k t) -> p k t", t=512)
        p.dma("sp", oa_v, oaT[:, ts].rearrange("(k p) t -> p k t", p=128), writes=slot[16:24])
        p.dma("sp", ob_v, obT[:, ts].rearrange("(k p) t -> p k t", p=128), writes=slot[24:32])
        for tt in range(4):
            p.dma("sp", x1[tt][0][:], xres[T * 512 + tt * 128:T * 512 + (tt + 1) * 128, :], writes=[x1[tt][1]])
        for fg in range(4):
            w_t, w_b = wring.next()
            wa_v = w_t[:, 0:4096].rearrange("p (k f) -> p k f", f=512)
            wb_v = w_t[:, 4096:8192].rearrange("p (k f) -> p k f", f=512)
            p.dma("sp", wa_v, wts["wa"][:, fg * 512:(fg + 1) * 512].rearrange("(k p) f -> p k f", p=128), writes=[w_b])
            p.dma("sp", wb_v, wts["wb"][:, fg * 512:(fg + 1) * 512].rearrange("(k p) f -> p k f", p=128), writes=[w_b])
            for fi in range(4):
                ft = fg * 4 + fi
                fs = slice(fi * 128, (fi + 1) * 128)
                psA, psAb = pr.next()
                for k in range(8):
                    p.op("pe", lambda e: e.matmul(psA[:, :], lhsT=wa_v[:, k, fs], rhs=oa_v[:, k, :], start=(k == 0), stop=(k == 7)),
                         reads=[w_b] + slot[16:24], writes=[psAb], inc=(k == 7))
                psB, psBb = pr.next()
                for k in range(8):
                    p.op("pe", lambda e: e.matmul(psB[:, :], lhsT=wb_v[:, k, fs], rhs=ob_v[:, k, :], start=(k == 0), stop=(k == 7)),
                         reads=[w_b] + slot[24:32], writes=[psBb], inc=(k == 7))
                ga_t, ga_b = sg_r.next()
                p.dma("sp", ga_t[:], sgT[ft * 128:(ft + 1) * 128, ts], writes=[ga_b])
                gb_t, gb_b = sg_r.next()
                p.dma("sp", gb_t[:], sgT[2048 + ft * 128:2048 + (ft + 1) * 128, ts], writes=[gb_b])
                m1_t, m1_b = m_r.next()
                p.op("dve", lambda e: e.tensor_tensor(out=m1_t[:], in0=psA[:, :], in1=ga_t[:], op=ALU.mult), reads=[psAb, ga_b], writes=[m1_b])
                m2_t, m2_b = m_r.next()
                p.op("dve", lambda e: e.tensor_tensor(out=m2_t[:], in0=psB[:, :], in1=gb_t[:], op=ALU.mult), reads=[psBb, gb_b], writes=[m2_b])
                p.op("pool", lambda e: e.tensor_tensor(out=aT(ft), in0=m1_t[:], in1=m2_t[:], op=ALU.add), reads=[m1_b, m2_b], writes=[slot[ft]])
        for nt in range(4):
            ns = slice(nt * 512, (nt + 1) * 512)
            w_t, w_b = wring.next()
            wo_v = w_t[:, :].rearrange("p (k n) -> p k n", n=512)
            p.dma("sp", wo_v, wts["wout"][:, ns].rearrange("(k p) n -> p k n", p=128), writes=[w_b])
            for tt in range(4):
                ps, psb = pr.next()
                for k in range(16):
                    p.op("pe", lambda e: e.matmul(ps[:, :], lhsT=aT(k)[:, tt * 128:(tt + 1) * 128], rhs=wo_v[:, k, :],
                                                  start=(k == 0), stop=(k == 15)), reads=[w_b, slot[k]], writes=[psb], inc=(k == 15))
                x_t, x_b = x1[tt]
                p.op("dve", lambda e: e.scalar_tensor_tensor(out=x_t[:, ns], in0=x_t[:, ns], scalar=ALPHA, in1=ps[:, :],
                                                              op0=ALU.mult, op1=ALU.add), reads=[x_b, psb], writes=[x_b])
        for tt in range(4):
            layer_norm(tt, "g1", "b1")
            transposes(tt, x1T)
        for cg in range(NJ // 4):
            wg_t, wg_b = wring.next()
            wg_v = wg_t[:, :].rearrange("p (k n) -> p k n", n=512)
            p.dma("sp", wg_v, wts["wfi"][:, cg * 512:(cg + 1) * 512].rearrange("(k p) n -> p k n", p=128), writes=[wg_b])
            wu_t, wu_b = wring.next()
            wu_v = wu_t[:, :].rearrange("p (k n) -> p k n", n=512)
            p.dma("sp", wu_v, wts["wfi"][:, D_FF + cg * 512:D_FF + (cg + 1) * 512].rearrange("(k p) n -> p k n", p=128), writes=[wu_b])
            for ci in range(4):
                jt = cg * 4 + ci
                cs = slice(ci * 128, (ci + 1) * 128)
                psG, psGb = pr.next()
                for k in range(16):
                    p.op("pe", lambda e: e.matmul(psG[:, :], lhsT=wg_v[:, k, cs], rhs=x1T[k][0][:, :], start=(k == 0), stop=(k == 15)),
                         reads=[wg_b, x1T[k][1]], writes=[psGb], inc=(k == 15))
                psU, psUb = pr.next()
                for k in range(16):
                    p.op("pe", lambda e: e.matmul(psU[:, :], lhsT=wu_v[:, k, cs], rhs=x1T[k][0][:, :], start=(k == 0), stop=(k == 15)),
                         reads=[wu_b, x1T[k][1]], writes=[psUb], inc=(k == 15))
                s_t, s_b = si_r.next()
                p.op("act", lambda e: e.activation(out=s_t[:], in_=psG[:, :], func=AF.Silu), reads=[psGb], writes=[s_b])
                p.op("dve", lambda e: e.tensor_tensor(out=aT(jt), in0=psU[:, :], in1=s_t[:], op=ALU.mult), reads=[psUb, s_b], writes=[slot[jt]])
        for nt in range(4):
            ns = slice(nt * 512, (nt + 1) * 512)
            acc = [pr.next() for _ in range(4)]
            for qd in range(4):
                w_t, w_b = wring.next()
                wf_v = w_t[:, 0:11 * 512].rearrange("p (k n) -> p k n", n=512)
                p.dma("sp", wf_v, wts["wfo"][qd * 11 * 128:(qd + 1) * 11 * 128, ns].rearrange("(k p) n -> p k n", p=128), writes=[w_b])
                for tt in range(4):
                    ps, psb = acc[tt]
                    for k in range(11):
                        j = qd * 11 + k
                        p.op("pe", lambda e: e.matmul(ps[:, :], lhsT=aT(j)[:, tt * 128:(tt + 1) * 128], rhs=wf_v[:, k, :],
                                                      start=(j == 0), stop=(j == NJ - 1)), reads=[w_b, slot[j]], writes=[psb],
                             inc=(k == 10))
            for tt in range(4):
                ps, psb = acc[tt]
                x_t, x_b = x1[tt]
                p.op("dve", lambda e: e.scalar_tensor_tensor(out=x_t[:, ns], in0=x_t[:, ns], scalar=ALPHA, in1=ps[:, :],
                                                              op0=ALU.mult, op1=ALU.add), reads=[x_b, psb], writes=[x_b])
        for tt in range(4):
            layer_norm(tt, "g2", "b2")
            p.dma("sp", xout[T * 512 + tt * 128:T * 512 + (tt + 1) * 128, :], x1[tt][0][:], reads=[x1[tt][1]])
            if xTb_next is not None:
                transposes(tt, x1T)
        if xTb_next is not None:
            for k in range(16):
                p.dma("sp", xTb_next[k * 128:(k + 1) * 128, ts], x1T[k][0][:, :], reads=[x1T[k][1]])


W_SHAPES = {"w_in": (D_MODEL, D_IN), "w_branch_a": (1024, D_MODEL), "w_branch_b": (1024, D_MODEL),
            "w_out": (D_MODEL, D_MODEL), "w_ffn_in": (D_MODEL, 2 * D_FF), "w_ffn_out": (D_FF, D_MODEL)}


def const_arrays(inputs, S, depth):
    NCH = S // CH
    c = dict(gdn_consts())
    c.update(dsa_consts(np.asarray(inputs["rel_bias"], np.float32)))
    c["gnw"] = np.ascontiguousarray(np.broadcast_to(np.asarray(inputs["gdn_norm_w"])[:depth, None, :], (depth, CH, 128))).astype(np.float32)
    cw = np.asarray(inputs["conv_w"])[:depth]
    c["cw"] = np.ascontiguousarray(cw.reshape(depth, 4, 3, 8, 128).transpose(0, 4, 2, 3, 1).reshape(depth, 128, 96)).astype(np.float32)
    dtb = np.zeros((depth, CH, NCH, 16), np.float32)
    negA = np.zeros((depth, CH, NCH, 16), np.float32)
    dtb[..., 0:8] = np.asarray(inputs["dt_bias"])[:depth, None, None, :]
    a_log = np.asarray(inputs["a_log"])[:depth]
    negA[..., 0:8] = a_log[:, None, None, :]
    c["dtb16"] = dtb.reshape(depth, CH, NCH * 16)
    c["alog16"] = negA.reshape(depth, CH, NCH * 16)
    for nm in ("ln1_g", "ln1_b", "ln2_g", "ln2_b"):
        c[nm] = np.ascontiguousarray(np.broadcast_to(np.asarray(inputs[nm])[:depth, None, :], (depth, 128, D_MODEL))).astype(np.float32)
    return c


def build_all(S, depth, cshapes, phases=("cast", "proj", "gdn", "dsa", "ffn")):
    p = Prog()
    x = p.dram("x", [S, D_MODEL], F32, "ExternalInput")
    xT = p.dram("xT", [D_MODEL, S], F32, "ExternalInput")
    wext = {nm: p.dram(nm, [depth] + list(sh), F32, "ExternalInput") for nm, sh in W_SHAPES.items()} if "cast" in phases else {}
    cst = {k: p.dram("c_" + k, list(sh), F32, "ExternalInput") for k, sh in cshapes.items()}
    out = p.dram("out", [S, D_MODEL], F32, "ExternalOutput")
    scr = {k: p.scratch("s_" + k, fn(S), dt) for k, (fn, dt) in SCR_SPEC.items()}
    scr["ident"] = cst["ident"]
    wb16 = {nm: [p.scratch("%s_b%d" % (nm, l), list(sh), BF16) for l in range(depth)] for nm, sh in W_SHAPES.items()}
    with p.phase():
        rings = (Ring(p, 3, [128, CAST_CH], F32), Ring(p, 3, [128, CAST_CH], BF16))
        k = emit_cast(p, xT, scr["xTb"], D_MODEL, S, rings)
        for l in range(depth if "cast" in phases else 0):
            for nm, sh in W_SHAPES.items():
                k = emit_cast(p, wext[nm][l], wb16[nm][l], sh[0], sh[1], rings, k)
    negA = p.scratch("s_negA", list(cshapes["alog16"]), F32)
    with p.phase():
        for l in range(depth):
            W = cshapes["alog16"][2]
            t = p.sb([CH, W], F32); tb = Buf()
            p.dma("sp", t[:], cst["alog16"][l, :, :], writes=[tb])
            p.op("act", lambda e: e.activation(out=t[:], in_=t[:], func=AF.Exp), reads=[tb], writes=[tb])
            p.op("dve", lambda e: e.tensor_scalar(out=t[:], in0=t[:], scalar1=-1.0, scalar2=None, op0=ALU.mult), reads=[tb], writes=[tb])
            p.dma("sp", negA[l, :, :], t[:], reads=[tb])
    cst["negA16"] = negA
    gbt = p.scratch("s_gbt", [3, CH, cshapes["alog16"][2]], F32)
    xbuf = [scr["xA"], scr["xB"]]
    for l in range(depth):
        last = (l == depth - 1)
        if "proj" in phases:
            with p.phase():
                emit_proj(p, S, scr["xTb"], wb16["w_in"][l], scr)
        if "gdn" in phases:
            with p.phase():
                emit_gdn_pre(p, S, scr, cst, l, gbt)
            for heads in ([0, 1, 2, 3], [4, 5, 6, 7]):
                with p.phase():
                    emit_gdn(p, S, scr, cst, l, heads, gbt)
        if "dsa" in phases:
            with p.phase():
                emit_dsa(p, S, scr, cst)
        if "ffn" not in phases:
            continue
        with p.phase():
            wts = {"wa": wb16["w_branch_a"][l], "wb": wb16["w_branch_b"][l], "wout": wb16["w_out"][l],
                   "wfi": wb16["w_ffn_in"][l], "wfo": wb16["w_ffn_out"][l]}
            lnp = {"g1": cst["ln1_g"][l], "b1": cst["ln1_b"][l], "g2": cst["ln2_g"][l], "b2": cst["ln2_b"][l]}
            emit_ffn(p, S, scr, wts, lnp, x if l == 0 else xbuf[(l - 1) % 2], out if last else xbuf[l % 2],
                     None if last else scr["xTb"])
    p.close()
    return p


def make_in_map(inputs, b, S, depth, cst):
    xb = np.ascontiguousarray(np.asarray(inputs["x"])[b, :S])
    m = {"x": xb, "xT": np.ascontiguousarray(xb.T)}
    for nm in W_SHAPES:
        m[nm] = np.ascontiguousarray(np.asarray(inputs[nm])[:depth])
    for k, v in cst.items():
        m["c_" + k] = v
    return m


def kernel(**inputs):
    S, depth = SEQ, DEPTH
    cst = const_arrays(inputs, S, depth)
    p = build_all(S, depth, {k: v.shape for k, v in cst.items()})
    in_maps = [make_in_map(inputs, b, S, depth, cst) for b in range(BATCH)]
    res = run(p, in_maps).results
    return np.stack([np.asarray(res[b]["out"], np.float32) for b in range(BATCH)], 0)
```

```python
import contextlib
import math
import numpy as np
import ml_dtypes
import concourse.bass as bass
import concourse.mybir as mybir
from concourse.bass_utils import run_bass_kernel_spmd

F32 = mybir.dt.float32
BF16 = mybir.dt.bfloat16
AF = mybir.ActivationFunctionType
ALU = mybir.AluOpType
AX = mybir.AxisListType
NPBF = ml_dtypes.bfloat16

D_MODEL = 2048
BATCH = 4
SEQ = 8192
DEPTH = 4
NCORE = 4
D_FF = 5632
D_IN = 11864
ALPHA = (2 * DEPTH) ** 0.25
LN_EPS = 1e-5
RMS_EPS = 1e-6


class Buf:
    __slots__ = ("w", "r", "name")

    def __init__(self, name=""):
        self.w = None
        self.r = {}
        self.name = name


class Prog:
    ND = 24

    def __init__(self):
        self.nc = bass.Bass("TRN2", target_bir_lowering=False)
        nc = self.nc
        self.es = contextlib.ExitStack()
        self.eng = {"pe": nc.tensor, "act": nc.scalar, "dve": nc.vector, "pool": nc.gpsimd, "sp": nc.sync}
        self.sem = {e: self.es.enter_context(nc.semaphore("s_" + e)) for e in self.eng}
        self.dsem = [self.es.enter_context(nc.semaphore("d%d" % i)) for i in range(self.ND)]
        self.cnt = {e: 0 for e in self.eng}
        self.pending = {e: False for e in self.eng}
        self.known = {e: {} for e in self.eng}
        self.ndma = 0
        self.nins = 0
        self._names = 0
        self.pes = None

    def dram(self, name, shape, dt, kind):
        return self.nc.dram_tensor(name, list(shape), dt, kind=kind).ap()

    def sb(self, shape, dt, name=None):
        self._names += 1
        st = self.pes if self.pes is not None else self.es
        return st.enter_context(self.nc.sbuf_tensor(name or ("sb%d" % self._names), list(shape), dt))

    def ps(self, shape, dt=F32, name=None):
        self._names += 1
        st = self.pes if self.pes is not None else self.es
        return st.enter_context(self.nc.psum_tensor(name or ("ps%d" % self._names), list(shape), dt))

    def scratch(self, name, shape, dt):
        return self.nc.dram_tensor(name, list(shape), dt, kind="Internal").ap()

    def barrier(self):
        for e in self.pending:
            assert not self.pending[e], "engine %s has un-inc'd instruction at barrier" % e
        deps = [("e", f, self.cnt[f]) for f in self.eng if self.cnt[f] > 0]
        for i in range(min(self.ndma, self.ND)):
            last = self.ndma - 1 - ((self.ndma - 1 - i) % self.ND)
            deps.append(("d", i, 16 * (last // self.ND + 1)))
        for e in self.eng:
            self._wait(e, [d for d in deps if not (d[0] == "e" and d[1] == e)])

    @contextlib.contextmanager
    def phase(self):
        assert self.pes is None
        self.pes = contextlib.ExitStack()
        try:
            yield
            self.barrier()
        finally:
            self.pes.close()
            self.pes = None

    def _deps(self, reads, writes):
        deps = []
        for b in reads:
            if b.w is not None:
                deps.append(b.w)
        for b in writes:
            if b.w is not None:
                deps.append(b.w)
            deps.extend(b.r.values())
        return deps

    def _wait(self, e, deps):
        kn = self.known[e]
        need = {}
        for (kind, key, val) in deps:
            if kind == "e" and key == e and e == "pe":
                continue
            k = (kind, key)
            if kn.get(k, 0) >= val:
                continue
            if need.get(k, 0) < val:
                need[k] = val
        for (kind, key), val in need.items():
            if kind == "e" and key == e:
                assert val <= self.cnt[e], "self-wait on pending (un-inc'd) instruction"
            s = self.sem[key] if kind == "e" else self.dsem[key]
            self.eng[e].wait_ge(s, val)
            kn[(kind, key)] = val
            self.nins += 1

    def _mark(self, tok, reads, writes):
        k = (tok[0], tok[1])
        for b in reads:
            b.r[k] = tok
        for b in writes:
            b.w = tok
            b.r = {}

    def op(self, e, fn, reads=(), writes=(), inc=True):
        self._wait(e, self._deps(reads, writes))
        ins = fn(self.eng[e])
        if inc:
            self.cnt[e] += 1
            ins.then_inc(self.sem[e], 1)
            tok = ("e", e, self.cnt[e])
            self.pending[e] = False
        else:
            tok = ("e", e, self.cnt[e] + 1)
            self.pending[e] = True
        self.nins += 1
        self._mark(tok, reads, writes)
        return ins

    def dma(self, q, out, in_, reads=(), writes=(), sink=None, **kw):
        i = self.ndma
        self.ndma += 1
        s = i % self.ND
        val = 16 * (i // self.ND + 1)
        deps = self._deps(reads, writes)
        if val > 16:
            deps.append(("d", s, val - 16))
        self._wait(q, deps)
        ins = self.eng[q].dma_start(out=out, in_=in_, **kw)
        ins.then_inc(self.dsem[s], 16)
        self.nins += 1
        self._mark(("d", s, val), reads, writes)
        if sink is not None:
            sink.append(("d", s, val))
        return ins

    def finish(self, sinks):
        deps = []
        for b in sinks:
            deps.extend(b)
        self._wait("sp", deps)
        for e in self.pending:
            assert not self.pending[e], "engine %s ends with un-inc'd instruction" % e

    def close(self):
        self.es.close()


class Ring:
    def __init__(self, p, n, shape, dt, psum=False):
        self.items = []
        for _ in range(n):
            t = p.ps(shape, dt) if psum else p.sb(shape, dt)
            self.items.append((t, Buf()))
        self.i = 0

    def next(self):
        it = self.items[self.i % len(self.items)]
        self.i += 1
        return it


def run(prog, in_maps):
    return run_bass_kernel_spmd(prog.nc, in_maps, core_ids=list(range(len(in_maps))))


CAST_CH = 4096


def emit_cast(p, src2d, dst2d, rows, cols, rings, k0=0):
    st, ob = rings
    sv = src2d.rearrange("(p a) c -> p (a c)", p=128)
    dv = dst2d.rearrange("(p a) c -> p (a c)", p=128)
    m = rows // 128 * cols
    k = k0
    for c0 in range(0, m, CAST_CH):
        n = min(CAST_CH, m - c0)
        s_t, s_b = st.next()
        o_t, o_b = ob.next()
        p.dma("sp", s_t[:, :n], sv[:, c0:c0 + n], writes=[s_b])
        e = ("dve", "act", "pool")[k % 3]
        k += 1
        if e == "act":
            p.op(e, lambda en: en.copy(out=o_t[:, :n], in_=s_t[:, :n]), reads=[s_b], writes=[o_b])
        else:
            p.op(e, lambda en: en.tensor_copy(out=o_t[:, :n], in_=s_t[:, :n]), reads=[s_b], writes=[o_b])
        p.dma("sp", dv[:, c0:c0 + n], o_t[:, :n], reads=[o_b])
    return k


C_QKV, C_A, C_Z, C_QB, C_KB, C_VB, C_QI, C_KI, C_WI, C_GA = 0, 3072, 3088, 4112, 5136, 6160, 7184, 7696, 7760, 7768


def proj_groups():
    g = []
    for i in range(6):
        g.append(("F", C_QKV + 512 * i, 512, "qkvT", 512 * i))
    g.append(("T", C_A, 16, "ab", 0))
    for i in range(2):
        g.append(("T", C_Z + 512 * i, 512, "z", 512 * i))
    for i in range(2):
        g.append(("F", C_QB + 512 * i, 512, "qbT", 512 * i))
    for i in range(2):
        g.append(("F", C_KB + 512 * i, 512, "kbT", 512 * i))
    for i in range(2):
        g.append(("T", C_VB + 512 * i, 512, "vb", 512 * i))
    g.append(("F", C_QI, 512, "qiT", 0))
    g.append(("F", C_KI, 64, "kiT", 0))
    g.append(("T", C_WI, 8, "wi", 0))
    for i in range(8):
        g.append(("F", C_GA + 512 * i, 512, "sgT", 512 * i))
    return g


SCR_SPEC = {
    "qkvT": (lambda S: [3072, S], F32), "qbT": (lambda S: [1024, S], BF16), "kbT": (lambda S: [1024, S], BF16),
    "qiT": (lambda S: [512, S], BF16), "kiT": (lambda S: [64, S], BF16), "sgT": (lambda S: [4096, S], F32),
    "ab": (lambda S: [S, 16], F32), "z": (lambda S: [S, 1024], F32), "vb": (lambda S: [S, 1024], BF16),
    "wi": (lambda S: [S, 8], F32), "oaT": (lambda S: [1024, S], BF16), "obT": (lambda S: [1024, S], BF16),
    "x1": (lambda S: [S, 2048], F32), "xA": (lambda S: [S, 2048], F32), "xB": (lambda S: [S, 2048], F32),
    "xTb": (lambda S: [2048, S], BF16),
}


def emit_proj(p, S, xTb, w, scr):
    TG = min(2048, S)
    KC = D_MODEL // 128
    wv = w.rearrange("(kc p) n -> p kc n", p=128)
    xb = [(p.sb([128, TG], BF16), Buf()) for _ in range(KC)]
    wr = Ring(p, 2, [128, KC, 512], BF16)
    pr = Ring(p, 6, [128, 512], F32, psum=True)
    sf = Ring(p, 4, [128, 512], F32)
    sh = Ring(p, 4, [128, 512], BF16)
    ev = 0
    for tg in range(S // TG):
        t0 = tg * TG
        for kc in range(KC):
            xt, xbuf = xb[kc]
            p.dma("sp", xt[:], xTb[kc * 128:(kc + 1) * 128, t0:t0 + TG], writes=[xbuf])
        for (mode, c0, n, oname, r0) in proj_groups():
            w_t, w_b = wr.next()
            p.dma("sp", w_t[:, :, :n], wv[:, :, c0:c0 + n], writes=[w_b])
            o_ap = scr[oname]
            o_dt = SCR_SPEC[oname][1]
            stg = sf if o_dt == F32 else sh
            if mode == "F":
                for ci in range(0, n, 128):
                    cn = min(128, n - ci)
                    for tt in range(TG // 512):
                        ps_t, ps_b = pr.next()
                        for kc in range(KC):
                            xt, xbuf = xb[kc]
                            p.op("pe", lambda en: en.matmul(ps_t[:cn, :], lhsT=w_t[:, kc, ci:ci + cn],
                                                            rhs=xt[:, tt * 512:(tt + 1) * 512],
                                                            start=(kc == 0), stop=(kc == KC - 1)),
                                 reads=[w_b, xbuf], writes=[ps_b], inc=(kc == KC - 1))
                        g_t, g_b = stg.next()
                        if oname == "sgT":
                            p.op("act", lambda en: en.activation(out=g_t[:cn, :], in_=ps_t[:cn, :], func=AF.Sigmoid),
                                 reads=[ps_b], writes=[g_b])
                        else:
                            ev += 1
                            if ev % 2:
                                p.op("dve", lambda en: en.tensor_copy(out=g_t[:cn, :], in_=ps_t[:cn, :]),
                                     reads=[ps_b], writes=[g_b])
                            else:
                                p.op("act", lambda en: en.copy(out=g_t[:cn, :], in_=ps_t[:cn, :]),
                                     reads=[ps_b], writes=[g_b])
                        p.dma("sp", o_ap[r0 + ci:r0 + ci + cn, t0 + tt * 512:t0 + (tt + 1) * 512], g_t[:cn, :],
                              reads=[g_b])
            else:
                for tt in range(TG // 128):
                    ps_t, ps_b = pr.next()
                    for kc in range(KC):
                        xt, xbuf = xb[kc]
                        p.op("pe", lambda en: en.matmul(ps_t[:, :n], lhsT=xt[:, tt * 128:(tt + 1) * 128],
                                                        rhs=w_t[:, kc, :n],
                                                        start=(kc == 0), stop=(kc == KC - 1)),
                             reads=[w_b, xbuf], writes=[ps_b], inc=(kc == KC - 1))
                    g_t, g_b = stg.next()
                    ev += 1
                    if ev % 2:
                        p.op("dve", lambda en: en.tensor_copy(out=g_t[:, :n], in_=ps_t[:, :n]),
                             reads=[ps_b], writes=[g_b])
                    else:
                        p.op("act", lambda en: en.copy(out=g_t[:, :n], in_=ps_t[:, :n]),
                             reads=[ps_b], writes=[g_b])
                    p.dma("sp", o_ap[t0 + tt * 128:t0 + (tt + 1) * 128, r0:r0 + n], g_t[:, :n],
                          reads=[g_b])


CH = 64


def gdn_consts():
    i = np.arange(CH)
    c = {}
    c["ident"] = np.eye(128, dtype=np.float32)
    c["ones"] = np.ones((128, 128), np.float32)
    c["ucum"] = (i[:, None] <= i[None, :]).astype(np.float32)
    c["stril"] = (i[:, None] > i[None, :]).astype(np.float32)
    c["triu"] = (i[:, None] <= i[None, :]).astype(np.float32)
    return c


def run_gens(always, stages=()):
    active = list(always)
    stages = [list(st) for st in stages]
    cur = stages.pop(0) if stages else []
    active += cur
    while active:
        for g in list(active):
            try:
                next(g)
            except StopIteration:
                active.remove(g)
                if g in cur:
                    cur.remove(g)
        if not cur and stages:
            cur = stages.pop(0)
            active += cur


def emit_gdn_pre(p, S, scr, cst, l, gbt):
    NCH = S // CH
    W = NCH * 16
    ab_t = p.sb([CH, W], F32); abb = Buf()
    p.dma("sp", ab_t[:].rearrange("c (n k) -> c n k", k=16), scr["ab"].rearrange("(n c) k -> c n k", c=CH), writes=[abb])
    dtb_t = p.sb([CH, W], F32); dtbb = Buf()
    p.dma("sp", dtb_t[:], cst["dtb16"][l, :, :], writes=[dtbb])
    negA_t = p.sb([CH, W], F32); negAb = Buf()
    p.dma("sp", negA_t[:], cst["negA16"][l, :, :], writes=[negAb])
    g_t = p.sb([CH, W], F32); gb = Buf()
    beta_t = p.sb([CH, W], F32); betab = Buf()
    nbeta_t = p.sb([CH, W], F32); nbetab = Buf()
    tmpw = p.sb([CH, W], F32); tmpwb = Buf()
    p.op("dve", lambda e: e.tensor_tensor(out=tmpw[:], in0=ab_t[:], in1=dtb_t[:], op=ALU.add), reads=[abb, dtbb], writes=[tmpwb])
    p.op("act", lambda e: e.activation(out=tmpw[:], in_=tmpw[:], func=AF.Exp), reads=[tmpwb], writes=[tmpwb])
    p.op("act", lambda e: e.activation(out=tmpw[:], in_=tmpw[:], func=AF.Ln, bias=1.0), reads=[tmpwb], writes=[tmpwb])
    p.op("dve", lambda e: e.tensor_tensor(out=g_t[:], in0=tmpw[:], in1=negA_t[:], op=ALU.mult), reads=[tmpwb, negAb], writes=[gb])
    p.op("act", lambda e: e.activation(out=beta_t[:], in_=ab_t[:], func=AF.Sigmoid), reads=[abb], writes=[betab])
    p.op("dve", lambda e: e.tensor_scalar(out=nbeta_t[:], in0=beta_t[:], scalar1=-1.0, scalar2=None, op0=ALU.mult), reads=[betab], writes=[nbetab])
    p.dma("sp", gbt[0, :, :], g_t[:], reads=[gb])
    p.dma("sp", gbt[1, :, :], beta_t[:], reads=[betab])
    p.dma("sp", gbt[2, :, :], nbeta_t[:], reads=[nbetab])


def emit_gdn(p, S, scr, cst, l, heads, gbt):
    NH = len(heads)
    NCH = S // CH
    TL = 256
    NT = S // TL
    CPT = TL // CH
    qkvT, zin, oaT = scr["qkvT"], scr["z"], scr["oaT"]

    def cload(ap, shape, dt=F32):
        t = p.sb(shape, dt)
        b = Buf()
        p.dma("sp", t[:], ap, writes=[b])
        return t, b

    ident, identb = cload(cst["ident"][:, :], [128, 128])
    ones, onesb = cload(cst["ones"][:, :], [128, 128])
    ucum, ucumb = cload(cst["ucum"][:, :], [CH, CH])
    stril, strilb = cload(cst["stril"][:, :], [CH, CH])
    triu, triub = cload(cst["triu"][:, :], [CH, CH])
    gnw_t, gnwb = cload(cst["gnw"][l, :, :], [CH, 128])
    cw_t, cwb = cload(cst["cw"][l, :, :], [128, 96])
    W = NCH * 16
    g_t, gb = cload(gbt[0, :, :], [CH, W])
    beta_t, betab = cload(gbt[1, :, :], [CH, W])
    nbeta_t, nbetab = cload(gbt[2, :, :], [CH, W])
    epsc = p.sb([128, 1], F32); epsb = Buf()
    p.op("pool", lambda e: e.memset(epsc[:], RMS_EPS), writes=[epsb])

    banks = [p.ps([128, 512], F32) for _ in range(8)]

    class QRing:
        def __init__(self, bank_ids):
            self.items = [(banks[b], Buf()) for b in bank_ids]
            self.i = 0

        def next(self):
            it = self.items[self.i % len(self.items)]
            self.i += 1
            return it
    qa = QRing([0, 1, 2])
    qd = QRing([3, 4, 5, 6])
    nrmb = Buf()
    nrm = [(banks[7][:, 0:256], nrmb), (banks[7][:, 0:256], nrmb)]
    nrm_i = [0]

    def mk(shape, dt):
        return [[(p.sb(shape, dt), Buf()) for _ in range(NH)] for _ in range(2)]
    qTf = mk([128, TL], F32); kTf = mk([128, TL], F32); vTf = mk([128, TL], F32)
    qTb = mk([128, TL], BF16); kTb = mk([128, TL], BF16)
    oT = mk([128, TL], BF16)

    def mkc(shape, dt):
        return [[[(p.sb(shape, dt), Buf()) for _ in range(CPT)] for _ in range(NH)] for _ in range(2)]
    TTb = mkc([CH, CH], BF16); intraTb = mkc([CH, CH], BF16); qdecTb = mkc([128, CH], BF16)
    kdecb = mkc([CH, 128], BF16); vbeta = mkc([CH, 128], F32); scol = mkc([CH, 2], F32); gtc = mkc([128, 1], F32)
    zc = mkc([CH, 128], F32)

    def mkp(shape, dt):
        return [[(p.sb(shape, dt), Buf()) for _ in range(CPT)] for _ in range(NH)]
    t_gbw = mkp([CH, 128], F32); t_gc = mkp([CH, 4], F32); t_eg = mkp([128, CH], F32); t_zr = mkp([CH, 128], F32)
    t_s64 = [mkp([CH, CH], F32) for _ in range(9)]
    t_x = [(p.sb([128, TL + 3], F32), Buf()) for _ in range(NH)]
    t_c = [(p.sb([128, TL], F32), Buf()) for _ in range(NH)]
    t_rn = [(p.sb([CH, 128], BF16), Buf()) for _ in range(NH)]
    t_vn = [(p.sb([CH, 128], BF16), Buf()) for _ in range(NH)]
    t_oraw = [(p.sb([CH, 128], F32), Buf()) for _ in range(NH)]
    t_o = [(p.sb([CH, 128], F32), Buf()) for _ in range(NH)]
    t_o2 = [(p.sb([CH, 128], F32), Buf()) for _ in range(NH)]
    t_ss = [(p.sb([CH, 4], F32), Buf()) for _ in range(NH)]
    St = [(p.sb([128, 128], F32), Buf()) for _ in range(NH)]
    Sb = [(p.sb([128, 128], BF16), Buf()) for _ in range(NH)]
    for h in range(NH):
        p.op("pool", lambda e: e.memset(St[h][0][:], 0.0), writes=[St[h][1]])
        p.op("pool", lambda e: e.memset(Sb[h][0][:], 0.0), writes=[Sb[h][1]])

    def conv_gen(ti, h):
        par, t0, hg = ti % 2, ti * TL, heads[h]
        x_t, x_b = t_x[h]
        c_t, c_b = t_c[h]
        for a in range(3):
            r0 = a * 1024 + hg * 128
            if t0 == 0:
                p.op("pool", lambda e: e.memset(x_t[:, 0:3], 0.0), writes=[x_b])
                p.dma("sp", x_t[:, 3:TL + 3], qkvT[r0:r0 + 128, 0:TL], writes=[x_b])
            else:
                p.dma("sp", x_t[:, :], qkvT[r0:r0 + 128, t0 - 3:t0 + TL], writes=[x_b])
            wcol = lambda j: cw_t[:, (a * 8 + hg) * 4 + j:(a * 8 + hg) * 4 + j + 1]
            p.op("dve", lambda e: e.tensor_scalar(out=c_t[:], in0=x_t[:, 0:TL], scalar1=wcol(0), scalar2=None, op0=ALU.mult),
                 reads=[x_b, cwb], writes=[c_b])
            yield
            for j in range(1, 4):
                p.op("dve", lambda e: e.scalar_tensor_tensor(out=c_t[:], in0=x_t[:, j:j + TL], scalar=wcol(j), in1=c_t[:],
                                                              op0=ALU.mult, op1=ALU.add), reads=[x_b, cwb, c_b], writes=[c_b])
                yield
            dstf = (qTf, kTf, vTf)[a][par][h]
            p.op("act", lambda e: e.activation(out=dstf[0][:], in_=c_t[:], func=AF.Silu), reads=[c_b], writes=[dstf[1]])
            yield
            if a < 2:
                p.op("pool", lambda e: e.tensor_tensor(out=c_t[:], in0=dstf[0][:], in1=dstf[0][:], op=ALU.mult),
                     reads=[dstf[1]], writes=[c_b])
                yield
                ps_t, ps_b = nrm[nrm_i[0] % 2]
                nrm_i[0] += 1
                p.op("pe", lambda e: e.matmul(ps_t, lhsT=ones[:, :], rhs=c_t[:], start=True, stop=True),
                     reads=[onesb, c_b], writes=[ps_b])
                p.op("act", lambda e: e.activation(out=c_t[:], in_=ps_t, func=AF.Sqrt, bias=epsc[:, 0:1]),
                     reads=[ps_b, epsb], writes=[c_b])
                yield
                p.op("dve", lambda e: e.reciprocal(out=c_t[:], in_=c_t[:]), reads=[c_b], writes=[c_b])
                sc = (128 ** -0.5) if a == 0 else 1.0
                p.op("dve", lambda e: e.scalar_tensor_tensor(out=dstf[0][:], in0=dstf[0][:], scalar=sc, in1=c_t[:],
                                                              op0=ALU.mult, op1=ALU.mult), reads=[dstf[1], c_b], writes=[dstf[1]])
                yield
                dstb = (qTb, kTb)[a][par][h]
                p.op("pool", lambda e: e.tensor_copy(out=dstb[0][:], in_=dstf[0][:]), reads=[dstf[1]], writes=[dstb[1]])
                yield

    def prep_gen(ti, h, ci):
        par, hg = ti % 2, heads[h]
        n = ti * CPT + ci
        colg, colb = n * 16 + hg, n * 16 + 8 + hg
        cs = slice(ci * CH, (ci + 1) * CH)
        gcol = g_t[:, colg:colg + 1]
        tmp = [t_s64[k][h][ci] for k in range(9)]
        (gcr_t, gcr_b), (e1_t, e1_b), (e2_t, e2_b) = tmp[0], tmp[1], tmp[2]
        Pa, Pb_, PTa, PTb_, TTa, TTb_ = tmp[3], tmp[4], tmp[5], tmp[6], tmp[7], tmp[8]
        zr_t, zr_b = t_zr[h][ci]
        gbw_t, gbw_b = t_gbw[h][ci]
        gc_t, gc_b = t_gc[h][ci]
        eg_t, eg_b = t_eg[h][ci]
        sc_t, sc_b = scol[par][h][ci]
        gt_t, gt_b = gtc[par][h][ci]
        p.dma("sp", zr_t[:], zin[n * CH:(n + 1) * CH, hg * 128:(hg + 1) * 128], writes=[zr_b])
        p.op("act", lambda e: e.activation(out=zc[par][h][ci][0][:], in_=zr_t[:], func=AF.Silu),
             reads=[zr_b], writes=[zc[par][h][ci][1]])
        p.op("dve", lambda e: e.tensor_scalar(out=gbw_t[:, :], in0=ones[:CH, :128], scalar1=gcol, scalar2=None, op0=ALU.mult),
             reads=[onesb, gb], writes=[gbw_b])
        psA, psAb = qa.next()
        p.op("pe", lambda e: e.matmul(psA[:, 0:CH], lhsT=gbw_t[:, :], rhs=ucum[:, :], start=True, stop=True),
             reads=[gbw_b, ucumb], writes=[psAb], inc=False)
        p.op("pe", lambda e: e.matmul(psA[:CH, CH:2 * CH], lhsT=ucum[:, :], rhs=gbw_t[:, :CH], start=True, stop=True),
             reads=[ucumb, gbw_b], writes=[psAb])
        p.op("act", lambda e: e.copy(out=gc_t[:, 0:1], in_=psA[:CH, CH:CH + 1]), reads=[psAb], writes=[gc_b])
        p.op("act", lambda e: e.copy(out=gc_t[:, 1:2], in_=psA[:CH, CH - 1:CH]), reads=[psAb], writes=[gc_b])
        p.op("act", lambda e: e.activation(out=gt_t[:, :], in_=psA[:, CH - 1:CH], func=AF.Exp), reads=[psAb], writes=[gt_b])
        p.op("act", lambda e: e.copy(out=gcr_t[:], in_=psA[:CH, 0:CH]), reads=[psAb], writes=[gcr_b])
        p.op("act", lambda e: e.activation(out=eg_t[:, :], in_=psA[:, 0:CH], func=AF.Exp), reads=[psAb], writes=[eg_b])
        yield
        p.op("act", lambda e: e.activation(out=gc_t[:, 2:3], in_=gc_t[:, 0:1], func=AF.Exp), reads=[gc_b], writes=[gc_b])
        p.op("act", lambda e: e.activation(out=sc_t[:, 1:2], in_=gc_t[:, 0:1], func=AF.Exp, scale=-1.0, bias=gc_t[:, 1:2]),
             reads=[gc_b, sc_b], writes=[sc_b])
        p.op("dve", lambda e: e.tensor_scalar(out=e1_t[:], in0=gcr_t[:], scalar1=gc_t[:, 0:1], scalar2=0.0,
                                              op0=ALU.subtract, op1=ALU.max), reads=[gcr_b, gc_b], writes=[e1_b])
        p.op("dve", lambda e: e.tensor_scalar(out=e2_t[:], in0=gcr_t[:], scalar1=gc_t[:, 0:1], scalar2=0.0,
                                              op0=ALU.subtract, op1=ALU.min), reads=[gcr_b, gc_b], writes=[e2_b])
        p.op("pool", lambda e: e.tensor_tensor(out=qdecTb[par][h][ci][0][:], in0=qTf[par][h][0][:, cs], in1=eg_t[:, :], op=ALU.mult),
             reads=[qTf[par][h][1], eg_b], writes=[qdecTb[par][h][ci][1]])
        yield
        p.op("dve", lambda e: e.tensor_tensor(out=sc_t[:, 0:1], in0=gc_t[:, 2:3], in1=nbeta_t[:, colb:colb + 1], op=ALU.mult),
             reads=[gc_b, nbetab, sc_b], writes=[sc_b])
        p.op("act", lambda e: e.activation(out=e1_t[:], in_=e1_t[:], func=AF.Exp, scale=-1.0), reads=[e1_b], writes=[e1_b])
        p.op("act", lambda e: e.activation(out=e2_t[:], in_=e2_t[:], func=AF.Exp), reads=[e2_b], writes=[e2_b])
        yield
        p.op("pool", lambda e: e.tensor_tensor(out=e2_t[:], in0=e2_t[:], in1=triu[:, :], op=ALU.mult), reads=[e2_b, triub], writes=[e2_b])
        kb_t, kb_b = kTb[par][h]
        qb_t, qb_b = qTb[par][h]
        psBk, psKb = qd.next()
        psK, psT, psV = psBk[:, 0:128], psBk[:, 128:256], psBk[:, 256:384]
        psTb = psVb = psKb
        p.op("pe", lambda e: e.matmul(psK[:CH, 0:CH], lhsT=kb_t[:, cs], rhs=kb_t[:, cs], start=True, stop=True),
             reads=[kb_b], writes=[psKb], inc=False)
        p.op("pe", lambda e: e.matmul(psK[:CH, CH:2 * CH], lhsT=kb_t[:, cs], rhs=qb_t[:, cs], start=True, stop=True),
             reads=[kb_b, qb_b], writes=[psKb], inc=False)
        p.op("pe", lambda e: e.transpose(psT[:CH, :], kTf[par][h][0][:, cs], ident[:, :]),
             reads=[kTf[par][h][1], identb], writes=[psTb], inc=False)
        p.op("pe", lambda e: e.transpose(psV[:CH, :], vTf[par][h][0][:, cs], ident[:, :]),
             reads=[vTf[par][h][1], identb], writes=[psVb])
        it_t, it_b = intraTb[par][h][ci]
        p.op("dve", lambda e: e.tensor_tensor(out=it_t[:], in0=psK[:CH, CH:2 * CH], in1=e2_t[:], op=ALU.mult),
             reads=[psKb, e2_b], writes=[it_b])
        kd_t, kd_b = kdecb[par][h][ci]
        p.op("dve", lambda e: e.tensor_scalar(out=kd_t[:], in0=psT[:CH, :], scalar1=sc_t[:, 1:2], scalar2=None, op0=ALU.mult),
             reads=[psTb, sc_b], writes=[kd_b])
        vb_t, vb_b = vbeta[par][h][ci]
        p.op("dve", lambda e: e.tensor_scalar(out=vb_t[:], in0=psV[:CH, :], scalar1=beta_t[:, colb:colb + 1], scalar2=None, op0=ALU.mult),
             reads=[psVb, betab], writes=[vb_b])
        n_t, n_b = Pa
        p.op("dve", lambda e: e.tensor_tensor(out=n_t[:], in0=psK[:CH, 0:CH], in1=e1_t[:], op=ALU.mult), reads=[psKb, e1_b], writes=[n_b])
        yield
        p.op("dve", lambda e: e.scalar_tensor_tensor(out=n_t[:], in0=n_t[:], scalar=nbeta_t[:, colb:colb + 1], in1=stril[:, :],
                                                      op0=ALU.mult, op1=ALU.mult), reads=[n_b, nbetab, strilb], writes=[n_b])
        yield
        psC, psCb = qd.next()
        p.op("pe", lambda e: e.transpose(psC[:CH, 0:CH], n_t[:, :], ident[:CH, :CH]), reads=[n_b, identb], writes=[psCb])
        nt_t, nt_b = PTa
        p.op("dve", lambda e: e.tensor_copy(out=nt_t[:], in_=psC[:CH, 0:CH]), reads=[psCb], writes=[nt_b])
        tt_t, tt_b = TTa
        p.op("dve", lambda e: e.tensor_tensor(out=tt_t[:], in0=psC[:CH, 0:CH], in1=ident[:CH, :CH], op=ALU.add),
             reads=[psCb, identb], writes=[tt_b])
        yield
        P_cur, PT_cur, TT_cur = Pa, PTa, TTa
        P_alt, PT_alt, TT_alt = Pb_, PTb_, TTb_
        for k in range(1, 6):
            psD, psDb = qa.next()
            p.op("pe", lambda e: e.matmul(psD[:CH, 0:CH], lhsT=PT_cur[0][:, :], rhs=P_cur[0][:, :], start=True, stop=True),
                 reads=[PT_cur[1], P_cur[1]], writes=[psDb], inc=(k == 5))
            if k < 5:
                p.op("pe", lambda e: e.matmul(psD[:CH, CH:2 * CH], lhsT=P_cur[0][:, :], rhs=PT_cur[0][:, :], start=True, stop=True),
                     reads=[PT_cur[1], P_cur[1]], writes=[psDb])
            p.op("act", lambda e: e.copy(out=P_alt[0][:], in_=psD[:CH, 0:CH]), reads=[psDb], writes=[P_alt[1]])
            if k < 5:
                p.op("act", lambda e: e.copy(out=PT_alt[0][:], in_=psD[:CH, CH:2 * CH]), reads=[psDb], writes=[PT_alt[1]])
            yield
            psE, psEb = qd.next()
            p.op("pe", lambda e: e.matmul(psE[:CH, 0:CH], lhsT=P_alt[0][:, :], rhs=TT_cur[0][:, :], start=True, stop=True),
                 reads=[P_alt[1], TT_cur[1]], writes=[psEb])
            p.op("dve", lambda e: e.tensor_tensor(out=TT_alt[0][:], in0=psE[:CH, 0:CH], in1=TT_cur[0][:], op=ALU.add),
                 reads=[psEb, TT_cur[1]], writes=[TT_alt[1]])
            yield
            P_cur, P_alt = P_alt, P_cur
            PT_cur, PT_alt = PT_alt, PT_cur
            TT_cur, TT_alt = TT_alt, TT_cur
        ttb_t, ttb_b = TTb[par][h][ci]
        p.op("pool", lambda e: e.tensor_copy(out=ttb_t[:], in_=TT_cur[0][:]), reads=[TT_cur[1]], writes=[ttb_b])
        yield

    def scan_gen(ti, h):
        par, t0, hg = ti % 2, ti * TL, heads[h]
        S_t, S_b = St[h]
        Sb_t, Sb_b = Sb[h]
        rn_t, rn_b = t_rn[h]
        vn_t, vn_b = t_vn[h]
        oraw_t, oraw_b = t_oraw[h]
        o_t, o_b = t_o[h]
        o2_t, o2_b = t_o2[h]
        ss_t, ss_b = t_ss[h]
        oT_t, oT_b = oT[par][h]
        for ci in range(CPT):
            cs = slice(ci * CH, (ci + 1) * CH)
            sc_t, sc_b = scol[par][h][ci]
            vb_t, vb_b = vbeta[par][h][ci]
            ps1, ps1b = qd.next()
            p.op("pe", lambda e: e.matmul(ps1[:CH, 0:128], lhsT=kTb[par][h][0][:, cs], rhs=Sb_t[:, :], start=True, stop=True),
                 reads=[kTb[par][h][1], Sb_b], writes=[ps1b])
            p.op("dve", lambda e: e.scalar_tensor_tensor(out=rn_t[:], in0=ps1[:CH, 0:128], scalar=sc_t[:, 0:1], in1=vb_t[:],
                                                          op0=ALU.mult, op1=ALU.add), reads=[ps1b, sc_b, vb_b], writes=[rn_b])
            yield
            ps2, ps2b = qa.next()
            p.op("pe", lambda e: e.matmul(ps2[:CH, 0:128], lhsT=TTb[par][h][ci][0][:, :], rhs=rn_t[:, :], start=True, stop=True),
                 reads=[TTb[par][h][ci][1], rn_b], writes=[ps2b])
            p.op("act", lambda e: e.copy(out=vn_t[:], in_=ps2[:CH, 0:128]), reads=[ps2b], writes=[vn_b])
            yield
            pso, psob = qa.next()
            p.op("pe", lambda e: e.matmul(pso[:CH, 0:128], lhsT=qdecTb[par][h][ci][0][:, :], rhs=Sb_t[:, :], start=True, stop=False),
                 reads=[qdecTb[par][h][ci][1], Sb_b], writes=[psob], inc=False)
            p.op("pe", lambda e: e.matmul(pso[:CH, 0:128], lhsT=intraTb[par][h][ci][0][:, :], rhs=vn_t[:, :], start=False, stop=True),
                 reads=[intraTb[par][h][ci][1], vn_b], writes=[psob], inc=False)
            ps3, ps3b = qd.next()
            p.op("pe", lambda e: e.matmul(ps3[:, 0:128], lhsT=kdecb[par][h][ci][0][:, :], rhs=vn_t[:, :], start=True, stop=True),
                 reads=[kdecb[par][h][ci][1], vn_b], writes=[psob, ps3b])
            gt_t, gt_b = gtc[par][h][ci]
            p.op("dve", lambda e: e.scalar_tensor_tensor(out=S_t[:], in0=S_t[:], scalar=gt_t[:, 0:1], in1=ps3[:, 0:128],
                                                          op0=ALU.mult, op1=ALU.add), reads=[S_b, gt_b, ps3b], writes=[S_b])
            p.op("act", lambda e: e.copy(out=oraw_t[:], in_=pso[:CH, 0:128]), reads=[psob], writes=[oraw_b])
            yield
            p.op("pool", lambda e: e.tensor_copy(out=Sb_t[:], in_=S_t[:]), reads=[S_b], writes=[Sb_b])
            p.op("act", lambda e: e.activation(out=o_t[:], in_=oraw_t[:], func=AF.Square), reads=[oraw_b], writes=[o_b])
            yield
            p.op("dve", lambda e: e.reduce_sum(out=ss_t[:, 0:1], in_=o_t[:], axis=AX.X), reads=[o_b], writes=[ss_b])
            yield
            p.op("act", lambda e: e.activation(out=ss_t[:, 1:2], in_=ss_t[:, 0:1], func=AF.Sqrt, scale=1.0 / 128, bias=epsc[:CH, 0:1]),
                 reads=[ss_b, epsb], writes=[ss_b])
            yield
            p.op("dve", lambda e: e.reciprocal(out=ss_t[:, 2:3], in_=ss_t[:, 1:2]), reads=[ss_b], writes=[ss_b])
            yield
            p.op("dve", lambda e: e.scalar_tensor_tensor(out=o_t[:], in0=oraw_t[:], scalar=ss_t[:, 2:3], in1=gnw_t[:, :],
                                                          op0=ALU.mult, op1=ALU.mult), reads=[oraw_b, ss_b, gnwb, o_b], writes=[o_b])
            yield
            p.op("pool", lambda e: e.tensor_tensor(out=o2_t[:], in0=o_t[:], in1=zc[par][h][ci][0][:], op=ALU.mult),
                 reads=[o_b, zc[par][h][ci][1]], writes=[o2_b])
            yield
            ps4, ps4b = qa.next()
            p.op("pe", lambda e: e.transpose(ps4[:, 0:CH], o2_t[:, :], ident[:CH, :CH]), reads=[o2_b, identb], writes=[ps4b])
            p.op("act", lambda e: e.copy(out=oT_t[:, cs], in_=ps4[:, 0:CH]), reads=[ps4b], writes=[oT_b])
            if ci == CPT - 1:
                p.dma("sp", oaT[hg * 128:(hg + 1) * 128, t0:t0 + TL], oT_t[:, :], reads=[oT_b])
            yield

    def conv_all(ti):
        return [conv_gen(ti, h) for h in range(NH)]

    def chunk_all(ti):
        return [prep_gen(ti, h, ci) for ci in range(CPT) for h in range(NH)]

    run_gens([], [conv_all(0), chunk_all(0)])
    for ti in range(NT):
        scans = [scan_gen(ti, h) for h in range(NH)]
        if ti + 1 < NT:
            run_gens(scans, [conv_all(ti + 1), chunk_all(ti + 1)])
        else:
            run_gens(scans)


NEG_CAUSAL = -1.0e30
NEG_TAKEN = -2.0e30
BIS_RANGE = 4096.0
BIS_ITERS = 30
BIS_CW = 2048


def t5_bucket_np(dist):
    n = np.maximum(dist, 0)
    max_exact = 16
    lr = np.log(np.maximum(n, 1).astype(np.float32) / np.float32(max_exact)) / np.float32(math.log(128 / max_exact))
    large = max_exact + (lr * np.float32(32 - max_exact)).astype(np.int32)
    large = np.minimum(large, 31)
    return np.where(n < max_exact, n, large)


def dsa_consts(rel_bias):
    c = {}
    sp = np.arange(128)[:, None]
    tq = np.arange(512)[None, :]
    nb = np.zeros((8, 5, 128, 512), np.float32)
    for r in range(-1, 4):
        dist = tq - (r * 128 + sp)
        b = t5_bucket_np(dist)
        for h in range(8):
            nb[h, r + 1] = np.where(dist >= 0, rel_bias[b, h], np.float32(-30000.0))
    c["nb"] = nb
    c["cbias"] = np.ascontiguousarray(np.broadcast_to(rel_bias[31][None, :], (128, 8))).astype(np.float32)
    cadd = np.zeros((4, 128, 512), np.float32)
    for qi in range(4):
        cadd[qi] = np.where(np.arange(512)[None, :] <= qi * 128 + np.arange(128)[:, None], 0.0, NEG_CAUSAL)
    c["cadd"] = cadd
    return c


def emit_dsa(p, S, scr, cst):
    NG = S // 512
    topk = min(256, S // 4)
    NR = topk // 8
    SCALE = 128 ** -0.5
    qiT, kiT, wi, qbT, kbT, vb, obT = scr["qiT"], scr["kiT"], scr["wi"], scr["qbT"], scr["kbT"], scr["vb"], scr["obT"]

    identf = p.sb([128, 128], F32); identfb = Buf()
    p.dma("sp", identf[:], cst["ident"][:, :], writes=[identfb])
    identb = p.sb([128, 128], BF16); identbb = Buf()
    p.op("pool", lambda e: e.tensor_copy(out=identb[:], in_=identf[:]), reads=[identfb], writes=[identbb])
    cadd_t = p.sb([128, 4, 512], F32); caddb = Buf()
    p.dma("sp", cadd_t[:], cst["cadd"].rearrange("q p s -> p q s"), writes=[caddb])
    cbias_t = p.sb([128, 8], F32); cbiasb = Buf()
    p.dma("sp", cbias_t[:], cst["cbias"][:, :], writes=[cbiasb])
    kiT2 = p.sb([128, S], BF16); kib = Buf()
    p.dma("sp", kiT2[0:64, :], kiT[:, :], writes=[kib])
    p.dma("sp", kiT2[64:128, :], kiT[:, :], writes=[kib])
    sc = p.sb([128, S], F32); scb = Buf()
    maskTs = [(p.sb([128, S // 128, 512], mybir.dt.uint8), Buf()) for _ in range(2)]

    lg = Ring(p, 3, [128, 512], F32, psum=True)
    ops = Ring(p, 4, [128, 512], F32, psum=True)
    ptf = Ring(p, 1, [128, 512], F32, psum=True)
    trb = ptf
    qi_r = Ring(p, 2, [128, 4, 128], BF16)
    wi_r = Ring(p, 2, [128, 8], F32)
    aw_r = Ring(p, 2, [128, 8], F32)
    sg_r = Ring(p, 2, [128, 8], F32)
    relu_r = Ring(p, 3, [128, 512], F32)
    lo_r = Ring(p, 2, [128, 4], F32)
    jk_r = Ring(p, 2, [128, BIS_CW], F32)
    cn_r = Ring(p, 2, [128, BIS_ITERS * 16], F32)
    mk_r = Ring(p, 2, [128, 512], F32)
    qT_r = Ring(p, 2, [128, 512], BF16)
    kT_r = Ring(p, 3, [128, 512], BF16)
    v_r = Ring(p, 4, [128, 4, 129], BF16)
    for (v_t, v_b) in v_r.items:
        p.op("pool", lambda e: e.memset(v_t[:, :, 128:129], 1.0), writes=[v_b])
    nb_r = Ring(p, 5, [128, 512], F32)
    lgt_r = Ring(p, 2, [128, 512], F32)
    P_r = Ring(p, 3, [128, 512], BF16)
    Pm_r = Ring(p, 5, [128, 512], BF16)
    on_r = Ring(p, 2, [128, 128], F32)
    rc_r = Ring(p, 4, [128, 1], F32)
    obT_r = Ring(p, 2, [128, 512], BF16)

    def indexer_gen(g):
        maskT, maskTb = maskTs[g % 2]
        L = 512 * (g + 1)
        for qi in range(4):
            t0 = g * 512 + qi * 128
            q_t, q_b = qi_r.next()
            p.dma("sp", q_t[:], qiT[:, t0:t0 + 128].rearrange("(hp p) t -> p hp t", p=128), writes=[q_b])
            w_t, w_b = wi_r.next()
            p.dma("sp", w_t[:], wi[t0:t0 + 128, :], writes=[w_b])
            aw_t, aw_b = aw_r.next()
            sg_t, sg_b = sg_r.next()
            p.op("act", lambda e: e.activation(out=aw_t[:], in_=w_t[:], func=AF.Abs), reads=[w_b], writes=[aw_b])
            p.op("act", lambda e: e.activation(out=sg_t[:], in_=w_t[:], func=AF.Sign), reads=[w_b], writes=[sg_b])
            for j in range(g + 1):
                js = slice(j * 512, (j + 1) * 512)
                for h in range(8):
                    hp, off = h // 2, (h % 2) * 64
                    ps, psb = lg.next()
                    p.op("pe", lambda e: e.matmul(ps[:, :], lhsT=q_t[off:off + 64, hp, :], rhs=kiT2[off:off + 64, js], start=True, stop=True),
                         reads=[q_b, kib], writes=[psb])
                    r_t, r_b = relu_r.next()
                    p.op("act", lambda e: e.activation(out=r_t[:], in_=ps[:, :], func=AF.Relu, scale=aw_t[:, h:h + 1]),
                         reads=[psb, aw_b], writes=[r_b])
                    if h == 0:
                        p.op("dve", lambda e: e.tensor_scalar(out=sc[:, js], in0=r_t[:], scalar1=sg_t[:, 0:1], scalar2=None, op0=ALU.mult),
                             reads=[r_b, sg_b], writes=[scb])
                    else:
                        p.op("dve", lambda e: e.scalar_tensor_tensor(out=sc[:, js], in0=r_t[:], scalar=sg_t[:, h:h + 1], in1=sc[:, js],
                                                                      op0=ALU.mult, op1=ALU.add), reads=[r_b, sg_b, scb], writes=[scb])
                    yield
            ds = slice(g * 512, (g + 1) * 512)
            p.op("dve", lambda e: e.tensor_tensor(out=sc[:, ds], in0=sc[:, ds], in1=cadd_t[:, qi, :], op=ALU.add),
                 reads=[scb, caddb], writes=[scb])
            lo_t, lo_b = lo_r.next()
            cn_t, cn_b = cn_r.next()
            p.op("dve", lambda e: e.memset(lo_t[:, 0:1], -BIS_RANGE), writes=[lo_b])
            p.op("dve", lambda e: e.memset(cn_t[:], 0.0), writes=[cn_b])
            yield
            for r in range(BIS_ITERS):
                w_r = BIS_RANGE / (2.0 ** r)
                p.op("dve", lambda e: e.tensor_scalar(out=lo_t[:, 1:2], in0=lo_t[:, 0:1], scalar1=w_r, scalar2=None, op0=ALU.add),
                     reads=[lo_b], writes=[lo_b])
                j_t, j_b = jk_r.next()
                for c0 in range(0, L, BIS_CW):
                    cw = min(BIS_CW, L - c0)
                    p.op("dve", lambda e: e.tensor_scalar(out=j_t[:, :cw], in0=sc[:, c0:c0 + cw], scalar1=lo_t[:, 1:2], scalar2=0.0,
                                                          op0=ALU.is_ge, op1=ALU.add, accum_out=cn_t[:, r * 16 + c0 // BIS_CW:r * 16 + c0 // BIS_CW + 1]),
                         reads=[scb, lo_b, cn_b], writes=[j_b, cn_b])
                nch = (L + BIS_CW - 1) // BIS_CW
                if nch > 1:
                    p.op("dve", lambda e: e.reduce_sum(out=cn_t[:, r * 16:r * 16 + 1], in_=cn_t[:, r * 16:r * 16 + nch], axis=AX.X),
                         reads=[cn_b], writes=[cn_b])
                p.op("dve", lambda e: e.tensor_single_scalar(out=lo_t[:, 2:3], in_=cn_t[:, r * 16:r * 16 + 1], scalar=topk - 0.5, op=ALU.is_ge),
                     reads=[cn_b, lo_b], writes=[lo_b])
                p.op("dve", lambda e: e.scalar_tensor_tensor(out=lo_t[:, 0:1], in0=lo_t[:, 2:3], scalar=w_r, in1=lo_t[:, 0:1],
                                                              op0=ALU.mult, op1=ALU.add), reads=[lo_b], writes=[lo_b])
                yield
            for j in range(g + 1):
                js = slice(j * 512, (j + 1) * 512)
                mk_t, mk_b = mk_r.next()
                p.op("dve", lambda e: e.tensor_scalar(out=mk_t[:], in0=sc[:, js], scalar1=lo_t[:, 0:1], scalar2=None, op0=ALU.is_ge),
                     reads=[scb, lo_b], writes=[mk_b])
                tp, tpb = trb.next()
                for i in range(4):
                    p.op("pe", lambda e: e.transpose(tp[:, i * 128:(i + 1) * 128], mk_t[:, i * 128:(i + 1) * 128], identf[:, :]),
                         reads=[mk_b, identfb], writes=[tpb], inc=(i == 3))
                p.op("act", lambda e: e.copy(out=maskT[:, 4 * j:4 * j + 4, qi * 128:(qi + 1) * 128],
                                             in_=tp[:, :].rearrange("p (a b) -> p a b", b=128)),
                     reads=[tpb], writes=[maskTb])
                yield
    def attention_gen(g):
        maskT, maskTb = maskTs[g % 2]
        for h in range(8):
            qT_t, qT_b = qT_r.next()
            p.dma("sp", qT_t[:], qbT[h * 128:(h + 1) * 128, g * 512:(g + 1) * 512], writes=[qT_b])
            nbt = {}
            for r in range(-1 if g > 0 else 0, 4):
                n_t, n_b = nb_r.next()
                p.dma("sp", n_t[:], cst["nb"][h, r + 1, :, :], writes=[n_b])
                nbt[r] = (n_t, n_b)
            obank = [ops.next() for _ in range(4)]
            oacc = [(obank[qi][0][:, 0:129], obank[qi][1]) for qi in range(4)]
            steps = [(j, kbi) for j in range(g + 1) for kbi in range(4)]
            SKEW = 2
            live = {}
            kv = {}
            for i in range(len(steps) + SKEW):
                if i < len(steps):
                    j, kbi = steps[i]
                    if kbi == 0:
                        kT_t, kT_b = kT_r.next()
                        p.dma("sp", kT_t[:], kbT[h * 128:(h + 1) * 128, j * 512:(j + 1) * 512], writes=[kT_b])
                        v_t, v_b = v_r.next()
                        p.dma("sp", v_t[:, :, 0:128], vb[j * 512:(j + 1) * 512, h * 128:(h + 1) * 128].rearrange("(kb p) e -> p kb e", p=128),
                              writes=[v_b])
                        kv[j] = (kT_t, kT_b, v_t, v_b)
                    kT_t, kT_b, v_t, v_b = kv[j]
                    kb = 4 * j + kbi
                    r = kb - 4 * g
                    ps, psb = lg.next()
                    p.op("pe", lambda e: e.matmul(ps[:, :], lhsT=kT_t[:, kbi * 128:(kbi + 1) * 128], rhs=qT_t[:, :], start=True, stop=True),
                         reads=[kT_b, qT_b], writes=[psb])
                    P_t, P_b = P_r.next()
                    if r >= -1:
                        n_t, n_b = nbt[r]
                        l_t, l_b = lgt_r.next()
                        p.op("dve", lambda e: e.scalar_tensor_tensor(out=l_t[:], in0=ps[:, :], scalar=SCALE, in1=n_t[:],
                                                                      op0=ALU.mult, op1=ALU.add), reads=[psb, n_b], writes=[l_b])
                        p.op("act", lambda e: e.activation(out=P_t[:], in_=l_t[:], func=AF.Exp), reads=[l_b], writes=[P_b])
                    else:
                        p.op("act", lambda e: e.activation(out=P_t[:], in_=ps[:, :], func=AF.Exp, scale=SCALE, bias=cbias_t[:, h:h + 1]),
                             reads=[psb, cbiasb], writes=[P_b])
                    Pm_t, Pm_b = Pm_r.next()
                    p.op("pool", lambda e: e.tensor_tensor(out=Pm_t[:], in0=P_t[:], in1=maskT[:, kb, :], op=ALU.mult),
                         reads=[P_b, maskTb], writes=[Pm_b])
                    live[i] = (Pm_t, Pm_b, v_t, v_b, kb, kbi)
                if i - SKEW >= 0:
                    Pm_t, Pm_b, v_t, v_b, kb, kbi = live.pop(i - SKEW)
                    qis = [qi for qi in range(4) if kb <= 4 * g + qi]
                    for qi in qis:
                        o_t, o_b = oacc[qi]
                        p.op("pe", lambda e: e.matmul(o_t, lhsT=Pm_t[:, qi * 128:(qi + 1) * 128], rhs=v_t[:, kbi, :],
                                                      start=(kb == 0), stop=(kb == 4 * g + qi)),
                             reads=[Pm_b, v_b], writes=[o_b], inc=(qi == qis[-1]))
                yield
            ob_t, ob_b = obT_r.next()
            for qi in range(4):
                o_t, o_b = oacc[qi]
                rc_t, rc_b = rc_r.next()
                p.op("dve", lambda e: e.reciprocal(out=rc_t[:], in_=o_t[:, 128:129]), reads=[o_b], writes=[rc_b])
                on_t, on_b = on_r.next()
                p.op("dve", lambda e: e.tensor_scalar(out=on_t[:], in0=o_t[:, 0:128], scalar1=rc_t[:, 0:1], scalar2=None, op0=ALU.mult),
                     reads=[o_b, rc_b], writes=[on_b])
                pt, ptb = ptf.next()
                p.op("pe", lambda e: e.transpose(pt[:, 0:128], on_t[:, :], identf[:, :]), reads=[on_b, identfb], writes=[ptb])
                p.op("act", lambda e: e.copy(out=ob_t[:, qi * 128:(qi + 1) * 128], in_=pt[:, 0:128]), reads=[ptb], writes=[ob_b])
                yield
            p.dma("sp", obT[h * 128:(h + 1) * 128, g * 512:(g + 1) * 512], ob_t[:, :], reads=[ob_b])

    run_gens([indexer_gen(0)])
    for g in range(NG):
        if g + 1 < NG:
            run_gens([attention_gen(g), indexer_gen(g + 1)])
        else:
            run_gens([attention_gen(g)])


def emit_ffn(p, S, scr, wts, lnp, xres, xout, xTb_next):
    oaT, obT, sgT = scr["oaT"], scr["obT"], scr["sgT"]
    NJ = D_FF // 128
    big = p.sb([128, NJ * 512], BF16)
    slot = [Buf() for _ in range(NJ)]
    aT = lambda j: big[:, j * 512:(j + 1) * 512]
    x1 = [(p.sb([128, 2048], F32), Buf()) for _ in range(4)]
    x1T = [(p.sb([128, 512], BF16), Buf()) for _ in range(16)]
    WS = 8192
    wring = Ring(p, 4, [128, WS], BF16)
    sg_r = Ring(p, 4, [128, 512], F32)
    m_r = Ring(p, 4, [128, 512], F32)
    si_r = Ring(p, 2, [128, 512], F32)
    ln_r = Ring(p, 2, [128, 2048], F32)
    st_r = Ring(p, 2, [128, 4, 6], F32)
    mv_r = Ring(p, 4, [128, 4], F32)
    epsc = p.sb([128, 1], F32); epsb = Buf()
    p.op("pool", lambda e: e.memset(epsc[:], LN_EPS), writes=[epsb])
    identf = p.sb([128, 128], F32); identfb = Buf()
    p.dma("sp", identf[:], scr["ident"][:, :], writes=[identfb])
    pr = Ring(p, 7, [128, 512], F32, psum=True)
    ptr = Ring(p, 1, [128, 512], F32, psum=True)

    def layer_norm(tt, gname, bname):
        x_t, x_b = x1[tt]
        st_t, st_b = st_r.next()
        for c in range(4):
            p.op("dve", lambda e: e.bn_stats(out=st_t[:, c, :], in_=x_t[:, c * 512:(c + 1) * 512]), reads=[x_b], writes=[st_b])
        mv_t, mv_b = mv_r.next()
        p.op("dve", lambda e: e.bn_aggr(out=mv_t[:, 0:2], in_=st_t[:].rearrange("p a b -> p (a b)")), reads=[st_b], writes=[mv_b])
        p.op("act", lambda e: e.activation(out=mv_t[:, 2:3], in_=mv_t[:, 1:2], func=AF.Sqrt, bias=epsc[:, 0:1]),
             reads=[mv_b, epsb], writes=[mv_b])
        p.op("dve", lambda e: e.reciprocal(out=mv_t[:, 2:3], in_=mv_t[:, 2:3]), reads=[mv_b], writes=[mv_b])
        p.op("dve", lambda e: e.scalar_tensor_tensor(out=mv_t[:, 3:4], in0=mv_t[:, 0:1], scalar=-1.0, in1=mv_t[:, 2:3],
                                                      op0=ALU.mult, op1=ALU.mult), reads=[mv_b], writes=[mv_b])
        p.op("act", lambda e: e.activation(out=x_t[:], in_=x_t[:], func=AF.Identity, scale=mv_t[:, 2:3], bias=mv_t[:, 3:4]),
             reads=[x_b, mv_b], writes=[x_b])
        g_t, g_b = ln_r.next()
        p.dma("sp", g_t[:], lnp[gname][:, :], writes=[g_b])
        p.op("pool", lambda e: e.tensor_tensor(out=x_t[:], in0=x_t[:], in1=g_t[:], op=ALU.mult), reads=[x_b, g_b], writes=[x_b])
        b_t, b_b = ln_r.next()
        p.dma("sp", b_t[:], lnp[bname][:, :], writes=[b_b])
        p.op("pool", lambda e: e.tensor_tensor(out=x_t[:], in0=x_t[:], in1=b_t[:], op=ALU.add), reads=[x_b, b_b], writes=[x_b])

    def transposes(tt, dst_tiles):
        x_t, x_b = x1[tt]
        for k4 in range(4):
            pt, ptb = ptr.next()
            for i in range(4):
                k = k4 * 4 + i
                p.op("pe", lambda e: e.transpose(pt[:, i * 128:(i + 1) * 128], x_t[:, k * 128:(k + 1) * 128], identf[:, :]),
                     reads=[x_b, identfb], writes=[ptb], inc=(i == 3))
            for i in range(4):
                k = k4 * 4 + i
                d_t, d_b = dst_tiles[k]
                p.op("act", lambda e: e.copy(out=d_t[:, tt * 128:(tt + 1) * 128], in_=pt[:, i * 128:(i + 1) * 128]),
                     reads=[ptb], writes=[d_b])

    for T in range(S // 512):
        ts = slice(T * 512, (T + 1) * 512)
        oa_v = big[:, 16 * 512:24 * 512].rearrange("p (k t) -> p k t", t=512)
        ob_v = big[:, 24 * 512:32 * 512].rearrange("p (k t) -> p k t", t=512)
        p.dma("sp", oa_v, oaT[:, ts].rearrange("(k p) t -> p k t", p=128), writes=slot[16:24])
        p.dma("sp", ob_v, obT[:, ts].rearrange("(k p) t -> p k t", p=128), writes=slot[24:32])
        for tt in range(4):
            p.dma("sp", x1[tt][0][:], xres[T * 512 + tt * 128:T * 512 + (tt + 1) * 128, :], writes=[x1[tt][1]])
        for fg in range(4):
            w_t, w_b = wring.next()
            wa_v = w_t[:, 0:4096].rearrange("p (k f) -> p k f", f=512)
            wb_v = w_t[:, 4096:8192].rearrange("p (k f) -> p k f", f=512)
            p.dma("sp", wa_v, wts["wa"][:, fg * 512:(fg + 1) * 512].rearrange("(k p) f -> p k f", p=128), writes=[w_b])
            p.dma("sp", wb_v, wts["wb"][:, fg * 512:(fg + 1) * 512].rearrange("(k p) f -> p k f", p=128), writes=[w_b])
            for fi in range(4):
                ft = fg * 4 + fi
                fs = slice(fi * 128, (fi + 1) * 128)
                psA, psAb = pr.next()
                for k in range(8):
                    p.op("pe", lambda e: e.matmul(psA[:, :], lhsT=wa_v[:, k, fs], rhs=oa_v[:, k, :], start=(k == 0), stop=(k == 7)),
                         reads=[w_b] + slot[16:24], writes=[psAb], inc=(k == 7))
                psB, psBb = pr.next()
                for k in range(8):
                    p.op("pe", lambda e: e.matmul(psB[:, :], lhsT=wb_v[:, k, fs], rhs=ob_v[:, k, :], start=(k == 0), stop=(k == 7)),
                         reads=[w_b] + slot[24:32], writes=[psBb], inc=(k == 7))
                ga_t, ga_b = sg_r.next()
                p.dma("sp", ga_t[:], sgT[ft * 128:(ft + 1) * 128, ts], writes=[ga_b])
                gb_t, gb_b = sg_r.next()
                p.dma("sp", gb_t[:], sgT[2048 + ft * 128:2048 + (ft + 1) * 128, ts], writes=[gb_b])
                m1_t, m1_b = m_r.next()
                p.op("dve", lambda e: e.tensor_tensor(out=m1_t[:], in0=psA[:, :], in1=ga_t[:], op=ALU.mult), reads=[psAb, ga_b], writes=[m1_b])
                m2_t, m2_b = m_r.next()
                p.op("dve", lambda e: e.tensor_tensor(out=m2_t[:], in0=psB[:, :], in1=gb_t[:], op=ALU.mult), reads=[psBb, gb_b], writes=[m2_b])
                p.op("pool", lambda e: e.tensor_tensor(out=aT(ft), in0=m1_t[:], in1=m2_t[:], op=ALU.add), reads=[m1_b, m2_b], writes=[slot[ft]])
        for nt in range(4):
            ns = slice(nt * 512, (nt + 1) * 512)
            w_t, w_b = wring.next()
            wo_v = w_t[:, :].rearrange("p (k n) -> p k n", n=512)
            p.dma("sp", wo_v, wts["wout"][:, ns].rearrange("(k p) n -> p k n", p=128), writes=[w_b])
            for tt in range(4):
                ps, psb = pr.next()
                for k in range(16):
                    p.op("pe", lambda e: e.matmul(ps[:, :], lhsT=aT(k)[:, tt * 128:(tt + 1) * 128], rhs=wo_v[:, k, :],
                                                  start=(k == 0), stop=(k == 15)), reads=[w_b, slot[k]], writes=[psb], inc=(k == 15))
                x_t, x_b = x1[tt]
                p.op("dve", lambda e: e.scalar_tensor_tensor(out=x_t[:, ns], in0=x_t[:, ns], scalar=ALPHA, in1=ps[:, :],
                                                              op0=ALU.mult, op1=ALU.add), reads=[x_b, psb], writes=[x_b])
        for tt in range(4):
            layer_norm(tt, "g1", "b1")
            transposes(tt, x1T)
        for cg in range(NJ // 4):
            wg_t, wg_b = wring.next()
            wg_v = wg_t[:, :].rearrange("p (k n) -> p k n", n=512)
            p.dma("sp", wg_v, wts["wfi"][:, cg * 512:(cg + 1) * 512].rearrange("(k p) n -> p k n", p=128), writes=[wg_b])
            wu_t, wu_b = wring.next()
            wu_v = wu_t[:, :].rearrange("p (k n) -> p k n", n=512)
            p.dma("sp", wu_v, wts["wfi"][:, D_FF + cg * 512:D_FF + (cg + 1) * 512].rearrange("(k p) n -> p k n", p=128), writes=[wu_b])
            for ci in range(4):
                jt = cg * 4 + ci
                cs = slice(ci * 128, (ci + 1) * 128)
                psG, psGb = pr.next()
                for k in range(16):
                    p.op("pe", lambda e: e.matmul(psG[:, :], lhsT=wg_v[:, k, cs], rhs=x1T[k][0][:, :], start=(k == 0), stop=(k == 15)),
                         reads=[wg_b, x1T[k][1]], writes=[psGb], inc=(k == 15))
                psU, psUb = pr.next()
                for k in range(16):
                    p.op("pe", lambda e: e.matmul(psU[:, :], lhsT=wu_v[:, k, cs], rhs=x1T[k][0][:, :], start=(k == 0), stop=(k == 15)),
                         reads=[wu_b, x1T[k][1]], writes=[psUb], inc=(k == 15))
                s_t, s_b = si_r.next()
                p.op("act", lambda e: e.activation(out=s_t[:], in_=psG[:, :], func=AF.Silu), reads=[psGb], writes=[s_b])
                p.op("dve", lambda e: e.tensor_tensor(out=aT(jt), in0=psU[:, :], in1=s_t[:], op=ALU.mult), reads=[psUb, s_b], writes=[slot[jt]])
        for nt in range(4):
            ns = slice(nt * 512, (nt + 1) * 512)
            acc = [pr.next() for _ in range(4)]
            for qd in range(4):
                w_t, w_b = wring.next()
                wf_v = w_t[:, 0:11 * 512].rearrange("p (k n) -> p k n", n=512)
                p.dma("sp", wf_v, wts["wfo"][qd * 11 * 128:(qd + 1) * 11 * 128, ns].rearrange("(k p) n -> p k n", p=128), writes=[w_b])
                for tt in range(4):
                    ps, psb = acc[tt]
                    for k in range(11):
                        j = qd * 11 + k
                        p.op("pe", lambda e: e.matmul(ps[:, :], lhsT=aT(j)[:, tt * 128:(tt + 1) * 128], rhs=wf_v[:, k, :],
                                                      start=(j == 0), stop=(j == NJ - 1)), reads=[w_b, slot[j]], writes=[psb],
                             inc=(k == 10))
            for tt in range(4):
                ps, psb = acc[tt]
                x_t, x_b = x1[tt]
                p.op("dve", lambda e: e.scalar_tensor_tensor(out=x_t[:, ns], in0=x_t[:, ns], scalar=ALPHA, in1=ps[:, :],
                                                              op0=ALU.mult, op1=ALU.add), reads=[x_b, psb], writes=[x_b])
        for tt in range(4):
            layer_norm(tt, "g2", "b2")
            p.dma("sp", xout[T * 512 + tt * 128:T * 512 + (tt + 1) * 128, :], x1[tt][0][:], reads=[x1[tt][1]])
            if xTb_next is not None:
                transposes(tt, x1T)
        if xTb_next is not None:
            for k in range(16):
                p.dma("sp", xTb_next[k * 128:(k + 1) * 128, ts], x1T[k][0][:, :], reads=[x1T[k][1]])


W_SHAPES = {"w_in": (D_MODEL, D_IN), "w_branch_a": (1024, D_MODEL), "w_branch_b": (1024, D_MODEL),
            "w_out": (D_MODEL, D_MODEL), "w_ffn_in": (D_MODEL, 2 * D_FF), "w_ffn_out": (D_FF, D_MODEL)}


def const_arrays(inputs, S, depth):
    NCH = S // CH
    c = dict(gdn_consts())
    c.update(dsa_consts(np.asarray(inputs["rel_bias"], np.float32)))
    c["gnw"] = np.ascontiguousarray(np.broadcast_to(np.asarray(inputs["gdn_norm_w"])[:depth, None, :], (depth, CH, 128))).astype(np.float32)
    cw = np.asarray(inputs["conv_w"])[:depth]
    c["cw"] = np.ascontiguousarray(cw.reshape(depth, 4, 3, 8, 128).transpose(0, 4, 2, 3, 1).reshape(depth, 128, 96)).astype(np.float32)
    dtb = np.zeros((depth, CH, NCH, 16), np.float32)
    negA = np.zeros((depth, CH, NCH, 16), np.float32)
    dtb[..., 0:8] = np.asarray(inputs["dt_bias"])[:depth, None, None, :]
    a_log = np.asarray(inputs["a_log"])[:depth]
    negA[..., 0:8] = a_log[:, None, None, :]
    c["dtb16"] = dtb.reshape(depth, CH, NCH * 16)
    c["alog16"] = negA.reshape(depth, CH, NCH * 16)
    for nm in ("ln1_g", "ln1_b", "ln2_g", "ln2_b"):
        c[nm] = np.ascontiguousarray(np.broadcast_to(np.asarray(inputs[nm])[:depth, None, :], (depth, 128, D_MODEL))).astype(np.float32)
    return c


def build_all(S, depth, cshapes, phases=("cast", "proj", "gdn", "dsa", "ffn")):
    p = Prog()
    x = p.dram("x", [S, D_MODEL], F32, "ExternalInput")
    xT = p.dram("xT", [D_MODEL, S], F32, "ExternalInput")
    wext = {nm: p.dram(nm, [depth] + list(sh), F32, "ExternalInput") for nm, sh in W_SHAPES.items()} if "cast" in phases else {}
    cst = {k: p.dram("c_" + k, list(sh), F32, "ExternalInput") for k, sh in cshapes.items()}
    out = p.dram("out", [S, D_MODEL], F32, "ExternalOutput")
    scr = {k: p.scratch("s_" + k, fn(S), dt) for k, (fn, dt) in SCR_SPEC.items()}
    scr["ident"] = cst["ident"]
    wb16 = {nm: [p.scratch("%s_b%d" % (nm, l), list(sh), BF16) for l in range(depth)] for nm, sh in W_SHAPES.items()}
    with p.phase():
        rings = (Ring(p, 3, [128, CAST_CH], F32), Ring(p, 3, [128, CAST_CH], BF16))
        k = emit_cast(p, xT, scr["xTb"], D_MODEL, S, rings)
        for l in range(depth if "cast" in phases else 0):
            for nm, sh in W_SHAPES.items():
                k = emit_cast(p, wext[nm][l], wb16[nm][l], sh[0], sh[1], rings, k)
    negA = p.scratch("s_negA", list(cshapes["alog16"]), F32)
    with p.phase():
        for l in range(depth):
            W = cshapes["alog16"][2]
            t = p.sb([CH, W], F32); tb = Buf()
            p.dma("sp", t[:], cst["alog16"][l, :, :], writes=[tb])
            p.op("act", lambda e: e.activation(out=t[:], in_=t[:], func=AF.Exp), reads=[tb], writes=[tb])
            p.op("dve", lambda e: e.tensor_scalar(out=t[:], in0=t[:], scalar1=-1.0, scalar2=None, op0=ALU.mult), reads=[tb], writes=[tb])
            p.dma("sp", negA[l, :, :], t[:], reads=[tb])
    cst["negA16"] = negA
    gbt = p.scratch("s_gbt", [3, CH, cshapes["alog16"][2]], F32)
    xbuf = [scr["xA"], scr["xB"]]
    for l in range(depth):
        last = (l == depth - 1)
        if "proj" in phases:
            with p.phase():
                emit_proj(p, S, scr["xTb"], wb16["w_in"][l], scr)
        if "gdn" in phases:
            with p.phase():
                emit_gdn_pre(p, S, scr, cst, l, gbt)
            for heads in ([0, 1, 2, 3], [4, 5, 6, 7]):
                with p.phase():
                    emit_gdn(p, S, scr, cst, l, heads, gbt)
        if "dsa" in phases:
            with p.phase():
                emit_dsa(p, S, scr, cst)
        if "ffn" not in phases:
            continue
        with p.phase():
            wts = {"wa": wb16["w_branch_a"][l], "wb": wb16["w_branch_b"][l], "wout": wb16["w_out"][l],
                   "wfi": wb16["w_ffn_in"][l], "wfo": wb16["w_ffn_out"][l]}
            lnp = {"g1": cst["ln1_g"][l], "b1": cst["ln1_b"][l], "g2": cst["ln2_g"][l], "b2": cst["ln2_b"][l]}
            emit_ffn(p, S, scr, wts, lnp, x if l == 0 else xbuf[(l - 1) % 2], out if last else xbuf[l % 2],
                     None if last else scr["xTb"])
    p.close()
    return p


def make_in_map(inputs, b, S, depth, cst):
    xb = np.ascontiguousarray(np.asarray(inputs["x"])[b, :S])
    m = {"x": xb, "xT": np.ascontiguousarray(xb.T)}
    for nm in W_SHAPES:
        m[nm] = np.ascontiguousarray(np.asarray(inputs[nm])[:depth])
    for k, v in cst.items():
        m["c_" + k] = v
    return m


def kernel(**inputs):
    S, depth = SEQ, DEPTH
    cst = const_arrays(inputs, S, depth)
    p = build_all(S, depth, {k: v.shape for k, v in cst.items()})
    in_maps = [make_in_map(inputs, b, S, depth, cst) for b in range(BATCH)]
    res = run(p, in_maps).results
    return np.stack([np.asarray(res[b]["out"], np.float32) for b in range(BATCH)], 0)
```
